# Optimizing a Trainium2 kernel written in Bass

```python
import math
import jax, jax.numpy as jnp
from jax import lax
import numpy as np


D_MODEL = 4096
BATCH = 2
SEQ = 8192
DEPTH = 2

CTX_LEN = 256
GRID_W = 64
N_MIXERS = 2
EXPAND = 2
D_INNER = EXPAND * D_MODEL
S5_GROUP = 16
S5_STATE = 64
S5_GROUPS = D_INNER // S5_GROUP
S5_CHUNK = 128
DT_MIN = 1e-3
DT_MAX = 1e-1
POOL_WINDOWS = (2, 4, 8, 16)
POOL_GROUPS = len(POOL_WINDOWS)
POOL_DIM = D_INNER // POOL_GROUPS
N_S5 = (DEPTH + 1) // 2
N_POOL = DEPTH // 2
RMS_EPS = 1e-6

kernel_name = 'hybrid_s5_pool_prefix_trunk'


def rmsnorm(x, w):
    xf = x.astype(jnp.float32)
    y = xf * lax.rsqrt(jnp.mean(xf * xf, axis=-1, keepdims=True) + RMS_EPS)
    return (y * w.astype(jnp.float32)).astype(x.dtype)


def adaln(cond, w, b):
    m = jax.nn.silu(cond) @ w + b
    return jnp.split(m, 3, axis=-1)


def s5_discretize(lam_re, lam_im, log_step, b_re, b_im, c_re, c_im):
    f32 = lambda a: a.astype(jnp.float32)
    lam = lax.complex(f32(lam_re), f32(lam_im))
    dt = jnp.exp(f32(log_step))[:, None]
    lam_bar = jnp.exp(lam * dt)
    b = lax.complex(f32(b_re), f32(b_im))
    b_bar = ((lam_bar - 1.0) / lam)[..., None] * b
    cm = lax.complex(f32(c_re), f32(c_im))
    return lam_bar, b_bar, cm


def _linear_recurrence(e1, e2):
    a1, b1 = e1
    a2, b2 = e2
    return a1 * a2, a2 * b1 + b2


def s5_scan(u, h0, lam_bar, b_bar, cm, reverse):
    bsz, length, _ = u.shape
    chunk = min(S5_CHUNK, length)
    n_chunks = length // chunk
    uf = u.astype(jnp.float32)
    if reverse:
        uf = uf[:, ::-1]
    ub = uf.reshape(bsz, n_chunks, chunk, S5_GROUPS, S5_GROUP).transpose(1, 0, 2, 3, 4)

    def step(h, u_blk):
        bu = jnp.einsum('gpj,btgj->btgp', b_bar, u_blk.astype(jnp.complex64))
        bu = bu.at[:, 0].add(lam_bar * h)
        a = jnp.broadcast_to(lam_bar, bu.shape)
        _, s = lax.associative_scan(_linear_recurrence, (a, bu), axis=1)
        y = jnp.einsum('gjp,btgp->btgj', cm, s).real
        return s[:, -1], y

    h_final, ys = lax.scan(step, h0, ub)
    y = ys.transpose(1, 0, 2, 3, 4).reshape(bsz, length, D_INNER)
    if reverse:
        y = y[:, ::-1]
    return y, h_final


def s5_bidirectional(u, uc, lam_re, lam_im, log_step, b_re, b_im, c_re, c_im):
    y_lat = 0.0
    y_ctx = 0.0
    for d, rev in enumerate((False, True)):
        lam_bar, b_bar, cm = s5_discretize(lam_re[d], lam_im[d], log_step[d],
                                           b_re[d], b_im[d], c_re[d], c_im[d])
        h0 = jnp.zeros((u.shape[0], S5_GROUPS, S5_STATE), jnp.complex64)
        yc, hc = s5_scan(uc, h0, lam_bar, b_bar, cm, rev)
        yl, _ = s5_scan(u, hc, lam_bar, b_bar, cm, rev)
        y_lat = y_lat + yl
        y_ctx = y_ctx + yc
    return y_lat, y_ctx


def s5_readout(y_scan, u, d_skip, w_glu, b_glu):
    y = y_scan + d_skip.astype(jnp.float32) * u.astype(jnp.float32)
    y = jax.nn.gelu(y).astype(u.dtype)
    return y * jax.nn.sigmoid(y @ w_glu + b_glu)


def window_bounds(n, w):
    pos = jnp.arange(n)
    return jnp.clip(pos - w // 2, 0, n), jnp.clip(pos + w - w // 2, 0, n)


def pool_group_deltas(u, on_grid):
    uf = u.astype(jnp.float32)
    bsz, length, _ = uf.shape
    deltas = []
    if on_grid:
        rows = length // GRID_W
        g = uf.reshape(bsz, rows, GRID_W, D_INNER)
        sat = jnp.pad(jnp.cumsum(jnp.cumsum(g, axis=1), axis=2), ((0, 0), (1, 0), (1, 0), (0, 0)))
        for k, w in enumerate(POOL_WINDOWS):
            s = sat[..., k * POOL_DIM:(k + 1) * POOL_DIM]
            r_lo, r_hi = window_bounds(rows, w)
            c_lo, c_hi = window_bounds(GRID_W, w)
            corner = lambda ri, ci: s[:, ri][:, :, ci]
            total = corner(r_hi, c_hi) - corner(r_lo, c_hi) - corner(r_hi, c_lo) + corner(r_lo, c_lo)
            count = ((r_hi - r_lo)[:, None] * (c_hi - c_lo)[None, :]).astype(jnp.float32)
            mean = (total / count[None, :, :, None]).reshape(bsz, length, POOL_DIM)
            deltas.append(mean - uf[..., k * POOL_DIM:(k + 1) * POOL_DIM])
    else:
        p = jnp.pad(jnp.cumsum(uf, axis=1), ((0, 0), (1, 0), (0, 0)))
        for k, w in enumerate(POOL_WINDOWS):
            sl = slice(k * POOL_DIM, (k + 1) * POOL_DIM)
            lo, hi = window_bounds(length, w)
            mean = (p[:, hi, sl] - p[:, lo, sl]) / (hi - lo).astype(jnp.float32)[None, :, None]
            deltas.append(mean - uf[..., sl])
    return deltas


def pool_mixer(u, on_grid, w_groups, scale):
    deltas = pool_group_deltas(u, on_grid)
    y = jnp.concatenate([d.astype(u.dtype) @ w_groups[k] for k, d in enumerate(deltas)], axis=-1)
    return y * scale


def setup_inputs(seed: int = 0) -> dict:
    key = jax.random.key(seed)
    ks = jax.random.split(key, 24)
    nrm = jax.random.normal
    G, P, J, E, D = S5_GROUPS, S5_STATE, S5_GROUP, D_INNER, D_MODEL
    lam_im_init = jnp.pi * jnp.arange(P, dtype=jnp.float32)
    return {
        'x': nrm(ks[0], (BATCH, SEQ, D), jnp.float32),
        'c': nrm(ks[1], (BATCH, D), jnp.float32),
        'ctx': nrm(ks[2], (BATCH, CTX_LEN, D), jnp.float32),
        'c_ctx': nrm(ks[3], (D,), jnp.float32),
        'norm_w': 1.0 + 0.01 * nrm(ks[4], (DEPTH, D), jnp.float32),
        'w_ada': nrm(ks[5], (DEPTH, D, 3 * D), jnp.float32) * D ** -0.5,
        'b_ada': 0.01 * nrm(ks[6], (DEPTH, 3 * D), jnp.float32),
        'w_in': nrm(ks[7], (DEPTH, D, 2 * E), jnp.float32) * D ** -0.5,
        'w_out': nrm(ks[8], (DEPTH, E, D), jnp.float32) * E ** -0.5,
        's5_lam_re': -0.5 + 0.01 * nrm(ks[9], (N_S5, 2, G, P), jnp.float32),
        's5_lam_im': lam_im_init + 0.01 * nrm(ks[10], (N_S5, 2, G, P), jnp.float32),
        's5_log_step': jax.random.uniform(ks[11], (N_S5, 2, G), jnp.float32,
                                          minval=math.log(DT_MIN), maxval=math.log(DT_MAX)),
        's5_b_re': nrm(ks[12], (N_S5, 2, G, P, J), jnp.float32) * (2 * J) ** -0.5,
        's5_b_im': nrm(ks[13], (N_S5, 2, G, P, J), jnp.float32) * (2 * J) ** -0.5,
        's5_c_re': nrm(ks[14], (N_S5, 2, G, J, P), jnp.float32) * P ** -0.5,
        's5_c_im': nrm(ks[15], (N_S5, 2, G, J, P), jnp.float32) * P ** -0.5,
        's5_d': nrm(ks[16], (N_S5, E), jnp.float32),
        's5_w_glu': nrm(ks[17], (N_S5, E, E), jnp.float32) * E ** -0.5,
        's5_b_glu': 0.01 * nrm(ks[18], (N_S5, E), jnp.float32),
        'pool_w': nrm(ks[19], (N_POOL, POOL_GROUPS, POOL_DIM, POOL_DIM), jnp.float32) * POOL_DIM ** -0.5,
        'pool_scale': 1.0 + 0.02 * nrm(ks[20], (N_POOL, E), jnp.float32),
        'final_norm_w': 1.0 + 0.01 * nrm(ks[21], (D,), jnp.float32),
    }


def reference(x, c, ctx, c_ctx, norm_w, w_ada, b_ada, w_in, w_out,
              s5_lam_re, s5_lam_im, s5_log_step, s5_b_re, s5_b_im, s5_c_re, s5_c_im,
              s5_d, s5_w_glu, s5_b_glu, pool_w, pool_scale, final_norm_w):
    for i in range(DEPTH):
        kind = i % N_MIXERS
        j = i // N_MIXERS
        ctx_later = any(l % N_MIXERS == 0 for l in range(i + 1, DEPTH))
        shift, scale, gate = adaln(c, w_ada[i], b_ada[i])
        h = rmsnorm(x, norm_w[i]) * (1.0 + scale[:, None]) + shift[:, None]
        u, z = jnp.split(h @ w_in[i], 2, axis=-1)
        if kind == 0 or ctx_later:
            shift_c, scale_c, gate_c = adaln(c_ctx, w_ada[i], b_ada[i])
            hc = rmsnorm(ctx, norm_w[i]) * (1.0 + scale_c) + shift_c
            uc, zc = jnp.split(hc @ w_in[i], 2, axis=-1)
        if kind == 0:
            y_lat_scan, y_ctx_scan = s5_bidirectional(
                u, uc, s5_lam_re[j], s5_lam_im[j], s5_log_step[j],
                s5_b_re[j], s5_b_im[j], s5_c_re[j], s5_c_im[j])
            y_lat = s5_readout(y_lat_scan, u, s5_d[j], s5_w_glu[j], s5_b_glu[j])
            if ctx_later:
                y_ctx = s5_readout(y_ctx_scan, uc, s5_d[j], s5_w_glu[j], s5_b_glu[j])
        else:
            y_lat = pool_mixer(u, True, pool_w[j], pool_scale[j])
            if ctx_later:
                y_ctx = pool_mixer(uc, False, pool_w[j], pool_scale[j])
        x = x + gate[:, None] * ((y_lat * jax.nn.silu(z)) @ w_out[i])
        if ctx_later:
            ctx = ctx + gate_c * ((y_ctx * jax.nn.silu(zc)) @ w_out[i])
    return rmsnorm(x, final_norm_w)
```

```python
import numpy as np
import concourse.bass as bass
import concourse.mybir as mybir

F32 = mybir.dt.float32
BF16 = mybir.dt.bfloat16
ALU = mybir.AluOpType
AF = mybir.ActivationFunctionType

ENGS = ("pe", "act", "dve", "pool", "sp")
SEM_EPOCH = 30000


class Op:
    __slots__ = ("eng", "fn", "deps", "is_dma", "dkey", "sig", "idx", "dma_wait")

    def __init__(self, eng, fn, is_dma=False, dkey=None):
        self.eng = eng
        self.fn = fn
        self.deps = []
        self.is_dma = is_dma
        self.dkey = dkey
        self.sig = False
        self.idx = None
        self.dma_wait = None


class Prog:
    def __init__(self, nc, same_engine_sync=True):
        self.nc = nc
        self.ops = []
        self.last_w = {}
        self.readers = {}
        self.same_engine_sync = same_engine_sync
        self.dma_count = {}
        self.ctx = []

    def _add(self, op, reads, writes):
        deps = []
        for k in reads:
            w = self.last_w.get(k)
            if w is not None:
                deps.append(w)
        for k in writes:
            w = self.last_w.get(k)
            if w is not None:
                deps.append(w)
            for r in self.readers.get(k, ()):
                deps.append(r)
        seen = set()
        for d in deps:
            if id(d) in seen or d is op:
                continue
            seen.add(id(d))
            if (not d.is_dma) and d.eng == op.eng and not op.is_dma:
                if d.eng == "pe" or not self.same_engine_sync:
                    continue
            op.deps.append(d)
            d.sig = True
        for k in reads:
            self.readers.setdefault(k, []).append(op)
        for k in writes:
            self.last_w[k] = op
            self.readers[k] = []
        self.ops.append(op)
        return op

    def op(self, eng, fn, reads=(), writes=()):
        return self._add(Op(eng, fn), reads, writes)

    def dma(self, eng, fn, dkey, reads=(), writes=()):
        o = Op(eng, fn, is_dma=True, dkey=dkey)
        ep, cnt = self.dma_count.get(dkey, (0, 0))
        if cnt + 16 > SEM_EPOCH:
            ep, cnt = ep + 1, 0
        cnt += 16
        self.dma_count[dkey] = (ep, cnt)
        o.dma_wait = (dkey, ep, cnt)
        return self._add(o, reads, writes)

    def emit(self, final_wait_ops=()):
        nc = self.nc
        counters = {e: [0, 0] for e in ENGS}
        sig_of = {}
        for o in self.ops:
            if o.is_dma:
                continue
            if o.sig:
                c = counters[o.eng]
                if c[1] + 1 > SEM_EPOCH:
                    c[0] += 1
                    c[1] = 0
                c[1] += 1
                sig_of[id(o)] = (("eng", o.eng), c[0], c[1])
        semkeys = set()
        for o in self.ops:
            if o.is_dma:
                semkeys.add((("dma", o.dkey), o.dma_wait[1]))
            elif o.sig:
                k = sig_of[id(o)]
                semkeys.add((k[0], k[1]))
        semkeys = sorted(semkeys, key=repr)
        self.n_sems = len(semkeys)
        sems = {}
        import contextlib
        stack = contextlib.ExitStack()
        for i, k in enumerate(semkeys):
            sems[k] = stack.enter_context(nc.semaphore(f"s{i}"))
        per_eng = {e: [] for e in ENGS}
        for o in self.ops:
            per_eng[o.eng].append(o)

        def wait_target(d):
            if d.is_dma:
                return (("dma", d.dkey), d.dma_wait[1]), d.dma_wait[2]
            k = sig_of[id(d)]
            return (k[0], k[1]), k[2]

        def run(engname, eng):
            waited = {}
            for o in per_eng[engname]:
                for d in o.deps:
                    sk, val = wait_target(d)
                    if waited.get(sk, 0) >= val:
                        continue
                    waited[sk] = val
                    eng.wait_ge(sems[sk], val)
                ins = o.fn(eng)
                if o.is_dma:
                    ins.then_inc(sems[(("dma", o.dkey), o.dma_wait[1])], 16)
                elif o.sig:
                    k = sig_of[id(o)]
                    ins.then_inc(sems[(k[0], k[1])], 1)
            if engname == "sp":
                for d in final_wait_ops:
                    sk, val = wait_target(d)
                    eng.wait_ge(sems[sk], val)

        with stack:
            with nc.Block() as block:
                @block.tensor
                def _(e):
                    run("pe", e)

                @block.scalar
                def _(e):
                    run("act", e)

                @block.vector
                def _(e):
                    run("dve", e)

                @block.gpsimd
                def _(e):
                    run("pool", e)

                @block.sync
                def _(e):
                    run("sp", e)


import contextlib
import ml_dtypes

D = 4096
E = 8192
KC = D // 128
EC = E // 128
EPS = 1e-6


def colv(v):
    v = np.ascontiguousarray(v, dtype=np.float32)
    return np.ascontiguousarray(v.reshape(-1, 128).T)


class WStream:
    def __init__(self, P, nc, st, kmax, wb, name="wt", nbuf=2):
        self.P, self.nc = P, nc
        self.wb = wb
        self.kmax = kmax
        self.bufs = [st.enter_context(nc.sbuf_tensor(f"{name}{i}", [128, kmax, wb], BF16)) for i in range(nbuf)]
        self.i = 0
        self.name = name

    def load(self, w, kc0, nkc, n0, ncols):
        s = self.i % len(self.bufs)
        self.i += 1
        buf = self.bufs[s]
        key = (self.name, s)
        wv = w.rearrange("(kc p) n -> p kc n", p=128)
        step = 8
        for q in range(0, nkc, step):
            qn = min(step, nkc - q)
            self.P.dma("pool", lambda e, q=q, qn=qn: e.dma_start(
                out=buf[:, q:q + qn, 0:ncols], in_=wv[:, kc0 + q:kc0 + q + qn, n0:n0 + ncols]),
                dkey=key, writes=[key])
        return buf, key


def build_stage_a(T, NOUT, with_gate=True):
    nc = bass.Bass("TRN2", target_bir_lowering=False)
    xT = nc.dram_tensor("xT", [D, T], F32, kind="ExternalInput").ap()
    modT = nc.dram_tensor("modT", [128, 96], F32, kind="ExternalInput").ap()
    nw = nc.dram_tensor("nw", [128, KC], F32, kind="ExternalInput").ap()
    w_in = nc.dram_tensor("w_in", [D, NOUT], F32, kind="ExternalInput").ap()
    uz = nc.dram_tensor("uz", [NOUT, T], BF16, kind="ExternalOutput").ap()
    TT = min(512, T)
    NT = T // TT
    WB = 256
    st = contextlib.ExitStack()
    with st:
        sb = lambda name, shape, dt: st.enter_context(nc.sbuf_tensor(name, shape, dt))
        hT = sb("hT", [128, KC, T], BF16)
        c_sb = sb("c_sb", [128, KC], F32)
        sc = sb("sc", [128, KC], BF16)
        nw_sb = sb("nw_sb", [128, KC], F32)
        ba_sb = sb("ba_sb", [128, 96], F32)
        mod = sb("mod", [128, 96], F32)
        a1 = sb("a1", [128, KC], F32)
        ones = sb("ones", [128, 128], BF16)
        xs = [sb(f"xs{i}", [128, T], F32) for i in range(2)]
        sq = [sb(f"sq{i}", [128, T], BF16) for i in range(2)]
        rstd = sb("rstd", [128, T], F32)
        ot = [sb(f"ot{i}", [128, T], BF16) for i in range(2)]
        ps = [st.enter_context(nc.psum_tensor(f"ps{i}", [128, 512], F32)) for i in range(8)]
        P = Prog(nc)
        ws = WStream(P, nc, st, KC, WB)
        P.dma("sp", lambda e: e.dma_start(out=nw_sb[:, :], in_=nw[:, :]), dkey=("prm", "nw"), writes=["nw"])
        P.dma("sp", lambda e: e.dma_start(out=mod[:, :], in_=modT[:, :]), dkey=("prm", "mod"), writes=["mod"])
        P.op("dve", lambda e: e.memset(ones[:, :], 1.0), writes=["ones"])
        P.op("dve", lambda e: e.scalar_tensor_tensor(out=a1[:, :], in0=mod[:, KC:2 * KC], scalar=1.0, in1=nw_sb[:, :],
                                                     op0=ALU.add, op1=ALU.mult), reads=["mod", "nw"], writes=["a1"])
        xv = xT.rearrange("(kc p) t -> p kc t", p=128)
        for kc in range(KC):
            s = kc % 2
            P.dma("sp", lambda e, s=s, kc=kc: e.dma_start(out=xs[s][:, :], in_=xv[:, kc, :]), dkey=("xs", s),
                  writes=[("xs", s)])
            P.op("act", lambda e, s=s: e.activation(out=sq[s][:, :], in_=xs[s][:, :], func=AF.Square),
                 reads=[("xs", s)], writes=[("sq", s)])
            for tt in range(NT):
                P.op("pe", lambda e, s=s, tt=tt, kc=kc: e.matmul(
                    ps[tt][:, 0:TT], ones[:, :], sq[s][:, tt * TT:(tt + 1) * TT], start=(kc == 0), stop=(kc == KC - 1)),
                    reads=[("sq", s), "ones"], writes=[("ps", tt)])
        eps_t = sb("eps_t", [128, 1], F32)
        P.op("dve", lambda e: e.memset(eps_t[:, :], EPS), writes=["eps"])
        for tt in range(NT):
            P.op("act", lambda e, tt=tt: e.activation(out=rstd[:, tt * TT:(tt + 1) * TT], in_=ps[tt][:, 0:TT],
                                                     func=AF.Sqrt, scale=1.0 / D, bias=eps_t[:, 0:1]),
                 reads=[("ps", tt), "eps"], writes=[("rs", tt)])
            P.op("dve", lambda e, tt=tt: e.reciprocal(out=rstd[:, tt * TT:(tt + 1) * TT],
                                                      in_=rstd[:, tt * TT:(tt + 1) * TT]),
                 reads=[("rs", tt)], writes=[("rs", tt)])
        for kc in range(KC):
            s = kc % 2
            P.dma("sp", lambda e, s=s, kc=kc: e.dma_start(out=xs[s][:, :], in_=xv[:, kc, :]), dkey=("xs", s),
                  writes=[("xs", s)])
            P.op("dve", lambda e, s=s: e.tensor_tensor(out=xs[s][:, :], in0=xs[s][:, :], in1=rstd[:, :], op=ALU.mult),
                 reads=[("xs", s)] + [("rs", tt) for tt in range(NT)], writes=[("xs", s)])
            P.op("act", lambda e, s=s, kc=kc: e.activation(out=hT[:, kc, :], in_=xs[s][:, :], func=AF.Identity,
                                                          scale=a1[:, kc:kc + 1], bias=mod[:, kc:kc + 1]),
                 reads=[("xs", s), "a1", "mod"], writes=[("h", kc)])
        hkeys = [("h", kc) for kc in range(KC)]
        pi = 0
        oi = 0
        outs = []
        for blk in range(NOUT // WB):
            buf, key = ws.load(w_in, 0, KC, blk * WB, WB)
            for o2 in range(WB // 128):
                osl = oi % 2
                oi += 1
                for tt in range(NT):
                    pb = pi % 8
                    pi += 1
                    for kc in range(KC):
                        P.op("pe", lambda e, pb=pb, buf=buf, kc=kc, o2=o2, tt=tt: e.matmul(
                            ps[pb][:, 0:TT], buf[:, kc, o2 * 128:(o2 + 1) * 128], hT[:, kc, tt * TT:(tt + 1) * TT],
                            start=(kc == 0), stop=(kc == KC - 1)),
                            reads=[key] + (hkeys if kc == 0 else []), writes=[("ps", pb) if pb < 4 else ("psx", pb)] )
                    pk = ("ps", pb) if pb < 4 else ("psx", pb)
                    if pi % 2 == 0:
                        P.op("act", lambda e, pb=pb, osl=osl, tt=tt: e.activation(
                            out=ot[osl][:, tt * TT:(tt + 1) * TT], in_=ps[pb][:, 0:TT], func=AF.Copy),
                            reads=[pk], writes=[("ot", osl, tt)])
                    else:
                        P.op("dve", lambda e, pb=pb, osl=osl, tt=tt: e.tensor_copy(
                            out=ot[osl][:, tt * TT:(tt + 1) * TT], in_=ps[pb][:, 0:TT]),
                            reads=[pk], writes=[("ot", osl, tt)])
                r0 = blk * WB + o2 * 128
                d = P.dma("sp", lambda e, osl=osl, r0=r0: e.dma_start(out=uz[r0:r0 + 128, :], in_=ot[osl][:, :]),
                          dkey=("ost", osl), reads=[("ot", osl, tt) for tt in range(NT)], writes=[("uz", r0)])
                outs.append(d)
        P.emit(final_wait_ops=outs[-2:])
    return nc


TWO_PI = 2.0 * np.pi


def build_stage_s5(LC, LX, NT8=8, SB=64, CH=256):
    L = LC + LX
    NG = 4 * NT8
    nc = bass.Bass("TRN2", target_bir_lowering=False)
    din = lambda n, s, dt=F32: nc.dram_tensor(n, s, dt, kind="ExternalInput").ap()
    u = din("u", [NT8, 128, 2, L], BF16)
    BpT = [din("BpT_re", [2, 128, NT8, 128]), din("BpT_im", [2, 128, NT8, 128])]
    lamR = [din("lamR_re", [2, 128, NT8, 128]), din("lamR_im", [2, 128, NT8, 128]), din("lsR", [2, 128, NT8, 128])]
    Cm = [din("Cm_re", [2, 128, NG, 32]), din("Cm_im", [2, 128, NG, 32])]
    lamM = [din("lamM_re", [2, 128, NG]), din("lamM_im", [2, 128, NG]), din("lsM", [2, 128, NG])]
    dskip = din("dskip", [128, NT8])
    ydir = [nc.dram_tensor(f"yd{d}", [NT8, 128, 2, LX], F32, kind="ExternalOutput").ap() for d in range(2)]
    yg = nc.dram_tensor("yg", [NT8, 128, 2, LX], BF16, kind="ExternalOutput").ap()
    st = contextlib.ExitStack()
    with st:
        sb = lambda name, shape, dt=F32: st.enter_context(nc.sbuf_tensor(name, shape, dt))
        P = Prog(nc)
        MAGIC = 12582912.0

        def lam_bar(pref, lre, lim, ls, n, keys):
            t = {k: sb(f"{pref}_{k}", [128, n]) for k in ("dt", "a", "th", "mag", "cs", "sn", "are", "aim", "k")}
            P.op("act", lambda e: e.activation(out=t["dt"][:], in_=ls, func=AF.Exp), reads=keys, writes=[pref + "dt"])
            P.op("dve", lambda e: e.tensor_tensor(out=t["a"][:], in0=lre, in1=t["dt"][:], op=ALU.mult),
                 reads=keys + [pref + "dt"], writes=[pref + "a"])
            P.op("dve", lambda e: e.tensor_tensor(out=t["th"][:], in0=lim, in1=t["dt"][:], op=ALU.mult),
                 reads=keys + [pref + "dt"], writes=[pref + "th"])
            P.op("act", lambda e: e.activation(out=t["mag"][:], in_=t["a"][:], func=AF.Exp), reads=[pref + "a"],
                 writes=[pref + "mag"])
            for nm, off in (("sn", 0.0), ("cs", 0.25)):
                P.op("dve", lambda e, nm=nm, off=off: e.tensor_scalar(out=t[nm][:], in0=t["th"][:], scalar1=1.0 / TWO_PI,
                                                                     scalar2=off, op0=ALU.mult, op1=ALU.add),
                     reads=[pref + "th"], writes=[pref + nm])
                P.op("dve", lambda e, nm=nm: e.tensor_scalar(out=t["k"][:], in0=t[nm][:], scalar1=MAGIC, scalar2=None,
                                                             op0=ALU.add), reads=[pref + nm], writes=[pref + "k"])
                P.op("dve", lambda e, nm=nm: e.tensor_scalar(out=t["k"][:], in0=t["k"][:], scalar1=-MAGIC, scalar2=None,
                                                             op0=ALU.add), reads=[pref + "k"], writes=[pref + "k"])
                P.op("dve", lambda e, nm=nm: e.tensor_tensor(out=t[nm][:], in0=t[nm][:], in1=t["k"][:], op=ALU.subtract),
                     reads=[pref + nm, pref + "k"], writes=[pref + nm])
                P.op("act", lambda e, nm=nm: e.activation(out=t[nm][:], in_=t[nm][:], func=AF.Sin, scale=TWO_PI),
                     reads=[pref + nm], writes=[pref + nm])
            P.op("dve", lambda e: e.tensor_tensor(out=t["are"][:], in0=t["mag"][:], in1=t["cs"][:], op=ALU.mult),
                 reads=[pref + "mag", pref + "cs"], writes=[pref + "are"])
            P.op("dve", lambda e: e.tensor_tensor(out=t["aim"][:], in0=t["mag"][:], in1=t["sn"][:], op=ALU.mult),
                 reads=[pref + "mag", pref + "sn"], writes=[pref + "aim"])
            return t["are"], t["aim"]

        LBz = [sb(f"LBz{d}", [128, NT8, 4, 2, 128], BF16) for d in range(2)]
        Cz = [sb(f"Cz{d}", [128, NG, 2, 128], BF16) for d in range(2)]
        A2 = [sb(f"A2_{d}", [128, 2, 2, NG]) for d in range(2)]
        dsk = sb("dsk", [128, NT8])
        P.dma("sp", lambda e: e.dma_start(out=dsk[:, :], in_=dskip[:, :]), dkey=("prm", "dsk"), writes=["dsk"])
        cm = [sb(f"cm{i}", [128, NG, 32]) for i in range(2)]
        lr = [sb(f"lr{i}", [128, 128]) for i in range(3)]
        bp = [sb(f"bp{i}", [128, 128]) for i in range(2)]
        ktmp = {k: sb(f"k_{k}", [128, 128]) for k in ("nre", "den", "t1", "t2", "kre", "kim", "o1", "o2")}

        def dir_params(d):
            pf = f"d{d}"
            lm = [sb(f"{pf}lm{i}", [128, NG]) for i in range(3)]
            for i in range(3):
                P.dma("sp", lambda e, i=i: e.dma_start(out=lm[i][:, :], in_=lamM[i][d]), dkey=("prm", "lm", d, i),
                      writes=[pf + f"lm{i}"])
            are, aim = lam_bar(pf + "M", lm[0][:], lm[1][:], lm[2][:], NG, [pf + f"lm{i}" for i in range(3)])
            for c in range(2):
                P.op("dve", lambda e, c=c: e.tensor_copy(out=A2[d][:, 0, c, :], in_=are[:, :]), reads=[pf + "Mare"],
                     writes=[pf + "A2"])
            P.op("dve", lambda e: e.tensor_scalar(out=A2[d][:, 1, 0, :], in0=aim[:, :], scalar1=-1.0, scalar2=None,
                                                  op0=ALU.mult), reads=[pf + "Maim"], writes=[pf + "A2"])
            P.op("dve", lambda e: e.tensor_copy(out=A2[d][:, 1, 1, :], in_=aim[:, :]), reads=[pf + "Maim"],
                 writes=[pf + "A2"])
            for i in range(2):
                P.dma("sp", lambda e, i=i: e.dma_start(out=cm[i][:, :, :], in_=Cm[i][d]), dkey=("prm", "cm", i),
                      writes=[f"cm{i}"])
            P.op("pool", lambda e: e.memset(Cz[d][:].rearrange("p a b c -> p (a b c)"), 0.0), writes=[pf + "Cz"])
            for q in range(4):
                P.op("dve", lambda e, q=q: e.tensor_copy(out=Cz[d][:, q::4, 0, 32 * q:32 * q + 32], in_=cm[0][:, q::4, :]),
                     reads=["cm0", "cm1"], writes=[pf + "Cz"])
                P.op("dve", lambda e, q=q: e.tensor_scalar(out=Cz[d][:, q::4, 1, 32 * q:32 * q + 32], in0=cm[1][:, q::4, :],
                                                           scalar1=-1.0, scalar2=None, op0=ALU.mult),
                     reads=["cm0", "cm1"], writes=[pf + "Cz"])
            P.op("pool", lambda e: e.memset(LBz[d][:].rearrange("p a b c m -> p (a b c m)"), 0.0), writes=[pf + "LBz"])
            for t in range(NT8):
                for i in range(3):
                    P.dma("sp", lambda e, i=i, t=t: e.dma_start(out=lr[i][:, :], in_=lamR[i][d, :, t, :]),
                          dkey=("prm", "lr", i), writes=[f"lr{i}"])
                for i in range(2):
                    P.dma("sp", lambda e, i=i, t=t: e.dma_start(out=bp[i][:, :], in_=BpT[i][d, :, t, :]),
                          dkey=("prm", "bp", i), writes=[f"bp{i}"])
                lam_bar_again("R", lr, ["lr0", "lr1", "lr2"])
                rre, rim = Rt["are"], Rt["aim"]
                K = lambda k: ktmp[k][:]
                ops = [
                    lambda e: e.tensor_scalar(out=K("nre"), in0=rre[:], scalar1=-1.0, scalar2=None, op0=ALU.add),
                    lambda e: e.tensor_tensor(out=K("t1"), in0=lr[0][:], in1=lr[0][:], op=ALU.mult),
                    lambda e: e.tensor_tensor(out=K("t2"), in0=lr[1][:], in1=lr[1][:], op=ALU.mult),
                    lambda e: e.tensor_tensor(out=K("den"), in0=K("t1"), in1=K("t2"), op=ALU.add),
                    lambda e: e.reciprocal(out=K("den"), in_=K("den")),
                    lambda e: e.tensor_tensor(out=K("t1"), in0=K("nre"), in1=lr[0][:], op=ALU.mult),
                    lambda e: e.tensor_tensor(out=K("t2"), in0=rim[:], in1=lr[1][:], op=ALU.mult),
                    lambda e: e.tensor_tensor(out=K("t1"), in0=K("t1"), in1=K("t2"), op=ALU.add),
                    lambda e: e.tensor_tensor(out=K("kre"), in0=K("t1"), in1=K("den"), op=ALU.mult),
                    lambda e: e.tensor_tensor(out=K("t1"), in0=rim[:], in1=lr[0][:], op=ALU.mult),
                    lambda e: e.tensor_tensor(out=K("t2"), in0=K("nre"), in1=lr[1][:], op=ALU.mult),
                    lambda e: e.tensor_tensor(out=K("t1"), in0=K("t1"), in1=K("t2"), op=ALU.subtract),
                    lambda e: e.tensor_tensor(out=K("kim"), in0=K("t1"), in1=K("den"), op=ALU.mult),
                    lambda e: e.tensor_tensor(out=K("t1"), in0=K("kre"), in1=bp[0][:], op=ALU.mult),
                    lambda e: e.tensor_tensor(out=K("t2"), in0=K("kim"), in1=bp[1][:], op=ALU.mult),
                    lambda e: e.tensor_tensor(out=K("o1"), in0=K("t1"), in1=K("t2"), op=ALU.subtract),
                    lambda e: e.tensor_tensor(out=K("t1"), in0=K("kre"), in1=bp[1][:], op=ALU.mult),
                    lambda e: e.tensor_tensor(out=K("t2"), in0=K("kim"), in1=bp[0][:], op=ALU.mult),
                    lambda e: e.tensor_tensor(out=K("o2"), in0=K("t1"), in1=K("t2"), op=ALU.add),
                ]
                for f in ops:
                    P.op("dve", f, reads=["kap", "lr0", "lr1", "lr2", "bp0", "bp1", "Rare", "Raim"], writes=["kap"])
                for q in range(3):
                    for c, nm in ((0, "o1"), (1, "o2")):
                        P.op("dve", lambda e, q=q, c=c, nm=nm, t=t: e.tensor_copy(
                            out=LBz[d][32 * q:32 * q + 32, t, q, c, :], in_=ktmp[nm][32 * q:32 * q + 32, :]),
                            reads=["kap"], writes=[pf + "LBz"])
                for c, nm in ((0, "o1"), (1, "o2")):
                    P.op("dve", lambda e, c=c, nm=nm, t=t: e.tensor_copy(
                        out=LBz[d][64:128, t, 3, c, :], in_=ktmp[nm][64:128, :]), reads=["kap"], writes=[pf + "LBz"])
                    P.op("dve", lambda e, c=c, t=t: e.memset(LBz[d][64:96, t, 3, c, :], 0.0), reads=["kap"],
                         writes=[pf + "LBz"])

        RR = []
        Rt = {}

        def lam_bar_again(pref, lr_, keys):
            t = Rt
            lre, lim, ls = lr_[0][:], lr_[1][:], lr_[2][:]
            P.op("act", lambda e: e.activation(out=t["dt"][:], in_=ls, func=AF.Exp), reads=keys, writes=[pref + "dt"])
            P.op("dve", lambda e: e.tensor_tensor(out=t["a"][:], in0=lre, in1=t["dt"][:], op=ALU.mult),
                 reads=keys + [pref + "dt"], writes=[pref + "a"])
            P.op("dve", lambda e: e.tensor_tensor(out=t["th"][:], in0=lim, in1=t["dt"][:], op=ALU.mult),
                 reads=keys + [pref + "dt"], writes=[pref + "th"])
            P.op("act", lambda e: e.activation(out=t["mag"][:], in_=t["a"][:], func=AF.Exp), reads=[pref + "a"],
                 writes=[pref + "mag"])
            for nm, off in (("sn", 0.0), ("cs", 0.25)):
                P.op("dve", lambda e, nm=nm, off=off: e.tensor_scalar(out=t[nm][:], in0=t["th"][:], scalar1=1.0 / TWO_PI,
                                                                     scalar2=off, op0=ALU.mult, op1=ALU.add),
                     reads=[pref + "th"], writes=[pref + nm])
                P.op("dve", lambda e, nm=nm: e.tensor_scalar(out=t["k"][:], in0=t[nm][:], scalar1=MAGIC, scalar2=None,
                                                             op0=ALU.add), reads=[pref + nm], writes=[pref + "k"])
                P.op("dve", lambda e, nm=nm: e.tensor_scalar(out=t["k"][:], in0=t["k"][:], scalar1=-MAGIC, scalar2=None,
                                                             op0=ALU.add), reads=[pref + "k"], writes=[pref + "k"])
                P.op("dve", lambda e, nm=nm: e.tensor_tensor(out=t[nm][:], in0=t[nm][:], in1=t["k"][:], op=ALU.subtract),
                     reads=[pref + nm, pref + "k"], writes=[pref + nm])
                P.op("act", lambda e, nm=nm: e.activation(out=t[nm][:], in_=t[nm][:], func=AF.Sin, scale=TWO_PI),
                     reads=[pref + nm], writes=[pref + nm])
            P.op("dve", lambda e: e.tensor_tensor(out=t["are"][:], in0=t["mag"][:], in1=t["cs"][:], op=ALU.mult),
                 reads=[pref + "mag", pref + "cs"], writes=[pref + "are"])
            P.op("dve", lambda e: e.tensor_tensor(out=t["aim"][:], in0=t["mag"][:], in1=t["sn"][:], op=ALU.mult),
                 reads=[pref + "mag", pref + "sn"], writes=[pref + "aim"])

        for k_ in ("dt", "a", "th", "mag", "cs", "sn", "are", "aim", "k"):
            Rt[k_] = sb(f"Rt_{k_}", [128, 128])
        for d in range(2):
            dir_params(d)

        ub = [[sb(f"ub{d}_{i}", [128, NT8, 2, SB], BF16) for i in range(2)] for d in range(2)]
        S_sb = [sb(f"S_sb{d}", [128, 2, NG, 2, SB], BF16) for d in range(2)]
        hb = [sb(f"hb{d}", [128, NG, 2, 2, SB], BF16) for d in range(2)]
        Hs = [[sb(f"H{d}_{i}", [128, 2, NG, 2]) for i in range(2)] for d in range(2)]
        T1 = [sb(f"T1_{d}", [128, 2, NG, 2]) for d in range(2)]
        T2 = [sb(f"T2_{d}", [128, 2, NG, 2]) for d in range(2)]
        ybuf = [[sb(f"yb{d}_{i}", [128, 2, SB]) for i in range(2)] for d in range(2)]
        psS = [[st.enter_context(nc.psum_tensor(f"psS{d}_{i}", [128, 4, 2, 2, SB], F32)) for i in range(2)] for d in range(2)]
        nblk = L // SB
        ncb = LC // SB
        orders = [list(range(nblk)), list(range(ncb - 1, -1, -1)) + list(range(nblk - 1, ncb - 1, -1))]
        state = [dict(ui=0, yi=0, hcur=0) for _ in range(2)]
        outs = []
        bc = lambda ap: ap.unsqueeze(3).broadcast_to([128, 2, NG, 2])

        def do_block(d, blk):
            eng = "dve" if d == 0 else "pool"
            pf = f"d{d}"
            stt = state[d]
            t0 = blk * SB
            is_x = blk >= ncb
            us = stt["ui"] % 2
            stt["ui"] += 1
            ubk = (pf, "ub", us)
            for b_ in range(2):
                P.dma("sp", lambda e, b_=b_: e.dma_start(out=ub[d][us][:, :, b_, :],
                                                        in_=u[:, :, b_, t0:t0 + SB].rearrange("t p s -> p t s")),
                      dkey=ubk, writes=[ubk])
            for t in range(NT8):
                pss = psS[d][t % 2]
                for q in range(4):
                    for c in range(2):
                        P.op("pe", lambda e, pss=pss, t=t, q=q, c=c: e.matmul(
                            pss[:, q, c, :, :], LBz[d][:, t, q, c, :], ub[d][us][:, t, :, :], start=True, stop=True),
                            reads=[ubk, pf + "LBz"], writes=[(pf, "psS", t % 2, q, c)])
                for c in range(2):
                    P.op("act", lambda e, pss=pss, t=t, c=c: e.activation(
                        out=S_sb[d][:, c, 4 * t:4 * t + 4, :, :], in_=pss[:, :, c, :, :], func=AF.Copy),
                        reads=[(pf, "psS", t % 2, q, c) for q in range(4)], writes=[(pf, "S", t, c)])
            skeys = [(pf, "S", t, c) for t in range(NT8) for c in range(2)]
            for j in range(SB):
                tb = j if d == 0 else SB - 1 - j
                hcur = stt["hcur"]
                Hc, Hn = Hs[d][hcur], Hs[d][1 - hcur]
                hk, hn = (pf, "H", hcur), (pf, "H", 1 - hcur)
                P.op(eng, lambda e, Hc=Hc: e.tensor_tensor(out=T1[d][:], in0=Hc[:], in1=bc(A2[d][:, 0, :, :]), op=ALU.mult),
                     reads=[hk, pf + "A2"], writes=[(pf, "T1")])
                P.op(eng, lambda e, Hc=Hc: e.tensor_tensor(out=T2[d][:, 0, :, :], in0=Hc[:, 1, :, :],
                                                           in1=A2[d][:, 1, 0, :].unsqueeze(2).broadcast_to([128, NG, 2]), op=ALU.mult),
                     reads=[hk, pf + "A2"], writes=[(pf, "T2a")])
                P.op(eng, lambda e, Hc=Hc: e.tensor_tensor(out=T2[d][:, 1, :, :], in0=Hc[:, 0, :, :],
                                                           in1=A2[d][:, 1, 1, :].unsqueeze(2).broadcast_to([128, NG, 2]), op=ALU.mult),
                     reads=[hk, pf + "A2"], writes=[(pf, "T2b")])
                P.op(eng, lambda e: e.tensor_tensor(out=T1[d][:], in0=T1[d][:], in1=T2[d][:], op=ALU.add),
                     reads=[(pf, "T1"), (pf, "T2a"), (pf, "T2b")], writes=[(pf, "T1")])
                P.op(eng, lambda e, Hn=Hn, tb=tb: e.tensor_tensor(out=Hn[:], in0=T1[d][:], in1=S_sb[d][:, :, :, :, tb], op=ALU.add),
                     reads=[(pf, "T1")] + (skeys if j in (0, SB - 1) else []), writes=[hn])
                if is_x:
                    P.op("act", lambda e, Hn=Hn, tb=tb: e.activation(out=hb[d][:, :, :, :, tb].rearrange("p g c b -> p c g b"),
                                                                    in_=Hn[:], func=AF.Copy),
                         reads=[hn], writes=[(pf, "hb", j)])
                stt["hcur"] = 1 - hcur
            if not is_x:
                return
            x0 = t0 - LC
            hbk = [(pf, "hb", j) for j in range(SB)]
            for t in range(NT8):
                ys = stt["yi"] % 2
                stt["yi"] += 1
                yps = psS[d][t % 2][:, 0, 0, :, :]
                first = True
                for q in range(4):
                    g = 4 * t + q
                    for c in range(2):
                        P.op("pe", lambda e, yps=yps, g=g, c=c, first=first, last=(q == 3 and c == 1): e.matmul(
                            yps, Cz[d][:, g, c, :], hb[d][:, g, c, :, :], start=first, stop=last),
                            reads=hbk + [pf + "Cz"], writes=[(pf, "psS", t % 2, 0, 0)])
                        first = False
                P.op("act", lambda e, ys=ys, yps=yps: e.activation(out=ybuf[d][ys][:, :, :], in_=yps, func=AF.Copy),
                     reads=[(pf, "psS", t % 2, 0, 0)], writes=[(pf, "yb", ys)])
                dd = P.dma("sp", lambda e, ys=ys, t=t: e.dma_start(out=ydir[d][t, :, :, x0:x0 + SB], in_=ybuf[d][ys][:, :, :]),
                           dkey=(pf, "yo", ys), reads=[(pf, "yb", ys)], writes=[("yd", d, t)])
                outs.append(dd)

        for d in range(2):
            eng = "dve" if d == 0 else "pool"
            P.op(eng, lambda e, d=d: e.memset(Hs[d][0][:].rearrange("p a b c -> p (a b c)"), 0.0), writes=[(f"d{d}", "H", 0)])
        for i in range(nblk):
            for d in range(2):
                do_block(d, orders[d][i])

        cy = [[sb(f"cy{k}_{i}", [128, 2, CH]) for i in range(2)] for k in range(2)]
        cu = [sb(f"cu{i}", [128, 2, CH], BF16) for i in range(2)]
        co = [sb(f"co{i}", [128, 2, CH], BF16) for i in range(2)]
        ci = 0
        fin = []
        for t in range(NT8):
            for x0 in range(0, LX, CH):
                s_ = ci % 2
                ci += 1
                for k in range(2):
                    P.dma("sp", lambda e, k=k, s_=s_, t=t, x0=x0: e.dma_start(out=cy[k][s_][:, :, :], in_=ydir[k][t, :, :, x0:x0 + CH]),
                          dkey=("cy", k, s_), reads=[("yd", k, t)], writes=[("cy", k, s_)])
                P.dma("sp", lambda e, s_=s_, t=t, x0=x0: e.dma_start(out=cu[s_][:, :, :], in_=u[t, :, :, LC + x0:LC + x0 + CH]),
                      dkey=("cu", s_), writes=[("cu", s_)])
                P.op("dve", lambda e, s_=s_: e.tensor_tensor(out=cy[0][s_][:], in0=cy[0][s_][:], in1=cy[1][s_][:], op=ALU.add),
                     reads=[("cy", 0, s_), ("cy", 1, s_)], writes=[("cy", 0, s_)])
                P.op("dve", lambda e, s_=s_, t=t: e.scalar_tensor_tensor(out=cy[0][s_][:], in0=cu[s_][:], scalar=dsk[:, t:t + 1],
                                                                      in1=cy[0][s_][:], op0=ALU.mult, op1=ALU.add),
                     reads=[("cy", 0, s_), ("cu", s_), "dsk"], writes=[("cy", 0, s_)])
                P.op("act", lambda e, s_=s_: e.activation(out=co[s_][:], in_=cy[0][s_][:], func=AF.Gelu),
                     reads=[("cy", 0, s_)], writes=[("co", s_)])
                fin.append(P.dma("sp", lambda e, s_=s_, t=t, x0=x0: e.dma_start(out=yg[t, :, :, x0:x0 + CH], in_=co[s_][:]),
                                 dkey=("cog", s_), reads=[("co", s_)], writes=[("yg", t, x0)]))
        P.emit(final_wait_ops=fin[-2:] + outs[-4:])
        print("s5 ops", len(P.ops), "sems", P.n_sems)
    return nc


class PsumRot:
    def __init__(self, nc, st, n=8):
        self.t = [st.enter_context(nc.psum_tensor(f"psr{i}", [128, 512], F32)) for i in range(n)]
        self.i = 0

    def next(self):
        i = self.i % len(self.t)
        self.i += 1
        return self.t[i], ("psr", i)


def outproj_residual(P, nc, ws, pr, v, vkeys, w_out, gate_sb, gate_key, xsrc, dst, c0, TT, xs, os_, cnt):
    last = []
    for oc2 in range(KC):
        buf, key = ws.load(w_out, 0, EC, oc2 * 128, 128)
        ps, pk = pr.next()
        for kc in range(EC):
            P.op("pe", lambda e, ps=ps, buf=buf, kc=kc: e.matmul(ps[:, 0:TT], buf[:, kc, 0:128], v[:, kc, :],
                                                                start=(kc == 0), stop=(kc == EC - 1)),
                 reads=[key] + (vkeys if kc in (0, EC - 1) else []), writes=[pk])
        s = cnt[0] % 2
        cnt[0] += 1
        P.dma("sp", lambda e, s=s, oc2=oc2: e.dma_start(out=xs[s][:, :], in_=xsrc[oc2 * 128:(oc2 + 1) * 128, c0:c0 + TT]),
              dkey=("xs", s), writes=[("xs", s)])
        P.op("dve", lambda e, s=s, ps=ps, oc2=oc2: e.scalar_tensor_tensor(
            out=os_[s][:, :], in0=ps[:, 0:TT], scalar=gate_sb[:, oc2:oc2 + 1], in1=xs[s][:, :], op0=ALU.mult, op1=ALU.add),
            reads=[pk, ("xs", s), gate_key], writes=[("os", s)])
        d = P.dma("sp", lambda e, s=s, oc2=oc2: e.dma_start(out=dst[oc2 * 128:(oc2 + 1) * 128, c0:c0 + TT], in_=os_[s][:, :]),
                  dkey=("oso", s), reads=[("os", s)], writes=[("dst", oc2, c0)])
        last.append(d)
    return last[-2:]


def build_stage_c(T, TT=512):
    nc = bass.Bass("TRN2", target_bir_lowering=False)
    din = lambda n, s, dt=F32: nc.dram_tensor(n, s, dt, kind="ExternalInput").ap()
    ygT = din("ygT", [E, T], BF16)
    zT = din("zT", [E, T], BF16)
    xT = din("xT", [D, T])
    gate = din("gate", [128, KC])
    w_glu = din("w_glu", [E, E])
    b_glu = din("b_glu", [128, EC])
    w_out = din("w_out", [E, D])
    x1T = nc.dram_tensor("x1T", [D, T], F32, kind="ExternalOutput").ap()
    st = contextlib.ExitStack()
    with st:
        sb = lambda name, shape, dt=F32: st.enter_context(nc.sbuf_tensor(name, shape, dt))
        P = Prog(nc)
        ws = WStream(P, nc, st, EC, 128)
        pr = PsumRot(nc, st)
        ygs = sb("ygs", [128, EC, TT], BF16)
        v = sb("v", [128, EC, TT], BF16)
        g_sb = sb("g_sb", [128, KC])
        bg_sb = sb("bg_sb", [128, EC])
        t1 = [sb(f"t1_{i}", [128, TT]) for i in range(2)]
        t2 = [sb(f"t2_{i}", [128, TT]) for i in range(2)]
        zs = [sb(f"zs{i}", [128, TT], BF16) for i in range(2)]
        xs = [sb(f"xs{i}", [128, TT]) for i in range(2)]
        os_ = [sb(f"os{i}", [128, TT]) for i in range(2)]
        P.dma("sp", lambda e: e.dma_start(out=g_sb[:, :], in_=gate[:, :]), dkey=("prm", "g"), writes=["gate"])
        P.dma("sp", lambda e: e.dma_start(out=bg_sb[:, :], in_=b_glu[:, :]), dkey=("prm", "bg"), writes=["bg"])
        ygv = ygT.rearrange("(kc p) t -> p kc t", p=128)
        cnt = [0]
        zi = 0
        fin = []
        for tt in range(T // TT):
            c0 = tt * TT
            for q in range(4):
                P.dma("sp", lambda e, q=q, c0=c0: e.dma_start(out=ygs[:, q * 16:(q + 1) * 16, :], in_=ygv[:, q * 16:(q + 1) * 16, c0:c0 + TT]),
                      dkey="ygs", writes=["ygs"])
            for oc in range(EC):
                buf, key = ws.load(w_glu, 0, EC, oc * 128, 128)
                ps, pk = pr.next()
                for kc in range(EC):
                    P.op("pe", lambda e, ps=ps, buf=buf, kc=kc: e.matmul(ps[:, 0:TT], buf[:, kc, 0:128], ygs[:, kc, :],
                                                                        start=(kc == 0), stop=(kc == EC - 1)),
                         reads=[key, "ygs"], writes=[pk])
                s = zi % 2
                zi += 1
                P.dma("sp", lambda e, s=s, oc=oc, c0=c0: e.dma_start(out=zs[s][:, :], in_=zT[oc * 128:(oc + 1) * 128, c0:c0 + TT]),
                      dkey=("zs", s), writes=[("zs", s)])
                P.op("act", lambda e, s=s, ps=ps, oc=oc: e.activation(out=t1[s][:, :], in_=ps[:, 0:TT], func=AF.Sigmoid,
                                                                     bias=bg_sb[:, oc:oc + 1]),
                     reads=[pk, "bg"], writes=[("t1", s)])
                P.op("act", lambda e, s=s: e.activation(out=t2[s][:, :], in_=zs[s][:, :], func=AF.Silu),
                     reads=[("zs", s)], writes=[("t2", s)])
                P.op("dve", lambda e, s=s, oc=oc: e.tensor_tensor(out=t1[s][:, :], in0=t1[s][:, :], in1=ygs[:, oc, :], op=ALU.mult),
                     reads=[("t1", s), "ygs"], writes=[("t1", s)])
                P.op("dve", lambda e, s=s, oc=oc: e.tensor_tensor(out=v[:, oc, :], in0=t1[s][:, :], in1=t2[s][:, :], op=ALU.mult),
                     reads=[("t1", s), ("t2", s)], writes=[("v", oc)])
            fin = outproj_residual(P, nc, ws, pr, v, [("v", oc) for oc in range(EC)], w_out, g_sb, "gate", xT, x1T, c0, TT,
                                   xs, os_, cnt)
        P.emit(final_wait_ops=fin)
        print("stage c ops", len(P.ops), "sems", P.n_sems)
    return nc


def build_stage_ada():
    nc = bass.Bass("TRN2", target_bir_lowering=False)
    cT3 = nc.dram_tensor("cT3", [128, KC, 3], F32, kind="ExternalInput").ap()
    wa = nc.dram_tensor("wa", [2, D, 1536], F32, kind="ExternalInput").ap()
    ba = nc.dram_tensor("ba", [2, 128, 12], F32, kind="ExternalInput").ap()
    modp = nc.dram_tensor("modp", [2, 128, 12, 3], F32, kind="ExternalOutput").ap()
    st = contextlib.ExitStack()
    with st:
        sb = lambda name, shape, dt=F32: st.enter_context(nc.sbuf_tensor(name, shape, dt))
        P = Prog(nc)
        ws = WStream(P, nc, st, KC, 256)
        c_sb = sb("c_sb", [128, KC, 3])
        sc = sb("sc", [128, KC, 3], BF16)
        ba_sb = sb("ba_sb", [128, 2, 12])
        res = sb("res", [128, 2, 12, 3])
        ps = st.enter_context(nc.psum_tensor("psA", [128, 2, 12, 4], F32))
        P.dma("sp", lambda e: e.dma_start(out=c_sb[:], in_=cT3[:]), dkey=("prm", "c"), writes=["c"])
        for l in range(2):
            P.dma("sp", lambda e, l=l: e.dma_start(out=ba_sb[:, l, :], in_=ba[l]), dkey=("prm", "ba", l), writes=[("ba", l)])
        P.op("act", lambda e: e.activation(out=sc[:].rearrange("p a b -> p (a b)"), in_=c_sb[:].rearrange("p a b -> p (a b)"),
                                           func=AF.Silu), reads=["c"], writes=["sc"])
        for l in range(2):
            for blk in range(6):
                buf, key = ws.load(wa[l], 0, KC, blk * 256, 256)
                for o2 in range(2):
                    oc = blk * 2 + o2
                    for kc in range(KC):
                        P.op("pe", lambda e, buf=buf, kc=kc, o2=o2, oc=oc, l=l: e.matmul(
                            ps[:, l, oc, 0:3], buf[:, kc, o2 * 128:(o2 + 1) * 128], sc[:, kc, :],
                            start=(kc == 0), stop=(kc == KC - 1)), reads=[key, "sc"], writes=["psA"])
        P.op("dve", lambda e: e.tensor_tensor(out=res[:], in0=ps[:, :, :, 0:3],
                                              in1=ba_sb[:].unsqueeze(3).broadcast_to([128, 2, 12, 3]), op=ALU.add),
             reads=["psA", ("ba", 0), ("ba", 1)], writes=["res"])
        od = P.dma("sp", lambda e: e.dma_start(out=modp.rearrange("l p o c -> p l o c"), in_=res[:]), dkey="out",
                   reads=["res"], writes=["out"])
        P.emit(final_wait_ops=[od])
    return nc


def build_stage_d(T=2048, TT=512):
    R, W = 32, 64
    RH = R + 16
    nc = bass.Bass("TRN2", target_bir_lowering=False)
    din = lambda n, s, dt=F32: nc.dram_tensor(n, s, dt, kind="ExternalInput").ap()
    uh = din("uh", [E, RH * W], BF16)
    zT = din("zT", [E, T], BF16)
    xT = din("xT", [D, T])
    gate = din("gate", [128, KC])
    pool_w = din("pool_w", [4, 2048, 2048])
    pscale = din("pscale", [128, EC])
    w_out = din("w_out", [E, D])
    fnw = din("fnw", [128, KC])
    invc = din("invc", [4, 128, T])
    vT = nc.dram_tensor("vT", [E, T], BF16, kind="ExternalOutput").ap()
    x2T = nc.dram_tensor("x2T", [D, T], F32, kind="ExternalOutput").ap()
    outT = nc.dram_tensor("outT", [D, T], F32, kind="ExternalOutput").ap()
    NTT = T // TT
    st = contextlib.ExitStack()
    with st:
        sb = lambda name, shape, dt=F32: st.enter_context(nc.sbuf_tensor(name, shape, dt))
        P = Prog(nc)
        ws = WStream(P, nc, st, EC, 128)
        pr = PsumRot(nc, st)
        big = sb("big", [128, 16 * T], BF16)
        dl = big[:].rearrange("p (i t) -> p i t", i=16)
        vt = big[:].rearrange("p (k t) -> p k t", k=EC)
        assert 16 * T == EC * TT
        g_sb = sb("g_sb", [128, KC])
        ps_sb = sb("ps_sb", [128, EC])
        fn_sb = sb("fn_sb", [128, KC])
        inv_sb = sb("inv_sb", [128, T])
        ut = [sb(f"ut{i}", [128, RH, W], BF16) for i in range(2)]
        WP = W + 16
        Wk = [sb(f"Wk{i}", [128, RH, WP]) for i in range(2)]
        t2 = [sb(f"t2_{i}", [128, TT]) for i in range(2)]
        zs = [sb(f"zs{i}", [128, TT], BF16) for i in range(2)]
        vst = [sb(f"vst{i}", [128, TT], BF16) for i in range(2)]
        xs = [sb(f"xs{i}", [128, TT]) for i in range(2)]
        os_ = [sb(f"os{i}", [128, TT]) for i in range(2)]
        x3 = [sb(f"x3_{i}", [128, T]) for i in range(2)]
        sq = [sb(f"sq{i}", [128, T], BF16) for i in range(2)]
        rstd = sb("rstd", [128, T])
        ones = sb("ones", [128, 128], BF16)
        eps_t = sb("eps_t", [128, 1])
        P.op("dve", lambda e: e.memset(ones[:, :], 1.0), writes=["ones"])
        P.op("dve", lambda e: e.memset(eps_t[:, :], EPS), writes=["eps"])
        P.dma("sp", lambda e: e.dma_start(out=g_sb[:, :], in_=gate[:, :]), dkey=("prm", "g"), writes=["gate"])
        P.dma("sp", lambda e: e.dma_start(out=ps_sb[:, :], in_=pscale[:, :]), dkey=("prm", "ps"), writes=["pscale"])
        P.dma("sp", lambda e: e.dma_start(out=fn_sb[:, :], in_=fnw[:, :]), dkey=("prm", "fn"), writes=["fnw"])
        uhv = uh.rearrange("(kc p) (r w) -> p kc r w", p=128, w=W)
        ui = 0
        zi = 0
        for k in range(4):
            P.dma("sp", lambda e, k=k: e.dma_start(out=inv_sb[:, :], in_=invc[k]), dkey=("prm", "inv"), writes=["inv"])
            for i in range(16):
                kc = 16 * k + i
                s = ui % 2
                ui += 1
                P.dma("sp", lambda e, s=s, kc=kc: e.dma_start(out=ut[s][:, :, :], in_=uhv[:, kc, :, :]), dkey=("ut", s),
                      writes=[("ut", s)])
                for lo_ in (0, WP - 8):
                    P.op("pool", lambda e, lo_=lo_: e.memset(Wk[0][:, :, lo_:lo_ + 8], 0.0), writes=["W0"])
                P.op("act", lambda e, s=s: e.activation(out=Wk[0][:, :, 8:8 + W], in_=ut[s][:, :, :], func=AF.Copy),
                     reads=[("ut", s)], writes=["W0"])
                cur = 0

                def step(fns, cur):
                    src, dst = Wk[cur], Wk[1 - cur]
                    for f in fns:
                        P.op("dve", lambda e, f=f, src=src, dst=dst: f(e, src, dst), reads=[f"W{cur}"], writes=[f"W{1 - cur}"])
                    return 1 - cur
                cur = step([lambda e, a, b: e.tensor_tensor(out=b[:, :, 1:WP], in0=a[:, :, 0:WP - 1], in1=a[:, :, 1:WP], op=ALU.add)], cur)
                for l in range(1, k + 1):
                    sh = 2 ** (l - 1)
                    cur = step([lambda e, a, b, sh=sh: e.tensor_tensor(out=b[:, :, sh:WP - sh], in0=a[:, :, 0:WP - 2 * sh], in1=a[:, :, 2 * sh:WP], op=ALU.add)], cur)
                cur = step([lambda e, a, b: e.tensor_tensor(out=b[:, 1:RH, :], in0=a[:, 0:RH - 1, :], in1=a[:, 1:RH, :], op=ALU.add)], cur)
                for l in range(1, k + 1):
                    sh = 2 ** (l - 1)
                    cur = step([lambda e, a, b, sh=sh: e.tensor_tensor(out=b[:, sh:RH - sh, :], in0=a[:, 0:RH - 2 * sh, :], in1=a[:, 2 * sh:RH, :], op=ALU.add)], cur)
                src, dst = Wk[cur], Wk[1 - cur]
                P.op("dve", lambda e, src=src, dst=dst: e.tensor_tensor(out=dst[:, 8:8 + R, 8:8 + W], in0=src[:, 8:8 + R, 8:8 + W],
                                                                        in1=inv_sb[:].rearrange("p (r w) -> p r w", w=W), op=ALU.mult),
                     reads=[f"W{cur}", "inv"], writes=[f"W{1 - cur}"])
                P.op("dve", lambda e, dst=dst, s=s, i=i: e.tensor_tensor(out=dl[:, i, :].rearrange("p (r w) -> p r w", w=W),
                                                                       in0=dst[:, 8:8 + R, 8:8 + W], in1=ut[s][:, 8:8 + R, :], op=ALU.subtract),
                     reads=[f"W{1 - cur}", ("ut", s)], writes=[("dl", i), "W0", "W1"])
            dkeys = [("dl", i) for i in range(16)]
            for oc in range(16):
                ocg = 16 * k + oc
                buf, key = ws.load(pool_w[k], 0, 16, oc * 128, 128)
                for tt in range(NTT):
                    ps, pk = pr.next()
                    for kc in range(16):
                        P.op("pe", lambda e, ps=ps, buf=buf, kc=kc, tt=tt: e.matmul(ps[:, 0:TT], buf[:, kc, 0:128],
                                                                                  dl[:, kc, tt * TT:(tt + 1) * TT],
                                                                                  start=(kc == 0), stop=(kc == 15)),
                             reads=[key] + dkeys, writes=[pk])
                    s = zi % 2
                    zi += 1
                    P.dma("sp", lambda e, s=s, ocg=ocg, tt=tt: e.dma_start(out=zs[s][:, :], in_=zT[ocg * 128:(ocg + 1) * 128, tt * TT:(tt + 1) * TT]),
                          dkey=("zs", s), writes=[("zs", s)])
                    P.op("act", lambda e, s=s: e.activation(out=t2[s][:, :], in_=zs[s][:, :], func=AF.Silu),
                         reads=[("zs", s)], writes=[("t2", s)])
                    P.op("dve", lambda e, s=s, ps=ps, ocg=ocg: e.scalar_tensor_tensor(
                        out=vst[s][:, :], in0=ps[:, 0:TT], scalar=ps_sb[:, ocg:ocg + 1], in1=t2[s][:, :], op0=ALU.mult, op1=ALU.mult),
                        reads=[pk, ("t2", s), "pscale"], writes=[("vst", s)])
                    P.dma("sp", lambda e, s=s, ocg=ocg, tt=tt: e.dma_start(out=vT[ocg * 128:(ocg + 1) * 128, tt * TT:(tt + 1) * TT], in_=vst[s][:, :]),
                          dkey=("vso", s), reads=[("vst", s)], writes=[("vT", ocg, tt)])
        vTv = vT.rearrange("(kc p) t -> p kc t", p=128)
        cnt = [0]
        for tt in range(NTT):
            c0 = tt * TT
            for q in range(4):
                P.dma("sp", lambda e, q=q, c0=c0: e.dma_start(out=vt[:, q * 16:(q + 1) * 16, :], in_=vTv[:, q * 16:(q + 1) * 16, c0:c0 + TT]),
                      dkey="vtl", reads=[("vT", ocg, tt) for ocg in range(EC)], writes=["vtile"] + [("dl", i) for i in range(16)])
            outproj_residual(P, nc, ws, pr, vt, ["vtile"], w_out, g_sb, "gate", xT, x2T, c0, TT, xs, os_, cnt)
        x2v = x2T.rearrange("(kc p) t -> p kc t", p=128)
        outv = outT.rearrange("(kc p) t -> p kc t", p=128)
        dst_keys = lambda kc: [("dst", kc, tt * TT) for tt in range(NTT)]
        for kc in range(KC):
            s = kc % 2
            P.dma("sp", lambda e, s=s, kc=kc: e.dma_start(out=x3[s][:, :], in_=x2v[:, kc, :]), dkey=("x3", s),
                  reads=dst_keys(kc), writes=[("x3", s)])
            P.op("act", lambda e, s=s: e.activation(out=sq[s][:, :], in_=x3[s][:, :], func=AF.Square),
                 reads=[("x3", s)], writes=[("sq", s)])
            for tt in range(NTT):
                P.op("pe", lambda e, s=s, tt=tt, kc=kc: e.matmul(pr.t[tt][:, 0:TT], ones[:, :], sq[s][:, tt * TT:(tt + 1) * TT],
                                                              start=(kc == 0), stop=(kc == KC - 1)),
                     reads=[("sq", s), "ones"], writes=[("psr", tt)])
        for tt in range(NTT):
            P.op("act", lambda e, tt=tt: e.activation(out=rstd[:, tt * TT:(tt + 1) * TT], in_=pr.t[tt][:, 0:TT],
                                                     func=AF.Sqrt, scale=1.0 / D, bias=eps_t[:, 0:1]),
                 reads=[("psr", tt), "eps"], writes=[("rs", tt)])
            P.op("dve", lambda e, tt=tt: e.reciprocal(out=rstd[:, tt * TT:(tt + 1) * TT], in_=rstd[:, tt * TT:(tt + 1) * TT]),
                 reads=[("rs", tt)], writes=[("rs", tt)])
        fin = []
        for kc in range(KC):
            s = kc % 2
            P.dma("sp", lambda e, s=s, kc=kc: e.dma_start(out=x3[s][:, :], in_=x2v[:, kc, :]), dkey=("x3", s),
                  reads=dst_keys(kc), writes=[("x3", s)])
            P.op("dve", lambda e, s=s: e.tensor_tensor(out=x3[s][:, :], in0=x3[s][:, :], in1=rstd[:, :], op=ALU.mult),
                 reads=[("x3", s)] + [("rs", tt) for tt in range(NTT)], writes=[("x3", s)])
            P.op("act", lambda e, s=s, kc=kc: e.activation(out=x3[s][:, :], in_=x3[s][:, :], func=AF.Copy, scale=fn_sb[:, kc:kc + 1]),
                 reads=[("x3", s), "fnw"], writes=[("x3", s)])
            fin.append(P.dma("sp", lambda e, s=s, kc=kc: e.dma_start(out=outv[:, kc, :], in_=x3[s][:, :]), dkey=("x3o", s),
                             reads=[("x3", s)], writes=[("out", kc)]))
        P.emit(final_wait_ops=fin[-2:])
        print("stage d ops", len(P.ops), "sems", P.n_sems)
    return nc


def inv_counts(q):
    out = np.zeros((4, 32 * 64), np.float32)
    for k, w in enumerate((2, 4, 8, 16)):
        r = np.arange(32 * q, 32 * q + 32)
        c = np.arange(64)
        rc = np.clip(r + w - w // 2, 0, 128) - np.clip(r - w // 2, 0, 128)
        cc = np.clip(c + w - w // 2, 0, 64) - np.clip(c - w // 2, 0, 64)
        out[k] = (1.0 / (rc[:, None] * cc[None, :]).astype(np.float32)).reshape(-1)
    return out


def s5_host_params(lam_re, lam_im, ls, b_re, b_im, c_re, c_im, g0, NT8):
    NG = 4 * NT8
    BpT = np.zeros((2, 2, 128, NT8, 128), np.float32)
    lamR = np.zeros((3, 2, 128, NT8, 128), np.float32)
    Cm = np.zeros((2, 2, 128, NG, 32), np.float32)
    lamM = np.zeros((3, 2, 128, NG), np.float32)
    for d in range(2):
        for g in range(NG):
            t, q = g // 4, g % 4
            for h in range(2):
                G = 2 * (g0 + g) + h
                ms = slice(64 * h, 64 * h + 64)
                rows = slice(32 * q + 16 * h, 32 * q + 16 * h + 16)
                BpT[0, d, rows, t, ms] = b_re[d, G].T
                BpT[1, d, rows, t, ms] = b_im[d, G].T
                lamR[0, d, 32 * q:32 * q + 32, t, ms] = lam_re[d, G][None, :]
                lamR[1, d, 32 * q:32 * q + 32, t, ms] = lam_im[d, G][None, :]
                lamR[2, d, 32 * q:32 * q + 32, t, ms] = ls[d, G]
                Cm[0, d, ms, g, 16 * h:16 * h + 16] = c_re[d, G].T
                Cm[1, d, ms, g, 16 * h:16 * h + 16] = c_im[d, G].T
                lamM[0, d, ms, g] = lam_re[d, G]
                lamM[1, d, ms, g] = lam_im[d, G]
                lamM[2, d, ms, g] = ls[d, G]
    return dict(BpT_re=BpT[0], BpT_im=BpT[1], lamR_re=lamR[0], lamR_im=lamR[1], lsR=lamR[2],
                Cm_re=Cm[0], Cm_im=Cm[1], lamM_re=lamM[0], lamM_im=lamM[1], lsM=lamM[2])


from concourse.bass_utils import run_bass_kernel_spmd

_BF = ml_dtypes.bfloat16


def _run(nc, ins):
    res = run_bass_kernel_spmd(nc, ins, core_ids=list(range(8)))
    return res.results


def kernel(x, c, ctx, c_ctx, norm_w, w_ada, b_ada, w_in, w_out, s5_lam_re, s5_lam_im, s5_log_step, s5_b_re, s5_b_im,
           s5_c_re, s5_c_im, s5_d, s5_w_glu, s5_b_glu, pool_w, pool_scale, final_norm_w):
    f32 = lambda a: np.asarray(a, np.float32)
    x, c, ctx, c_ctx, norm_w, w_ada, b_ada, w_in, w_out = map(f32, (x, c, ctx, c_ctx, norm_w, w_ada, b_ada, w_in, w_out))
    T = 2048
    nc_ada = build_stage_ada()
    cv = np.stack([colv(c[0]), colv(c[1]), colv(c_ctx)], axis=-1)
    ins = []
    for j in range(8):
        wa = np.ascontiguousarray(w_ada.reshape(2, D, 8, 1536)[:, :, j, :])
        ba = np.stack([colv(b_ada[l].reshape(8, 1536)[j]) for l in range(2)])
        ins.append(dict(cT3=cv, wa=wa, ba=ba))
    r = _run(nc_ada, ins)
    mp = np.stack([np.asarray(q["modp"]) for q in r])
    mod = np.ascontiguousarray(mp.transpose(1, 4, 2, 0, 3).reshape(2, 3, 128, 96))

    nc_a = build_stage_a(T, 2 * E)
    ins = []
    for core in range(8):
        b, q = core // 4, core % 4
        ins.append(dict(xT=np.ascontiguousarray(x[b, q * T:(q + 1) * T, :].T), modT=mod[0, b], nw=colv(norm_w[0]),
                        w_in=w_in[0]))
    r = _run(nc_a, ins)
    uz0 = [np.asarray(q["uz"]) for q in r]
    nc_ac = build_stage_a(64, E)
    w_in0_u = np.ascontiguousarray(w_in[0][:, :E])
    ins = []
    for core in range(8):
        b, q = core // 4, core % 4
        ins.append(dict(xT=np.ascontiguousarray(ctx[b, q * 64:(q + 1) * 64, :].T), modT=mod[0, 2], nw=colv(norm_w[0]),
                        w_in=w_in0_u))
    r = _run(nc_ac, ins)
    uc0 = [np.asarray(q["uz"]) for q in r]
    del w_in0_u

    LC, LX = 256, 8192
    U = [np.concatenate([uc0[4 * b + q] for q in range(4)] + [uz0[4 * b + q][:E] for q in range(4)], axis=1) for b in range(2)]
    nc_s5 = build_stage_s5(LC, LX, 8)
    ins = []
    lam_re, lam_im, ls = f32(s5_lam_re)[0], f32(s5_lam_im)[0], f32(s5_log_step)[0]
    b_re, b_im, c_re, c_im = f32(s5_b_re)[0], f32(s5_b_im)[0], f32(s5_c_re)[0], f32(s5_c_im)[0]
    for j in range(8):
        uj = np.stack([U[b][j * 1024:(j + 1) * 1024].reshape(8, 128, LC + LX) for b in range(2)], axis=2)
        prm = s5_host_params(lam_re, lam_im, ls, b_re, b_im, c_re, c_im, j * 32, 8)
        ins.append(dict(u=np.ascontiguousarray(uj), dskip=colv(f32(s5_d)[0][j * 1024:(j + 1) * 1024]), **prm))
    r = _run(nc_s5, ins)
    del U
    YG = [np.concatenate([np.asarray(r[j]["yg"])[:, :, b, :].reshape(1024, LX) for j in range(8)], axis=0) for b in range(2)]

    nc_c = build_stage_c(T)
    ins = []
    for core in range(8):
        b, q = core // 4, core % 4
        ins.append(dict(ygT=np.ascontiguousarray(YG[b][:, q * T:(q + 1) * T]), zT=np.ascontiguousarray(uz0[core][E:]),
                        xT=np.ascontiguousarray(x[b, q * T:(q + 1) * T, :].T), gate=np.ascontiguousarray(mod[0, b][:, 64:96]),
                        w_glu=f32(s5_w_glu)[0], b_glu=colv(f32(s5_b_glu)[0]), w_out=w_out[0]))
    r = _run(nc_c, ins)
    x1T = [np.asarray(q["x1T"]) for q in r]
    del YG, uz0, uc0

    ins = []
    for core in range(8):
        b = core // 4
        ins.append(dict(xT=x1T[core], modT=mod[1, b], nw=colv(norm_w[1]), w_in=w_in[1]))
    r = _run(nc_a, ins)
    uz1 = [np.asarray(q["uz"]) for q in r]

    nc_d = build_stage_d()
    ins = []
    for core in range(8):
        b, q = core // 4, core % 4
        uh = np.zeros((E, 48, 64), _BF)
        uh[:, 8:40, :] = uz1[core][:E].reshape(E, 32, 64)
        if q > 0:
            uh[:, 0:8, :] = uz1[core - 1][:E].reshape(E, 32, 64)[:, 24:32, :]
        if q < 3:
            uh[:, 40:48, :] = uz1[core + 1][:E].reshape(E, 32, 64)[:, 0:8, :]
        inv = np.ascontiguousarray(np.broadcast_to(inv_counts(q)[:, None, :], (4, 128, T)))
        ins.append(dict(uh=uh.reshape(E, 48 * 64), zT=np.ascontiguousarray(uz1[core][E:]), xT=x1T[core],
                        gate=np.ascontiguousarray(mod[1, b][:, 64:96]), pool_w=f32(pool_w)[0], pscale=colv(f32(pool_scale)[0]),
                        w_out=w_out[1], fnw=colv(f32(final_norm_w)), invc=inv))
    r = _run(nc_d, ins)
    out = np.empty((2, 8192, D), np.float32)
    for core in range(8):
        b, q = core // 4, core % 4
        out[b, q * T:(q + 1) * T, :] = np.asarray(r[core]["outT"]).T
    return out
```

```python
import numpy as np
import concourse.bass as bass
import concourse.mybir as mybir

F32 = mybir.dt.float32
BF16 = mybir.dt.bfloat16
ALU = mybir.AluOpType
AF = mybir.ActivationFunctionType

ENGS = ("pe", "act", "dve", "pool", "sp")
SEM_EPOCH = 30000


class Op:
    __slots__ = ("eng", "fn", "deps", "is_dma", "dkey", "sig", "idx", "dma_wait")

    def __init__(self, eng, fn, is_dma=False, dkey=None):
        self.eng = eng
        self.fn = fn
        self.deps = []
        self.is_dma = is_dma
        self.dkey = dkey
        self.sig = False
        self.idx = None
        self.dma_wait = None


class Prog:
    def __init__(self, nc, same_engine_sync=True):
        self.nc = nc
        self.ops = []
        self.last_w = {}
        self.readers = {}
        self.same_engine_sync = same_engine_sync
        self.dma_count = {}
        self.ctx = []

    def _add(self, op, reads, writes):
        deps = []
        for k in reads:
            w = self.last_w.get(k)
            if w is not None:
                deps.append(w)
        for k in writes:
            w = self.last_w.get(k)
            if w is not None:
                deps.append(w)
            for r in self.readers.get(k, ()):
                deps.append(r)
        seen = set()
        for d in deps:
            if id(d) in seen or d is op:
                continue
            seen.add(id(d))
            if (not d.is_dma) and d.eng == op.eng and not op.is_dma:
                ses = self.same_engine_sync
                if d.eng == "pe" or ses is False or (ses is not True and d.eng not in ses):
                    continue
            op.deps.append(d)
            d.sig = True
        for k in reads:
            self.readers.setdefault(k, []).append(op)
        for k in writes:
            self.last_w[k] = op
            self.readers[k] = []
        self.ops.append(op)
        return op

    def op(self, eng, fn, reads=(), writes=()):
        return self._add(Op(eng, fn), reads, writes)

    def dma(self, eng, fn, dkey, reads=(), writes=()):
        o = Op(eng, fn, is_dma=True, dkey=dkey)
        ep, cnt = self.dma_count.get(dkey, (0, 0))
        if cnt + 16 > SEM_EPOCH:
            ep, cnt = ep + 1, 0
        cnt += 16
        self.dma_count[dkey] = (ep, cnt)
        o.dma_wait = (dkey, ep, cnt)
        return self._add(o, reads, writes)

    def emit(self, final_wait_ops=()):
        nc = self.nc
        counters = {e: [0, 0] for e in ENGS}
        sig_of = {}
        for o in self.ops:
            if o.is_dma:
                continue
            if o.sig:
                c = counters[o.eng]
                if c[1] + 1 > SEM_EPOCH:
                    c[0] += 1
                    c[1] = 0
                c[1] += 1
                sig_of[id(o)] = (("eng", o.eng), c[0], c[1])
        semkeys = set()
        for o in self.ops:
            if o.is_dma:
                semkeys.add((("dma", o.dkey), o.dma_wait[1]))
            elif o.sig:
                k = sig_of[id(o)]
                semkeys.add((k[0], k[1]))
        semkeys = sorted(semkeys, key=repr)
        self.n_sems = len(semkeys)
        sems = {}
        import contextlib
        stack = contextlib.ExitStack()
        for i, k in enumerate(semkeys):
            sems[k] = stack.enter_context(nc.semaphore(f"s{i}"))
        per_eng = {e: [] for e in ENGS}
        for o in self.ops:
            per_eng[o.eng].append(o)

        def wait_target(d):
            if d.is_dma:
                return (("dma", d.dkey), d.dma_wait[1]), d.dma_wait[2]
            k = sig_of[id(d)]
            return (k[0], k[1]), k[2]

        def run(engname, eng):
            waited = {}
            for o in per_eng[engname]:
                for d in o.deps:
                    sk, val = wait_target(d)
                    if waited.get(sk, 0) >= val:
                        continue
                    waited[sk] = val
                    eng.wait_ge(sems[sk], val)
                ins = o.fn(eng)
                if o.is_dma:
                    ins.then_inc(sems[(("dma", o.dkey), o.dma_wait[1])], 16)
                elif o.sig:
                    k = sig_of[id(o)]
                    ins.then_inc(sems[(k[0], k[1])], 1)
            if engname == "sp":
                for d in final_wait_ops:
                    sk, val = wait_target(d)
                    eng.wait_ge(sems[sk], val)

        with stack:
            with nc.Block() as block:
                @block.tensor
                def _(e):
                    run("pe", e)

                @block.scalar
                def _(e):
                    run("act", e)

                @block.vector
                def _(e):
                    run("dve", e)

                @block.gpsimd
                def _(e):
                    run("pool", e)

                @block.sync
                def _(e):
                    run("sp", e)


import contextlib
import ml_dtypes

D = 4096
E = 8192
KC = D // 128
EC = E // 128
EPS = 1e-6


def colv(v):
    v = np.ascontiguousarray(v, dtype=np.float32)
    return np.ascontiguousarray(v.reshape(-1, 128).T)


class WStream:
    def __init__(self, P, nc, st, kmax, wb, name="wt", nbuf=2):
        self.P, self.nc = P, nc
        self.wb = wb
        self.kmax = kmax
        self.bufs = [st.enter_context(nc.sbuf_tensor(f"{name}{i}", [128, kmax, wb], BF16)) for i in range(nbuf)]
        self.i = 0
        self.name = name

    def load(self, w, kc0, nkc, n0, ncols):
        s = self.i % len(self.bufs)
        self.i += 1
        buf = self.bufs[s]
        key = (self.name, s)
        wv = w.rearrange("(kc p) n -> p kc n", p=128)
        step = 8
        for q in range(0, nkc, step):
            qn = min(step, nkc - q)
            self.P.dma("pool", lambda e, q=q, qn=qn: e.dma_start(
                out=buf[:, q:q + qn, 0:ncols], in_=wv[:, kc0 + q:kc0 + q + qn, n0:n0 + ncols]),
                dkey=key, writes=[key])
        return buf, key


def build_stage_a(T, NOUT, with_gate=True):
    nc = bass.Bass("TRN2", target_bir_lowering=False)
    xT = nc.dram_tensor("xT", [D, T], F32, kind="ExternalInput").ap()
    modT = nc.dram_tensor("modT", [128, 96], F32, kind="ExternalInput").ap()
    nw = nc.dram_tensor("nw", [128, KC], F32, kind="ExternalInput").ap()
    w_in = nc.dram_tensor("w_in", [D, NOUT], F32, kind="ExternalInput").ap()
    uz = nc.dram_tensor("uz", [NOUT, T], BF16, kind="ExternalOutput").ap()
    TT = min(512, T)
    NT = T // TT
    WB = 256
    st = contextlib.ExitStack()
    with st:
        sb = lambda name, shape, dt: st.enter_context(nc.sbuf_tensor(name, shape, dt))
        hT = sb("hT", [128, KC, T], BF16)
        c_sb = sb("c_sb", [128, KC], F32)
        sc = sb("sc", [128, KC], BF16)
        nw_sb = sb("nw_sb", [128, KC], F32)
        ba_sb = sb("ba_sb", [128, 96], F32)
        mod = sb("mod", [128, 96], F32)
        a1 = sb("a1", [128, KC], F32)
        ones = sb("ones", [128, 128], BF16)
        xs = [sb(f"xs{i}", [128, T], F32) for i in range(2)]
        sq = [sb(f"sq{i}", [128, T], BF16) for i in range(2)]
        rstd = sb("rstd", [128, T], F32)
        ot = [sb(f"ot{i}", [128, T], BF16) for i in range(2)]
        ps = [st.enter_context(nc.psum_tensor(f"ps{i}", [128, 512], F32)) for i in range(8)]
        P = Prog(nc)
        ws = WStream(P, nc, st, KC, WB)
        P.dma("sp", lambda e: e.dma_start(out=nw_sb[:, :], in_=nw[:, :]), dkey=("prm", "nw"), writes=["nw"])
        P.dma("sp", lambda e: e.dma_start(out=mod[:, :], in_=modT[:, :]), dkey=("prm", "mod"), writes=["mod"])
        P.op("dve", lambda e: e.memset(ones[:, :], 1.0), writes=["ones"])
        P.op("dve", lambda e: e.scalar_tensor_tensor(out=a1[:, :], in0=mod[:, KC:2 * KC], scalar=1.0, in1=nw_sb[:, :],
                                                     op0=ALU.add, op1=ALU.mult), reads=["mod", "nw"], writes=["a1"])
        xv = xT.rearrange("(kc p) t -> p kc t", p=128)
        for kc in range(KC):
            s = kc % 2
            P.dma("sp", lambda e, s=s, kc=kc: e.dma_start(out=xs[s][:, :], in_=xv[:, kc, :]), dkey=("xs", s),
                  writes=[("xs", s)])
            P.op("act", lambda e, s=s: e.activation(out=sq[s][:, :], in_=xs[s][:, :], func=AF.Square),
                 reads=[("xs", s)], writes=[("sq", s)])
            for tt in range(NT):
                P.op("pe", lambda e, s=s, tt=tt, kc=kc: e.matmul(
                    ps[tt][:, 0:TT], ones[:, :], sq[s][:, tt * TT:(tt + 1) * TT], start=(kc == 0), stop=(kc == KC - 1)),
                    reads=[("sq", s), "ones"], writes=[("ps", tt)])
        eps_t = sb("eps_t", [128, 1], F32)
        P.op("dve", lambda e: e.memset(eps_t[:, :], EPS), writes=["eps"])
        for tt in range(NT):
            P.op("act", lambda e, tt=tt: e.activation(out=rstd[:, tt * TT:(tt + 1) * TT], in_=ps[tt][:, 0:TT],
                                                     func=AF.Sqrt, scale=1.0 / D, bias=eps_t[:, 0:1]),
                 reads=[("ps", tt), "eps"], writes=[("rs", tt)])
            P.op("dve", lambda e, tt=tt: e.reciprocal(out=rstd[:, tt * TT:(tt + 1) * TT],
                                                      in_=rstd[:, tt * TT:(tt + 1) * TT]),
                 reads=[("rs", tt)], writes=[("rs", tt)])
        for kc in range(KC):
            s = kc % 2
            P.dma("sp", lambda e, s=s, kc=kc: e.dma_start(out=xs[s][:, :], in_=xv[:, kc, :]), dkey=("xs", s),
                  writes=[("xs", s)])
            P.op("dve", lambda e, s=s: e.tensor_tensor(out=xs[s][:, :], in0=xs[s][:, :], in1=rstd[:, :], op=ALU.mult),
                 reads=[("xs", s)] + [("rs", tt) for tt in range(NT)], writes=[("xs", s)])
            P.op("act", lambda e, s=s, kc=kc: e.activation(out=hT[:, kc, :], in_=xs[s][:, :], func=AF.Identity,
                                                          scale=a1[:, kc:kc + 1], bias=mod[:, kc:kc + 1]),
                 reads=[("xs", s), "a1", "mod"], writes=[("h", kc)])
        hkeys = [("h", kc) for kc in range(KC)]
        pi = 0
        oi = 0
        outs = []
        for blk in range(NOUT // WB):
            buf, key = ws.load(w_in, 0, KC, blk * WB, WB)
            for o2 in range(WB // 128):
                osl = oi % 2
                oi += 1
                for tt in range(NT):
                    pb = pi % 8
                    pi += 1
                    for kc in range(KC):
                        P.op("pe", lambda e, pb=pb, buf=buf, kc=kc, o2=o2, tt=tt: e.matmul(
                            ps[pb][:, 0:TT], buf[:, kc, o2 * 128:(o2 + 1) * 128], hT[:, kc, tt * TT:(tt + 1) * TT],
                            start=(kc == 0), stop=(kc == KC - 1)),
                            reads=[key] + (hkeys if kc == 0 else []), writes=[("ps", pb) if pb < 4 else ("psx", pb)] )
                    pk = ("ps", pb) if pb < 4 else ("psx", pb)
                    if pi % 2 == 0:
                        P.op("act", lambda e, pb=pb, osl=osl, tt=tt: e.activation(
                            out=ot[osl][:, tt * TT:(tt + 1) * TT], in_=ps[pb][:, 0:TT], func=AF.Copy),
                            reads=[pk], writes=[("ot", osl, tt)])
                    else:
                        P.op("dve", lambda e, pb=pb, osl=osl, tt=tt: e.tensor_copy(
                            out=ot[osl][:, tt * TT:(tt + 1) * TT], in_=ps[pb][:, 0:TT]),
                            reads=[pk], writes=[("ot", osl, tt)])
                r0 = blk * WB + o2 * 128
                d = P.dma("sp", lambda e, osl=osl, r0=r0: e.dma_start(out=uz[r0:r0 + 128, :], in_=ot[osl][:, :]),
                          dkey=("ost", osl), reads=[("ot", osl, tt) for tt in range(NT)], writes=[("uz", r0)])
                outs.append(d)
        P.emit(final_wait_ops=outs[-2:])
    return nc


TWO_PI = 2.0 * np.pi


def build_stage_s5(LC, LX, NT8=8, SB=64, CH=256, SES=True):
    L = LC + LX
    NG = 4 * NT8
    nc = bass.Bass("TRN2", target_bir_lowering=False)
    din = lambda n, s, dt=F32: nc.dram_tensor(n, s, dt, kind="ExternalInput").ap()
    u = din("u", [NT8, 128, 2, L], BF16)
    BpT = [din("BpT_re", [2, 128, NT8, 128]), din("BpT_im", [2, 128, NT8, 128])]
    lamR = [din("lamR_re", [2, 128, NT8, 128]), din("lamR_im", [2, 128, NT8, 128]), din("lsR", [2, 128, NT8, 128])]
    Cm = [din("Cm_re", [2, 128, NG, 32]), din("Cm_im", [2, 128, NG, 32])]
    lamM = [din("lamM_re", [2, 128, NG]), din("lamM_im", [2, 128, NG]), din("lsM", [2, 128, NG])]
    dskip = din("dskip", [128, NT8])
    ydir = [nc.dram_tensor(f"yd{d}", [NT8, 128, 2, LX], F32, kind="ExternalOutput").ap() for d in range(2)]
    yg = nc.dram_tensor("yg", [NT8, 128, 2, LX], BF16, kind="ExternalOutput").ap()
    st = contextlib.ExitStack()
    with st:
        sb = lambda name, shape, dt=F32: st.enter_context(nc.sbuf_tensor(name, shape, dt))
        P = Prog(nc, same_engine_sync=SES)
        MAGIC = 12582912.0

        def lam_bar(pref, lre, lim, ls, n, keys):
            t = {k: sb(f"{pref}_{k}", [128, n]) for k in ("dt", "a", "th", "mag", "cs", "sn", "are", "aim", "k")}
            P.op("act", lambda e: e.activation(out=t["dt"][:], in_=ls, func=AF.Exp), reads=keys, writes=[pref + "dt"])
            P.op("dve", lambda e: e.tensor_tensor(out=t["a"][:], in0=lre, in1=t["dt"][:], op=ALU.mult),
                 reads=keys + [pref + "dt"], writes=[pref + "a"])
            P.op("dve", lambda e: e.tensor_tensor(out=t["th"][:], in0=lim, in1=t["dt"][:], op=ALU.mult),
                 reads=keys + [pref + "dt"], writes=[pref + "th"])
            P.op("act", lambda e: e.activation(out=t["mag"][:], in_=t["a"][:], func=AF.Exp), reads=[pref + "a"],
                 writes=[pref + "mag"])
            for nm, off in (("sn", 0.0), ("cs", 0.25)):
                P.op("dve", lambda e, nm=nm, off=off: e.tensor_scalar(out=t[nm][:], in0=t["th"][:], scalar1=1.0 / TWO_PI,
                                                                     scalar2=off, op0=ALU.mult, op1=ALU.add),
                     reads=[pref + "th"], writes=[pref + nm])
                P.op("dve", lambda e, nm=nm: e.tensor_scalar(out=t["k"][:], in0=t[nm][:], scalar1=MAGIC, scalar2=None,
                                                             op0=ALU.add), reads=[pref + nm], writes=[pref + "k"])
                P.op("dve", lambda e, nm=nm: e.tensor_scalar(out=t["k"][:], in0=t["k"][:], scalar1=-MAGIC, scalar2=None,
                                                             op0=ALU.add), reads=[pref + "k"], writes=[pref + "k"])
                P.op("dve", lambda e, nm=nm: e.tensor_tensor(out=t[nm][:], in0=t[nm][:], in1=t["k"][:], op=ALU.subtract),
                     reads=[pref + nm, pref + "k"], writes=[pref + nm])
                P.op("act", lambda e, nm=nm: e.activation(out=t[nm][:], in_=t[nm][:], func=AF.Sin, scale=TWO_PI),
                     reads=[pref + nm], writes=[pref + nm])
            P.op("dve", lambda e: e.tensor_tensor(out=t["are"][:], in0=t["mag"][:], in1=t["cs"][:], op=ALU.mult),
                 reads=[pref + "mag", pref + "cs"], writes=[pref + "are"])
            P.op("dve", lambda e: e.tensor_tensor(out=t["aim"][:], in0=t["mag"][:], in1=t["sn"][:], op=ALU.mult),
                 reads=[pref + "mag", pref + "sn"], writes=[pref + "aim"])
            return t["are"], t["aim"]

        LBz = [sb(f"LBz{d}", [128, NT8, 4, 2, 128], BF16) for d in range(2)]
        Cz = [sb(f"Cz{d}", [128, NG, 2, 128], BF16) for d in range(2)]
        A2 = [sb(f"A2_{d}", [128, 2, 2, NG]) for d in range(2)]
        dsk = sb("dsk", [128, NT8])
        P.dma("sp", lambda e: e.dma_start(out=dsk[:, :], in_=dskip[:, :]), dkey=("prm", "dsk"), writes=["dsk"])
        cm = [sb(f"cm{i}", [128, NG, 32]) for i in range(2)]
        lr = [sb(f"lr{i}", [128, 128]) for i in range(3)]
        bp = [sb(f"bp{i}", [128, 128]) for i in range(2)]
        ktmp = {k: sb(f"k_{k}", [128, 128]) for k in ("nre", "den", "t1", "t2", "kre", "kim", "o1", "o2")}

        def dir_params(d):
            pf = f"d{d}"
            lm = [sb(f"{pf}lm{i}", [128, NG]) for i in range(3)]
            for i in range(3):
                P.dma("sp", lambda e, i=i: e.dma_start(out=lm[i][:, :], in_=lamM[i][d]), dkey=("prm", "lm", d, i),
                      writes=[pf + f"lm{i}"])
            are, aim = lam_bar(pf + "M", lm[0][:], lm[1][:], lm[2][:], NG, [pf + f"lm{i}" for i in range(3)])
            for c in range(2):
                P.op("dve", lambda e, c=c: e.tensor_copy(out=A2[d][:, 0, c, :], in_=are[:, :]), reads=[pf + "Mare"],
                     writes=[pf + "A2"])
            P.op("dve", lambda e: e.tensor_scalar(out=A2[d][:, 1, 0, :], in0=aim[:, :], scalar1=-1.0, scalar2=None,
                                                  op0=ALU.mult), reads=[pf + "Maim"], writes=[pf + "A2"])
            P.op("dve", lambda e: e.tensor_copy(out=A2[d][:, 1, 1, :], in_=aim[:, :]), reads=[pf + "Maim"],
                 writes=[pf + "A2"])
            for i in range(2):
                P.dma("sp", lambda e, i=i: e.dma_start(out=cm[i][:, :, :], in_=Cm[i][d]), dkey=("prm", "cm", i),
                      writes=[f"cm{i}"])
            P.op("pool", lambda e: e.memset(Cz[d][:].rearrange("p a b c -> p (a b c)"), 0.0), writes=[pf + "Cz"])
            for q in range(4):
                P.op("dve", lambda e, q=q: e.tensor_copy(out=Cz[d][:, q::4, 0, 32 * q:32 * q + 32], in_=cm[0][:, q::4, :]),
                     reads=["cm0", "cm1"], writes=[pf + "Cz"])
                P.op("dve", lambda e, q=q: e.tensor_scalar(out=Cz[d][:, q::4, 1, 32 * q:32 * q + 32], in0=cm[1][:, q::4, :],
                                                           scalar1=-1.0, scalar2=None, op0=ALU.mult),
                     reads=["cm0", "cm1"], writes=[pf + "Cz"])
            P.op("pool", lambda e: e.memset(LBz[d][:].rearrange("p a b c m -> p (a b c m)"), 0.0), writes=[pf + "LBz"])
            for t in range(NT8):
                for i in range(3):
                    P.dma("sp", lambda e, i=i, t=t: e.dma_start(out=lr[i][:, :], in_=lamR[i][d, :, t, :]),
                          dkey=("prm", "lr", i), writes=[f"lr{i}"])
                for i in range(2):
                    P.dma("sp", lambda e, i=i, t=t: e.dma_start(out=bp[i][:, :], in_=BpT[i][d, :, t, :]),
                          dkey=("prm", "bp", i), writes=[f"bp{i}"])
                lam_bar_again("R", lr, ["lr0", "lr1", "lr2"])
                rre, rim = Rt["are"], Rt["aim"]
                K = lambda k: ktmp[k][:]
                ops = [
                    lambda e: e.tensor_scalar(out=K("nre"), in0=rre[:], scalar1=-1.0, scalar2=None, op0=ALU.add),
                    lambda e: e.tensor_tensor(out=K("t1"), in0=lr[0][:], in1=lr[0][:], op=ALU.mult),
                    lambda e: e.tensor_tensor(out=K("t2"), in0=lr[1][:], in1=lr[1][:], op=ALU.mult),
                    lambda e: e.tensor_tensor(out=K("den"), in0=K("t1"), in1=K("t2"), op=ALU.add),
                    lambda e: e.reciprocal(out=K("den"), in_=K("den")),
                    lambda e: e.tensor_tensor(out=K("t1"), in0=K("nre"), in1=lr[0][:], op=ALU.mult),
                    lambda e: e.tensor_tensor(out=K("t2"), in0=rim[:], in1=lr[1][:], op=ALU.mult),
                    lambda e: e.tensor_tensor(out=K("t1"), in0=K("t1"), in1=K("t2"), op=ALU.add),
                    lambda e: e.tensor_tensor(out=K("kre"), in0=K("t1"), in1=K("den"), op=ALU.mult),
                    lambda e: e.tensor_tensor(out=K("t1"), in0=rim[:], in1=lr[0][:], op=ALU.mult),
                    lambda e: e.tensor_tensor(out=K("t2"), in0=K("nre"), in1=lr[1][:], op=ALU.mult),
                    lambda e: e.tensor_tensor(out=K("t1"), in0=K("t1"), in1=K("t2"), op=ALU.subtract),
                    lambda e: e.tensor_tensor(out=K("kim"), in0=K("t1"), in1=K("den"), op=ALU.mult),
                    lambda e: e.tensor_tensor(out=K("t1"), in0=K("kre"), in1=bp[0][:], op=ALU.mult),
                    lambda e: e.tensor_tensor(out=K("t2"), in0=K("kim"), in1=bp[1][:], op=ALU.mult),
                    lambda e: e.tensor_tensor(out=K("o1"), in0=K("t1"), in1=K("t2"), op=ALU.subtract),
                    lambda e: e.tensor_tensor(out=K("t1"), in0=K("kre"), in1=bp[1][:], op=ALU.mult),
                    lambda e: e.tensor_tensor(out=K("t2"), in0=K("kim"), in1=bp[0][:], op=ALU.mult),
                    lambda e: e.tensor_tensor(out=K("o2"), in0=K("t1"), in1=K("t2"), op=ALU.add),
                ]
                for f in ops:
                    P.op("dve", f, reads=["kap", "lr0", "lr1", "lr2", "bp0", "bp1", "Rare", "Raim"], writes=["kap"])
                for q in range(3):
                    for c, nm in ((0, "o1"), (1, "o2")):
                        P.op("dve", lambda e, q=q, c=c, nm=nm, t=t: e.tensor_copy(
                            out=LBz[d][32 * q:32 * q + 32, t, q, c, :], in_=ktmp[nm][32 * q:32 * q + 32, :]),
                            reads=["kap"], writes=[pf + "LBz"])
                for c, nm in ((0, "o1"), (1, "o2")):
                    P.op("dve", lambda e, c=c, nm=nm, t=t: e.tensor_copy(
                        out=LBz[d][64:128, t, 3, c, :], in_=ktmp[nm][64:128, :]), reads=["kap"], writes=[pf + "LBz"])
                    P.op("dve", lambda e, c=c, t=t: e.memset(LBz[d][64:96, t, 3, c, :], 0.0), reads=["kap"],
                         writes=[pf + "LBz"])

        RR = []
        Rt = {}

        def lam_bar_again(pref, lr_, keys):
            t = Rt
            lre, lim, ls = lr_[0][:], lr_[1][:], lr_[2][:]
            P.op("act", lambda e: e.activation(out=t["dt"][:], in_=ls, func=AF.Exp), reads=keys, writes=[pref + "dt"])
            P.op("dve", lambda e: e.tensor_tensor(out=t["a"][:], in0=lre, in1=t["dt"][:], op=ALU.mult),
                 reads=keys + [pref + "dt"], writes=[pref + "a"])
            P.op("dve", lambda e: e.tensor_tensor(out=t["th"][:], in0=lim, in1=t["dt"][:], op=ALU.mult),
                 reads=keys + [pref + "dt"], writes=[pref + "th"])
            P.op("act", lambda e: e.activation(out=t["mag"][:], in_=t["a"][:], func=AF.Exp), reads=[pref + "a"],
                 writes=[pref + "mag"])
            for nm, off in (("sn", 0.0), ("cs", 0.25)):
                P.op("dve", lambda e, nm=nm, off=off: e.tensor_scalar(out=t[nm][:], in0=t["th"][:], scalar1=1.0 / TWO_PI,
                                                                     scalar2=off, op0=ALU.mult, op1=ALU.add),
                     reads=[pref + "th"], writes=[pref + nm])
                P.op("dve", lambda e, nm=nm: e.tensor_scalar(out=t["k"][:], in0=t[nm][:], scalar1=MAGIC, scalar2=None,
                                                             op0=ALU.add), reads=[pref + nm], writes=[pref + "k"])
                P.op("dve", lambda e, nm=nm: e.tensor_scalar(out=t["k"][:], in0=t["k"][:], scalar1=-MAGIC, scalar2=None,
                                                             op0=ALU.add), reads=[pref + "k"], writes=[pref + "k"])
                P.op("dve", lambda e, nm=nm: e.tensor_tensor(out=t[nm][:], in0=t[nm][:], in1=t["k"][:], op=ALU.subtract),
                     reads=[pref + nm, pref + "k"], writes=[pref + nm])
                P.op("act", lambda e, nm=nm: e.activation(out=t[nm][:], in_=t[nm][:], func=AF.Sin, scale=TWO_PI),
                     reads=[pref + nm], writes=[pref + nm])
            P.op("dve", lambda e: e.tensor_tensor(out=t["are"][:], in0=t["mag"][:], in1=t["cs"][:], op=ALU.mult),
                 reads=[pref + "mag", pref + "cs"], writes=[pref + "are"])
            P.op("dve", lambda e: e.tensor_tensor(out=t["aim"][:], in0=t["mag"][:], in1=t["sn"][:], op=ALU.mult),
                 reads=[pref + "mag", pref + "sn"], writes=[pref + "aim"])

        for k_ in ("dt", "a", "th", "mag", "cs", "sn", "are", "aim", "k"):
            Rt[k_] = sb(f"Rt_{k_}", [128, 128])
        for d in range(2):
            dir_params(d)

        ub = [[sb(f"ub{d}_{i}", [128, NT8, 2, SB], BF16) for i in range(2)] for d in range(2)]
        S_all = sb("S_all", [128, 2, 2, NG, 2, SB], BF16)
        hb_all = sb("hb_all", [128, 2, 2, NG, 2, SB], BF16)
        NH = 2 * 2 * NG * 2
        Hs = [sb(f"H_{i}", [128, 2, 2, NG * 2]) for i in range(2)]
        A1r = sb("A1r", [128, 2, 2, NG, 2])
        A2r = sb("A2r", [128, 2, 2, NG, 2])
        T1 = sb("T1", [128, 2, 2, NG * 2])
        T2 = sb("T2", [128, 2, 2, NG * 2])
        ybuf = [[sb(f"yb{d}_{i}", [128, 2, SB]) for i in range(2)] for d in range(2)]
        psS = [[st.enter_context(nc.psum_tensor(f"psS{d}_{i}", [128, 4, 2, 2, SB], F32)) for i in range(2)] for d in range(2)]
        for d in range(2):
            pf = f"d{d}"
            for c in range(2):
                for b_ in range(2):
                    P.op("dve", lambda e, d=d, c=c, b_=b_: e.tensor_copy(out=A1r[:, c, d, :, b_], in_=A2[d][:, 0, 0, :]),
                         reads=[pf + "A2"], writes=["A1r"])
                    P.op("dve", lambda e, d=d, c=c, b_=b_: e.tensor_copy(out=A2r[:, c, d, :, b_], in_=A2[d][:, 1, c, :]),
                         reads=[pf + "A2"], writes=["A2r"])
        nblk = L // SB
        ncb = LC // SB
        orders = [list(range(nblk)), list(range(ncb - 1, -1, -1)) + list(range(nblk - 1, ncb - 1, -1))]
        outs = []
        SPP = 2 * 2 * NG * 2 * SB
        CST = NG * 2 * SB
        DST = 2 * CST
        stt = dict(ui=0, yi=0, hcur=0)

        def step_ap(tens, j):
            return bass.AP(tens, j, [[SPP, 128], [CST, 2], [DST + SB - 1 - 2 * j, 2], [SB, NG * 2]])

        fl3 = lambda t_: t_[:].rearrange("p c d n -> p (c d n)")
        P.op("dve", lambda e: e.memset(fl3(Hs[0]), 0.0), writes=[("H", 0)])

        def do_iter(i):
            is_x = i >= ncb
            us = stt["ui"] % 2
            stt["ui"] += 1
            blks = [orders[d][i] for d in range(2)]
            for d in range(2):
                pf = f"d{d}"
                t0 = blks[d] * SB
                ubk = (pf, "ub", us)
                for b_ in range(2):
                    P.dma("sp", lambda e, b_=b_, d=d, t0=t0: e.dma_start(out=ub[d][us][:, :, b_, :],
                                                                       in_=u[:, :, b_, t0:t0 + SB].rearrange("t p s -> p t s")),
                          dkey=ubk, writes=[ubk])
                for t in range(NT8):
                    pss = psS[d][t % 2]
                    for q in range(4):
                        for c in range(2):
                            P.op("pe", lambda e, pss=pss, t=t, q=q, c=c, d=d: e.matmul(
                                pss[:, q, c, :, :], LBz[d][:, t, q, c, :], ub[d][us][:, t, :, :], start=True, stop=True),
                                reads=[ubk, pf + "LBz"], writes=[(pf, "psS", t % 2, q, c)])
                    for c in range(2):
                        P.op("act", lambda e, pss=pss, t=t, c=c, d=d: e.activation(
                            out=S_all[:, d, c, 4 * t:4 * t + 4, :, :], in_=pss[:, :, c, :, :], func=AF.Copy),
                            reads=[(pf, "psS", t % 2, q, c) for q in range(4)], writes=[("S", d, t, c)])
            skeys = [("S", d, t, c) for d in range(2) for t in range(NT8) for c in range(2)]
            for j in range(SB):
                hcur = stt["hcur"]
                Hc, Hn = Hs[hcur], Hs[1 - hcur]
                hk, hn = ("H", hcur), ("H", 1 - hcur)
                P.op("dve", lambda e, Hc=Hc: e.tensor_tensor(out=fl3(T1), in0=fl3(Hc), in1=A1r[:].rearrange("p c d g b -> p (c d g b)"), op=ALU.mult),
                     reads=[hk, "A1r"], writes=["T1"])
                P.op("dve", lambda e, Hc=Hc: e.tensor_tensor(out=T2[:, 0, :, :], in0=Hc[:, 1, :, :],
                                                             in1=A2r[:, 0, :, :, :].rearrange("p d g b -> p d (g b)"), op=ALU.mult),
                     reads=[hk, "A2r"], writes=["T2a"])
                P.op("dve", lambda e, Hc=Hc: e.tensor_tensor(out=T2[:, 1, :, :], in0=Hc[:, 0, :, :],
                                                             in1=A2r[:, 1, :, :, :].rearrange("p d g b -> p d (g b)"), op=ALU.mult),
                     reads=[hk, "A2r"], writes=["T2b"])
                P.op("dve", lambda e: e.tensor_tensor(out=fl3(T1), in0=fl3(T1), in1=fl3(T2), op=ALU.add),
                     reads=["T1", "T2a", "T2b"], writes=["T1"])
                P.op("dve", lambda e, Hn=Hn, j=j: e.tensor_tensor(out=Hn[:], in0=T1[:], in1=step_ap(S_all, j), op=ALU.add),
                     reads=["T1"] + (skeys if j in (0, SB - 1) else []), writes=[hn])
                if is_x:
                    P.op("act", lambda e, Hn=Hn, j=j: e.activation(out=step_ap(hb_all, j), in_=Hn[:], func=AF.Copy),
                         reads=[hn], writes=[("hb", j)])
                stt["hcur"] = 1 - hcur
            if not is_x:
                return
            hbk = [("hb", j) for j in range(SB)]
            for d in range(2):
                pf = f"d{d}"
                x0 = blks[d] * SB - LC
                for t in range(NT8):
                    ys = stt["yi"] % 2
                    stt["yi"] += 1
                    yps = psS[d][t % 2][:, 0, 0, :, :]
                    first = True
                    for q in range(4):
                        g = 4 * t + q
                        for c in range(2):
                            P.op("pe", lambda e, yps=yps, g=g, c=c, d=d, first=first, last=(q == 3 and c == 1): e.matmul(
                                yps, Cz[d][:, g, c, :], hb_all[:, d, c, g, :, :], start=first, stop=last),
                                reads=hbk + [pf + "Cz"], writes=[(pf, "psS", t % 2, 0, 0)])
                            first = False
                    P.op("act", lambda e, ys=ys, yps=yps, d=d: e.activation(out=ybuf[d][ys][:, :, :], in_=yps, func=AF.Copy),
                         reads=[(pf, "psS", t % 2, 0, 0)], writes=[(pf, "yb", ys)])
                    dd = P.dma("sp", lambda e, ys=ys, t=t, d=d, x0=x0: e.dma_start(out=ydir[d][t, :, :, x0:x0 + SB], in_=ybuf[d][ys][:, :, :]),
                               dkey=(pf, "yo", ys), reads=[(pf, "yb", ys)], writes=[("yd", d, t)])
                    outs.append(dd)

        for i in range(nblk):
            do_iter(i)

        cy = [[sb(f"cy{k}_{i}", [128, 2, CH]) for i in range(2)] for k in range(2)]
        cu = [sb(f"cu{i}", [128, 2, CH], BF16) for i in range(2)]
        co = [sb(f"co{i}", [128, 2, CH], BF16) for i in range(2)]
        ci = 0
        fin = []
        for t in range(NT8):
            for x0 in range(0, LX, CH):
                s_ = ci % 2
                ci += 1
                for k in range(2):
                    P.dma("sp", lambda e, k=k, s_=s_, t=t, x0=x0: e.dma_start(out=cy[k][s_][:, :, :], in_=ydir[k][t, :, :, x0:x0 + CH]),
                          dkey=("cy", k, s_), reads=[("yd", k, t)], writes=[("cy", k, s_)])
                P.dma("sp", lambda e, s_=s_, t=t, x0=x0: e.dma_start(out=cu[s_][:, :, :], in_=u[t, :, :, LC + x0:LC + x0 + CH]),
                      dkey=("cu", s_), writes=[("cu", s_)])
                P.op("dve", lambda e, s_=s_: e.tensor_tensor(out=cy[0][s_][:], in0=cy[0][s_][:], in1=cy[1][s_][:], op=ALU.add),
                     reads=[("cy", 0, s_), ("cy", 1, s_)], writes=[("cy", 0, s_)])
                P.op("dve", lambda e, s_=s_, t=t: e.scalar_tensor_tensor(out=cy[0][s_][:], in0=cu[s_][:], scalar=dsk[:, t:t + 1],
                                                                      in1=cy[0][s_][:], op0=ALU.mult, op1=ALU.add),
                     reads=[("cy", 0, s_), ("cu", s_), "dsk"], writes=[("cy", 0, s_)])
                P.op("act", lambda e, s_=s_: e.activation(out=co[s_][:], in_=cy[0][s_][:], func=AF.Gelu),
                     reads=[("cy", 0, s_)], writes=[("co", s_)])
                fin.append(P.dma("sp", lambda e, s_=s_, t=t, x0=x0: e.dma_start(out=yg[t, :, :, x0:x0 + CH], in_=co[s_][:]),
                                 dkey=("cog", s_), reads=[("co", s_)], writes=[("yg", t, x0)]))
        P.emit(final_wait_ops=fin[-2:] + outs[-4:])
        print("s5 ops", len(P.ops), "sems", P.n_sems)
    return nc


class PsumRot:
    def __init__(self, nc, st, n=8):
        self.t = [st.enter_context(nc.psum_tensor(f"psr{i}", [128, 512], F32)) for i in range(n)]
        self.i = 0

    def next(self):
        i = self.i % len(self.t)
        self.i += 1
        return self.t[i], ("psr", i)


def outproj_residual(P, nc, ws, pr, v, vkeys, w_out, gate_sb, gate_key, xsrc, dst, c0, TT, xs, os_, cnt):
    last = []
    for oc2 in range(KC):
        buf, key = ws.load(w_out, 0, EC, oc2 * 128, 128)
        ps, pk = pr.next()
        for kc in range(EC):
            P.op("pe", lambda e, ps=ps, buf=buf, kc=kc: e.matmul(ps[:, 0:TT], buf[:, kc, 0:128], v[:, kc, :],
                                                                start=(kc == 0), stop=(kc == EC - 1)),
                 reads=[key] + (vkeys if kc in (0, EC - 1) else []), writes=[pk])
        s = cnt[0] % 2
        cnt[0] += 1
        P.dma("sp", lambda e, s=s, oc2=oc2: e.dma_start(out=xs[s][:, :], in_=xsrc[oc2 * 128:(oc2 + 1) * 128, c0:c0 + TT]),
              dkey=("xs", s), writes=[("xs", s)])
        P.op("dve", lambda e, s=s, ps=ps, oc2=oc2: e.scalar_tensor_tensor(
            out=os_[s][:, :], in0=ps[:, 0:TT], scalar=gate_sb[:, oc2:oc2 + 1], in1=xs[s][:, :], op0=ALU.mult, op1=ALU.add),
            reads=[pk, ("xs", s), gate_key], writes=[("os", s)])
        d = P.dma("sp", lambda e, s=s, oc2=oc2: e.dma_start(out=dst[oc2 * 128:(oc2 + 1) * 128, c0:c0 + TT], in_=os_[s][:, :]),
                  dkey=("oso", s), reads=[("os", s)], writes=[("dst", oc2, c0)])
        last.append(d)
    return last[-2:]


def build_stage_c(T, TT=512):
    nc = bass.Bass("TRN2", target_bir_lowering=False)
    din = lambda n, s, dt=F32: nc.dram_tensor(n, s, dt, kind="ExternalInput").ap()
    ygT = din("ygT", [E, T], BF16)
    zT = din("zT", [E, T], BF16)
    xT = din("xT", [D, T])
    gate = din("gate", [128, KC])
    w_glu = din("w_glu", [E, E])
    b_glu = din("b_glu", [128, EC])
    w_out = din("w_out", [E, D])
    x1T = nc.dram_tensor("x1T", [D, T], F32, kind="ExternalOutput").ap()
    st = contextlib.ExitStack()
    with st:
        sb = lambda name, shape, dt=F32: st.enter_context(nc.sbuf_tensor(name, shape, dt))
        P = Prog(nc)
        ws = WStream(P, nc, st, EC, 128)
        pr = PsumRot(nc, st)
        ygs = sb("ygs", [128, EC, TT], BF16)
        v = sb("v", [128, EC, TT], BF16)
        g_sb = sb("g_sb", [128, KC])
        bg_sb = sb("bg_sb", [128, EC])
        t1 = [sb(f"t1_{i}", [128, TT]) for i in range(2)]
        t2 = [sb(f"t2_{i}", [128, TT]) for i in range(2)]
        zs = [sb(f"zs{i}", [128, TT], BF16) for i in range(2)]
        xs = [sb(f"xs{i}", [128, TT]) for i in range(2)]
        os_ = [sb(f"os{i}", [128, TT]) for i in range(2)]
        P.dma("sp", lambda e: e.dma_start(out=g_sb[:, :], in_=gate[:, :]), dkey=("prm", "g"), writes=["gate"])
        P.dma("sp", lambda e: e.dma_start(out=bg_sb[:, :], in_=b_glu[:, :]), dkey=("prm", "bg"), writes=["bg"])
        ygv = ygT.rearrange("(kc p) t -> p kc t", p=128)
        cnt = [0]
        zi = 0
        fin = []
        for tt in range(T // TT):
            c0 = tt * TT
            for q in range(4):
                P.dma("sp", lambda e, q=q, c0=c0: e.dma_start(out=ygs[:, q * 16:(q + 1) * 16, :], in_=ygv[:, q * 16:(q + 1) * 16, c0:c0 + TT]),
                      dkey="ygs", writes=["ygs"])
            for oc in range(EC):
                buf, key = ws.load(w_glu, 0, EC, oc * 128, 128)
                ps, pk = pr.next()
                for kc in range(EC):
                    P.op("pe", lambda e, ps=ps, buf=buf, kc=kc: e.matmul(ps[:, 0:TT], buf[:, kc, 0:128], ygs[:, kc, :],
                                                                        start=(kc == 0), stop=(kc == EC - 1)),
                         reads=[key, "ygs"], writes=[pk])
                s = zi % 2
                zi += 1
                P.dma("sp", lambda e, s=s, oc=oc, c0=c0: e.dma_start(out=zs[s][:, :], in_=zT[oc * 128:(oc + 1) * 128, c0:c0 + TT]),
                      dkey=("zs", s), writes=[("zs", s)])
                P.op("act", lambda e, s=s, ps=ps, oc=oc: e.activation(out=t1[s][:, :], in_=ps[:, 0:TT], func=AF.Sigmoid,
                                                                     bias=bg_sb[:, oc:oc + 1]),
                     reads=[pk, "bg"], writes=[("t1", s)])
                P.op("act", lambda e, s=s: e.activation(out=t2[s][:, :], in_=zs[s][:, :], func=AF.Silu),
                     reads=[("zs", s)], writes=[("t2", s)])
                P.op("dve", lambda e, s=s, oc=oc: e.tensor_tensor(out=t1[s][:, :], in0=t1[s][:, :], in1=ygs[:, oc, :], op=ALU.mult),
                     reads=[("t1", s), "ygs"], writes=[("t1", s)])
                P.op("dve", lambda e, s=s, oc=oc: e.tensor_tensor(out=v[:, oc, :], in0=t1[s][:, :], in1=t2[s][:, :], op=ALU.mult),
                     reads=[("t1", s), ("t2", s)], writes=[("v", oc)])
            fin = outproj_residual(P, nc, ws, pr, v, [("v", oc) for oc in range(EC)], w_out, g_sb, "gate", xT, x1T, c0, TT,
                                   xs, os_, cnt)
        P.emit(final_wait_ops=fin)
        print("stage c ops", len(P.ops), "sems", P.n_sems)
    return nc


def build_stage_ada():
    nc = bass.Bass("TRN2", target_bir_lowering=False)
    cT3 = nc.dram_tensor("cT3", [128, KC, 3], F32, kind="ExternalInput").ap()
    wa = nc.dram_tensor("wa", [2, D, 1536], F32, kind="ExternalInput").ap()
    ba = nc.dram_tensor("ba", [2, 128, 12], F32, kind="ExternalInput").ap()
    modp = nc.dram_tensor("modp", [2, 128, 12, 3], F32, kind="ExternalOutput").ap()
    st = contextlib.ExitStack()
    with st:
        sb = lambda name, shape, dt=F32: st.enter_context(nc.sbuf_tensor(name, shape, dt))
        P = Prog(nc)
        ws = WStream(P, nc, st, KC, 256)
        c_sb = sb("c_sb", [128, KC, 3])
        sc = sb("sc", [128, KC, 3], BF16)
        ba_sb = sb("ba_sb", [128, 2, 12])
        res = sb("res", [128, 2, 12, 3])
        ps = st.enter_context(nc.psum_tensor("psA", [128, 2, 12, 4], F32))
        P.dma("sp", lambda e: e.dma_start(out=c_sb[:], in_=cT3[:]), dkey=("prm", "c"), writes=["c"])
        for l in range(2):
            P.dma("sp", lambda e, l=l: e.dma_start(out=ba_sb[:, l, :], in_=ba[l]), dkey=("prm", "ba", l), writes=[("ba", l)])
        P.op("act", lambda e: e.activation(out=sc[:].rearrange("p a b -> p (a b)"), in_=c_sb[:].rearrange("p a b -> p (a b)"),
                                           func=AF.Silu), reads=["c"], writes=["sc"])
        for l in range(2):
            for blk in range(6):
                buf, key = ws.load(wa[l], 0, KC, blk * 256, 256)
                for o2 in range(2):
                    oc = blk * 2 + o2
                    for kc in range(KC):
                        P.op("pe", lambda e, buf=buf, kc=kc, o2=o2, oc=oc, l=l: e.matmul(
                            ps[:, l, oc, 0:3], buf[:, kc, o2 * 128:(o2 + 1) * 128], sc[:, kc, :],
                            start=(kc == 0), stop=(kc == KC - 1)), reads=[key, "sc"], writes=["psA"])
        P.op("dve", lambda e: e.tensor_tensor(out=res[:], in0=ps[:, :, :, 0:3],
                                              in1=ba_sb[:].unsqueeze(3).broadcast_to([128, 2, 12, 3]), op=ALU.add),
             reads=["psA", ("ba", 0), ("ba", 1)], writes=["res"])
        od = P.dma("sp", lambda e: e.dma_start(out=modp.rearrange("l p o c -> p l o c"), in_=res[:]), dkey="out",
                   reads=["res"], writes=["out"])
        P.emit(final_wait_ops=[od])
    return nc


def build_stage_d(T=2048, TT=512):
    R, W = 32, 64
    RH = R + 16
    nc = bass.Bass("TRN2", target_bir_lowering=False)
    din = lambda n, s, dt=F32: nc.dram_tensor(n, s, dt, kind="ExternalInput").ap()
    uh = din("uh", [E, RH * W], BF16)
    zT = din("zT", [E, T], BF16)
    xT = din("xT", [D, T])
    gate = din("gate", [128, KC])
    pool_w = din("pool_w", [4, 2048, 2048])
    pscale = din("pscale", [128, EC])
    w_out = din("w_out", [E, D])
    fnw = din("fnw", [128, KC])
    invc = din("invc", [4, 128, T])
    vT = nc.dram_tensor("vT", [E, T], BF16, kind="ExternalOutput").ap()
    x2T = nc.dram_tensor("x2T", [D, T], F32, kind="ExternalOutput").ap()
    outT = nc.dram_tensor("outT", [D, T], F32, kind="ExternalOutput").ap()
    NTT = T // TT
    st = contextlib.ExitStack()
    with st:
        sb = lambda name, shape, dt=F32: st.enter_context(nc.sbuf_tensor(name, shape, dt))
        P = Prog(nc)
        ws = WStream(P, nc, st, EC, 128)
        pr = PsumRot(nc, st)
        big = sb("big", [128, 16 * T], BF16)
        dl = big[:].rearrange("p (i t) -> p i t", i=16)
        vt = big[:].rearrange("p (k t) -> p k t", k=EC)
        assert 16 * T == EC * TT
        g_sb = sb("g_sb", [128, KC])
        ps_sb = sb("ps_sb", [128, EC])
        fn_sb = sb("fn_sb", [128, KC])
        inv_sb = sb("inv_sb", [128, T])
        ut = [sb(f"ut{i}", [128, RH, W], BF16) for i in range(2)]
        WP = W + 16
        Wk = [sb(f"Wk{i}", [128, RH, WP]) for i in range(2)]
        t2 = [sb(f"t2_{i}", [128, TT]) for i in range(2)]
        zs = [sb(f"zs{i}", [128, TT], BF16) for i in range(2)]
        vst = [sb(f"vst{i}", [128, TT], BF16) for i in range(2)]
        xs = [sb(f"xs{i}", [128, TT]) for i in range(2)]
        os_ = [sb(f"os{i}", [128, TT]) for i in range(2)]
        x3 = [sb(f"x3_{i}", [128, T]) for i in range(2)]
        sq = [sb(f"sq{i}", [128, T], BF16) for i in range(2)]
        rstd = sb("rstd", [128, T])
        ones = sb("ones", [128, 128], BF16)
        eps_t = sb("eps_t", [128, 1])
        P.op("dve", lambda e: e.memset(ones[:, :], 1.0), writes=["ones"])
        P.op("dve", lambda e: e.memset(eps_t[:, :], EPS), writes=["eps"])
        P.dma("sp", lambda e: e.dma_start(out=g_sb[:, :], in_=gate[:, :]), dkey=("prm", "g"), writes=["gate"])
        P.dma("sp", lambda e: e.dma_start(out=ps_sb[:, :], in_=pscale[:, :]), dkey=("prm", "ps"), writes=["pscale"])
        P.dma("sp", lambda e: e.dma_start(out=fn_sb[:, :], in_=fnw[:, :]), dkey=("prm", "fn"), writes=["fnw"])
        uhv = uh.rearrange("(kc p) (r w) -> p kc r w", p=128, w=W)
        ui = 0
        zi = 0
        for k in range(4):
            P.dma("sp", lambda e, k=k: e.dma_start(out=inv_sb[:, :], in_=invc[k]), dkey=("prm", "inv"), writes=["inv"])
            for i in range(16):
                kc = 16 * k + i
                s = ui % 2
                ui += 1
                P.dma("sp", lambda e, s=s, kc=kc: e.dma_start(out=ut[s][:, :, :], in_=uhv[:, kc, :, :]), dkey=("ut", s),
                      writes=[("ut", s)])
                for lo_ in (0, WP - 8):
                    P.op("pool", lambda e, lo_=lo_: e.memset(Wk[0][:, :, lo_:lo_ + 8], 0.0), writes=["W0"])
                P.op("act", lambda e, s=s: e.activation(out=Wk[0][:, :, 8:8 + W], in_=ut[s][:, :, :], func=AF.Copy),
                     reads=[("ut", s)], writes=["W0"])
                cur = 0

                def step(fns, cur):
                    src, dst = Wk[cur], Wk[1 - cur]
                    for f in fns:
                        P.op("dve", lambda e, f=f, src=src, dst=dst: f(e, src, dst), reads=[f"W{cur}"], writes=[f"W{1 - cur}"])
                    return 1 - cur
                cur = step([lambda e, a, b: e.tensor_tensor(out=b[:, :, 1:WP], in0=a[:, :, 0:WP - 1], in1=a[:, :, 1:WP], op=ALU.add)], cur)
                for l in range(1, k + 1):
                    sh = 2 ** (l - 1)
                    cur = step([lambda e, a, b, sh=sh: e.tensor_tensor(out=b[:, :, sh:WP - sh], in0=a[:, :, 0:WP - 2 * sh], in1=a[:, :, 2 * sh:WP], op=ALU.add)], cur)
                cur = step([lambda e, a, b: e.tensor_tensor(out=b[:, 1:RH, :], in0=a[:, 0:RH - 1, :], in1=a[:, 1:RH, :], op=ALU.add)], cur)
                for l in range(1, k + 1):
                    sh = 2 ** (l - 1)
                    cur = step([lambda e, a, b, sh=sh: e.tensor_tensor(out=b[:, sh:RH - sh, :], in0=a[:, 0:RH - 2 * sh, :], in1=a[:, 2 * sh:RH, :], op=ALU.add)], cur)
                src, dst = Wk[cur], Wk[1 - cur]
                P.op("dve", lambda e, src=src, dst=dst: e.tensor_tensor(out=dst[:, 8:8 + R, 8:8 + W], in0=src[:, 8:8 + R, 8:8 + W],
                                                                        in1=inv_sb[:].rearrange("p (r w) -> p r w", w=W), op=ALU.mult),
                     reads=[f"W{cur}", "inv"], writes=[f"W{1 - cur}"])
                P.op("dve", lambda e, dst=dst, s=s, i=i: e.tensor_tensor(out=dl[:, i, :].rearrange("p (r w) -> p r w", w=W),
                                                                       in0=dst[:, 8:8 + R, 8:8 + W], in1=ut[s][:, 8:8 + R, :], op=ALU.subtract),
                     reads=[f"W{1 - cur}", ("ut", s)], writes=[("dl", i), "W0", "W1"])
            dkeys = [("dl", i) for i in range(16)]
            for oc in range(16):
                ocg = 16 * k + oc
                buf, key = ws.load(pool_w[k], 0, 16, oc * 128, 128)
                for tt in range(NTT):
                    ps, pk = pr.next()
                    for kc in range(16):
                        P.op("pe", lambda e, ps=ps, buf=buf, kc=kc, tt=tt: e.matmul(ps[:, 0:TT], buf[:, kc, 0:128],
                                                                                  dl[:, kc, tt * TT:(tt + 1) * TT],
                                                                                  start=(kc == 0), stop=(kc == 15)),
                             reads=[key] + dkeys, writes=[pk])
                    s = zi % 2
                    zi += 1
                    P.dma("sp", lambda e, s=s, ocg=ocg, tt=tt: e.dma_start(out=zs[s][:, :], in_=zT[ocg * 128:(ocg + 1) * 128, tt * TT:(tt + 1) * TT]),
                          dkey=("zs", s), writes=[("zs", s)])
                    P.op("act", lambda e, s=s: e.activation(out=t2[s][:, :], in_=zs[s][:, :], func=AF.Silu),
                         reads=[("zs", s)], writes=[("t2", s)])
                    P.op("dve", lambda e, s=s, ps=ps, ocg=ocg: e.scalar_tensor_tensor(
                        out=vst[s][:, :], in0=ps[:, 0:TT], scalar=ps_sb[:, ocg:ocg + 1], in1=t2[s][:, :], op0=ALU.mult, op1=ALU.mult),
                        reads=[pk, ("t2", s), "pscale"], writes=[("vst", s)])
                    P.dma("sp", lambda e, s=s, ocg=ocg, tt=tt: e.dma_start(out=vT[ocg * 128:(ocg + 1) * 128, tt * TT:(tt + 1) * TT], in_=vst[s][:, :]),
                          dkey=("vso", s), reads=[("vst", s)], writes=[("vT", ocg, tt)])
        vTv = vT.rearrange("(kc p) t -> p kc t", p=128)
        cnt = [0]
        for tt in range(NTT):
            c0 = tt * TT
            for q in range(4):
                P.dma("sp", lambda e, q=q, c0=c0: e.dma_start(out=vt[:, q * 16:(q + 1) * 16, :], in_=vTv[:, q * 16:(q + 1) * 16, c0:c0 + TT]),
                      dkey="vtl", reads=[("vT", ocg, tt) for ocg in range(EC)], writes=["vtile"] + [("dl", i) for i in range(16)])
            outproj_residual(P, nc, ws, pr, vt, ["vtile"], w_out, g_sb, "gate", xT, x2T, c0, TT, xs, os_, cnt)
        x2v = x2T.rearrange("(kc p) t -> p kc t", p=128)
        outv = outT.rearrange("(kc p) t -> p kc t", p=128)
        dst_keys = lambda kc: [("dst", kc, tt * TT) for tt in range(NTT)]
        for kc in range(KC):
            s = kc % 2
            P.dma("sp", lambda e, s=s, kc=kc: e.dma_start(out=x3[s][:, :], in_=x2v[:, kc, :]), dkey=("x3", s),
                  reads=dst_keys(kc), writes=[("x3", s)])
            P.op("act", lambda e, s=s: e.activation(out=sq[s][:, :], in_=x3[s][:, :], func=AF.Square),
                 reads=[("x3", s)], writes=[("sq", s)])
            for tt in range(NTT):
                P.op("pe", lambda e, s=s, tt=tt, kc=kc: e.matmul(pr.t[tt][:, 0:TT], ones[:, :], sq[s][:, tt * TT:(tt + 1) * TT],
                                                              start=(kc == 0), stop=(kc == KC - 1)),
                     reads=[("sq", s), "ones"], writes=[("psr", tt)])
        for tt in range(NTT):
            P.op("act", lambda e, tt=tt: e.activation(out=rstd[:, tt * TT:(tt + 1) * TT], in_=pr.t[tt][:, 0:TT],
                                                     func=AF.Sqrt, scale=1.0 / D, bias=eps_t[:, 0:1]),
                 reads=[("psr", tt), "eps"], writes=[("rs", tt)])
            P.op("dve", lambda e, tt=tt: e.reciprocal(out=rstd[:, tt * TT:(tt + 1) * TT], in_=rstd[:, tt * TT:(tt + 1) * TT]),
                 reads=[("rs", tt)], writes=[("rs", tt)])
        fin = []
        for kc in range(KC):
            s = kc % 2
            P.dma("sp", lambda e, s=s, kc=kc: e.dma_start(out=x3[s][:, :], in_=x2v[:, kc, :]), dkey=("x3", s),
                  reads=dst_keys(kc), writes=[("x3", s)])
            P.op("dve", lambda e, s=s: e.tensor_tensor(out=x3[s][:, :], in0=x3[s][:, :], in1=rstd[:, :], op=ALU.mult),
                 reads=[("x3", s)] + [("rs", tt) for tt in range(NTT)], writes=[("x3", s)])
            P.op("act", lambda e, s=s, kc=kc: e.activation(out=x3[s][:, :], in_=x3[s][:, :], func=AF.Copy, scale=fn_sb[:, kc:kc + 1]),
                 reads=[("x3", s), "fnw"], writes=[("x3", s)])
            fin.append(P.dma("sp", lambda e, s=s, kc=kc: e.dma_start(out=outv[:, kc, :], in_=x3[s][:, :]), dkey=("x3o", s),
                             reads=[("x3", s)], writes=[("out", kc)]))
        P.emit(final_wait_ops=fin[-2:])
        print("stage d ops", len(P.ops), "sems", P.n_sems)
    return nc


def inv_counts(q):
    out = np.zeros((4, 32 * 64), np.float32)
    for k, w in enumerate((2, 4, 8, 16)):
        r = np.arange(32 * q, 32 * q + 32)
        c = np.arange(64)
        rc = np.clip(r + w - w // 2, 0, 128) - np.clip(r - w // 2, 0, 128)
        cc = np.clip(c + w - w // 2, 0, 64) - np.clip(c - w // 2, 0, 64)
        out[k] = (1.0 / (rc[:, None] * cc[None, :]).astype(np.float32)).reshape(-1)
    return out


def s5_host_params(lam_re, lam_im, ls, b_re, b_im, c_re, c_im, g0, NT8):
    NG = 4 * NT8
    BpT = np.zeros((2, 2, 128, NT8, 128), np.float32)
    lamR = np.zeros((3, 2, 128, NT8, 128), np.float32)
    Cm = np.zeros((2, 2, 128, NG, 32), np.float32)
    lamM = np.zeros((3, 2, 128, NG), np.float32)
    for d in range(2):
        for g in range(NG):
            t, q = g // 4, g % 4
            for h in range(2):
                G = 2 * (g0 + g) + h
                ms = slice(64 * h, 64 * h + 64)
                rows = slice(32 * q + 16 * h, 32 * q + 16 * h + 16)
                BpT[0, d, rows, t, ms] = b_re[d, G].T
                BpT[1, d, rows, t, ms] = b_im[d, G].T
                lamR[0, d, 32 * q:32 * q + 32, t, ms] = lam_re[d, G][None, :]
                lamR[1, d, 32 * q:32 * q + 32, t, ms] = lam_im[d, G][None, :]
                lamR[2, d, 32 * q:32 * q + 32, t, ms] = ls[d, G]
                Cm[0, d, ms, g, 16 * h:16 * h + 16] = c_re[d, G].T
                Cm[1, d, ms, g, 16 * h:16 * h + 16] = c_im[d, G].T
                lamM[0, d, ms, g] = lam_re[d, G]
                lamM[1, d, ms, g] = lam_im[d, G]
                lamM[2, d, ms, g] = ls[d, G]
    return dict(BpT_re=BpT[0], BpT_im=BpT[1], lamR_re=lamR[0], lamR_im=lamR[1], lsR=lamR[2],
                Cm_re=Cm[0], Cm_im=Cm[1], lamM_re=lamM[0], lamM_im=lamM[1], lsM=lamM[2])


from concourse.bass_utils import run_bass_kernel_spmd

_BF = ml_dtypes.bfloat16


def _run(nc, ins):
    res = run_bass_kernel_spmd(nc, ins, core_ids=list(range(8)))
    return res.results


def kernel(x, c, ctx, c_ctx, norm_w, w_ada, b_ada, w_in, w_out, s5_lam_re, s5_lam_im, s5_log_step, s5_b_re, s5_b_im,
           s5_c_re, s5_c_im, s5_d, s5_w_glu, s5_b_glu, pool_w, pool_scale, final_norm_w):
    f32 = lambda a: np.asarray(a, np.float32)
    x, c, ctx, c_ctx, norm_w, w_ada, b_ada, w_in, w_out = map(f32, (x, c, ctx, c_ctx, norm_w, w_ada, b_ada, w_in, w_out))
    T = 2048
    nc_ada = build_stage_ada()
    cv = np.stack([colv(c[0]), colv(c[1]), colv(c_ctx)], axis=-1)
    ins = []
    for j in range(8):
        wa = np.ascontiguousarray(w_ada.reshape(2, D, 8, 1536)[:, :, j, :])
        ba = np.stack([colv(b_ada[l].reshape(8, 1536)[j]) for l in range(2)])
        ins.append(dict(cT3=cv, wa=wa, ba=ba))
    r = _run(nc_ada, ins)
    mp = np.stack([np.asarray(q["modp"]) for q in r])
    mod = np.ascontiguousarray(mp.transpose(1, 4, 2, 0, 3).reshape(2, 3, 128, 96))

    nc_a = build_stage_a(T, 2 * E)
    ins = []
    for core in range(8):
        b, q = core // 4, core % 4
        ins.append(dict(xT=np.ascontiguousarray(x[b, q * T:(q + 1) * T, :].T), modT=mod[0, b], nw=colv(norm_w[0]),
                        w_in=w_in[0]))
    r = _run(nc_a, ins)
    uz0 = [np.asarray(q["uz"]) for q in r]
    nc_ac = build_stage_a(64, E)
    w_in0_u = np.ascontiguousarray(w_in[0][:, :E])
    ins = []
    for core in range(8):
        b, q = core // 4, core % 4
        ins.append(dict(xT=np.ascontiguousarray(ctx[b, q * 64:(q + 1) * 64, :].T), modT=mod[0, 2], nw=colv(norm_w[0]),
                        w_in=w_in0_u))
    r = _run(nc_ac, ins)
    uc0 = [np.asarray(q["uz"]) for q in r]
    del w_in0_u

    LC, LX = 256, 8192
    U = [np.concatenate([uc0[4 * b + q] for q in range(4)] + [uz0[4 * b + q][:E] for q in range(4)], axis=1) for b in range(2)]
    nc_s5 = build_stage_s5(LC, LX, 8)
    ins = []
    lam_re, lam_im, ls = f32(s5_lam_re)[0], f32(s5_lam_im)[0], f32(s5_log_step)[0]
    b_re, b_im, c_re, c_im = f32(s5_b_re)[0], f32(s5_b_im)[0], f32(s5_c_re)[0], f32(s5_c_im)[0]
    for j in range(8):
        uj = np.stack([U[b][j * 1024:(j + 1) * 1024].reshape(8, 128, LC + LX) for b in range(2)], axis=2)
        prm = s5_host_params(lam_re, lam_im, ls, b_re, b_im, c_re, c_im, j * 32, 8)
        ins.append(dict(u=np.ascontiguousarray(uj), dskip=colv(f32(s5_d)[0][j * 1024:(j + 1) * 1024]), **prm))
    r = _run(nc_s5, ins)
    del U
    YG = [np.concatenate([np.asarray(r[j]["yg"])[:, :, b, :].reshape(1024, LX) for j in range(8)], axis=0) for b in range(2)]

    nc_c = build_stage_c(T)
    ins = []
    for core in range(8):
        b, q = core // 4, core % 4
        ins.append(dict(ygT=np.ascontiguousarray(YG[b][:, q * T:(q + 1) * T]), zT=np.ascontiguousarray(uz0[core][E:]),
                        xT=np.ascontiguousarray(x[b, q * T:(q + 1) * T, :].T), gate=np.ascontiguousarray(mod[0, b][:, 64:96]),
                        w_glu=f32(s5_w_glu)[0], b_glu=colv(f32(s5_b_glu)[0]), w_out=w_out[0]))
    r = _run(nc_c, ins)
    x1T = [np.asarray(q["x1T"]) for q in r]
    del YG, uz0, uc0

    ins = []
    for core in range(8):
        b = core // 4
        ins.append(dict(xT=x1T[core], modT=mod[1, b], nw=colv(norm_w[1]), w_in=w_in[1]))
    r = _run(nc_a, ins)
    uz1 = [np.asarray(q["uz"]) for q in r]

    nc_d = build_stage_d()
    ins = []
    for core in range(8):
        b, q = core // 4, core % 4
        uh = np.zeros((E, 48, 64), _BF)
        uh[:, 8:40, :] = uz1[core][:E].reshape(E, 32, 64)
        if q > 0:
            uh[:, 0:8, :] = uz1[core - 1][:E].reshape(E, 32, 64)[:, 24:32, :]
        if q < 3:
            uh[:, 40:48, :] = uz1[core + 1][:E].reshape(E, 32, 64)[:, 0:8, :]
        inv = np.ascontiguousarray(np.broadcast_to(inv_counts(q)[:, None, :], (4, 128, T)))
        ins.append(dict(uh=uh.reshape(E, 48 * 64), zT=np.ascontiguousarray(uz1[core][E:]), xT=x1T[core],
                        gate=np.ascontiguousarray(mod[1, b][:, 64:96]), pool_w=f32(pool_w)[0], pscale=colv(f32(pool_scale)[0]),
                        w_out=w_out[1], fnw=colv(f32(final_norm_w)), invc=inv))
    r = _run(nc_d, ins)
    out = np.empty((2, 8192, D), np.float32)
    for core in range(8):
        b, q = core // 4, core % 4
        out[b, q * T:(q + 1) * T, :] = np.asarray(r[core]["outT"]).T
    return out
```

```python
import numpy as np
import concourse.bass as bass
import concourse.mybir as mybir

F32 = mybir.dt.float32
BF16 = mybir.dt.bfloat16
ALU = mybir.AluOpType
AF = mybir.ActivationFunctionType

ENGS = ("pe", "act", "dve", "pool", "sp")
SEM_EPOCH = 30000


class Op:
    __slots__ = ("eng", "fn", "deps", "is_dma", "dkey", "sig", "idx", "dma_wait")

    def __init__(self, eng, fn, is_dma=False, dkey=None):
        self.eng = eng
        self.fn = fn
        self.deps = []
        self.is_dma = is_dma
        self.dkey = dkey
        self.sig = False
        self.idx = None
        self.dma_wait = None


class Prog:
    def __init__(self, nc, same_engine_sync=True):
        self.nc = nc
        self.ops = []
        self.last_w = {}
        self.readers = {}
        self.same_engine_sync = same_engine_sync
        self.dma_count = {}
        self.ctx = []

    def _add(self, op, reads, writes):
        deps = []
        for k in reads:
            w = self.last_w.get(k)
            if w is not None:
                deps.append(w)
        for k in writes:
            w = self.last_w.get(k)
            if w is not None:
                deps.append(w)
            for r in self.readers.get(k, ()):
                deps.append(r)
        seen = set()
        for d in deps:
            if id(d) in seen or d is op:
                continue
            seen.add(id(d))
            if (not d.is_dma) and d.eng == op.eng and not op.is_dma:
                ses = self.same_engine_sync
                if d.eng == "pe" or ses is False or (ses is not True and d.eng not in ses):
                    continue
            op.deps.append(d)
            d.sig = True
        for k in reads:
            self.readers.setdefault(k, []).append(op)
        for k in writes:
            self.last_w[k] = op
            self.readers[k] = []
        self.ops.append(op)
        return op

    def op(self, eng, fn, reads=(), writes=()):
        return self._add(Op(eng, fn), reads, writes)

    def dma(self, eng, fn, dkey, reads=(), writes=()):
        o = Op(eng, fn, is_dma=True, dkey=dkey)
        ep, cnt = self.dma_count.get(dkey, (0, 0))
        if cnt + 16 > SEM_EPOCH:
            ep, cnt = ep + 1, 0
        cnt += 16
        self.dma_count[dkey] = (ep, cnt)
        o.dma_wait = (dkey, ep, cnt)
        return self._add(o, reads, writes)

    def emit(self, final_wait_ops=()):
        nc = self.nc
        counters = {e: [0, 0] for e in ENGS}
        sig_of = {}
        for o in self.ops:
            if o.is_dma:
                continue
            if o.sig:
                c = counters[o.eng]
                if c[1] + 1 > SEM_EPOCH:
                    c[0] += 1
                    c[1] = 0
                c[1] += 1
                sig_of[id(o)] = (("eng", o.eng), c[0], c[1])
        semkeys = set()
        for o in self.ops:
            if o.is_dma:
                semkeys.add((("dma", o.dkey), o.dma_wait[1]))
            elif o.sig:
                k = sig_of[id(o)]
                semkeys.add((k[0], k[1]))
        semkeys = sorted(semkeys, key=repr)
        self.n_sems = len(semkeys)
        sems = {}
        import contextlib
        stack = contextlib.ExitStack()
        for i, k in enumerate(semkeys):
            sems[k] = stack.enter_context(nc.semaphore(f"s{i}"))
        per_eng = {e: [] for e in ENGS}
        for o in self.ops:
            per_eng[o.eng].append(o)

        def wait_target(d):
            if d.is_dma:
                return (("dma", d.dkey), d.dma_wait[1]), d.dma_wait[2]
            k = sig_of[id(d)]
            return (k[0], k[1]), k[2]

        def run(engname, eng):
            waited = {}
            for o in per_eng[engname]:
                for d in o.deps:
                    sk, val = wait_target(d)
                    if waited.get(sk, 0) >= val:
                        continue
                    waited[sk] = val
                    eng.wait_ge(sems[sk], val)
                ins = o.fn(eng)
                if o.is_dma:
                    ins.then_inc(sems[(("dma", o.dkey), o.dma_wait[1])], 16)
                elif o.sig:
                    k = sig_of[id(o)]
                    ins.then_inc(sems[(k[0], k[1])], 1)
            if engname == "sp":
                for d in final_wait_ops:
                    sk, val = wait_target(d)
                    eng.wait_ge(sems[sk], val)

        with stack:
            with nc.Block() as block:
                @block.tensor
                def _(e):
                    run("pe", e)

                @block.scalar
                def _(e):
                    run("act", e)

                @block.vector
                def _(e):
                    run("dve", e)

                @block.gpsimd
                def _(e):
                    run("pool", e)

                @block.sync
                def _(e):
                    run("sp", e)


import contextlib
import ml_dtypes

D = 4096
E = 8192
KC = D // 128
EC = E // 128
EPS = 1e-6


def colv(v):
    v = np.ascontiguousarray(v, dtype=np.float32)
    return np.ascontiguousarray(v.reshape(-1, 128).T)


class WStream:
    def __init__(self, P, nc, st, kmax, wb, name="wt", nbuf=2):
        self.P, self.nc = P, nc
        self.wb = wb
        self.kmax = kmax
        self.bufs = [st.enter_context(nc.sbuf_tensor(f"{name}{i}", [128, kmax, wb], BF16)) for i in range(nbuf)]
        self.i = 0
        self.name = name

    def load(self, w, kc0, nkc, n0, ncols):
        s = self.i % len(self.bufs)
        self.i += 1
        buf = self.bufs[s]
        key = (self.name, s)
        wv = w.rearrange("(kc p) n -> p kc n", p=128)
        step = 8
        for q in range(0, nkc, step):
            qn = min(step, nkc - q)
            self.P.dma("pool", lambda e, q=q, qn=qn: e.dma_start(
                out=buf[:, q:q + qn, 0:ncols], in_=wv[:, kc0 + q:kc0 + q + qn, n0:n0 + ncols]),
                dkey=key, writes=[key])
        return buf, key


def build_stage_a(T, NOUT, with_gate=True):
    nc = bass.Bass("TRN2", target_bir_lowering=False)
    xT = nc.dram_tensor("xT", [D, T], F32, kind="ExternalInput").ap()
    modT = nc.dram_tensor("modT", [128, 96], F32, kind="ExternalInput").ap()
    nw = nc.dram_tensor("nw", [128, KC], F32, kind="ExternalInput").ap()
    w_in = nc.dram_tensor("w_in", [D, NOUT], F32, kind="ExternalInput").ap()
    uz = nc.dram_tensor("uz", [NOUT, T], BF16, kind="ExternalOutput").ap()
    TT = min(512, T)
    NT = T // TT
    WB = 256
    st = contextlib.ExitStack()
    with st:
        sb = lambda name, shape, dt: st.enter_context(nc.sbuf_tensor(name, shape, dt))
        hT = sb("hT", [128, KC, T], BF16)
        c_sb = sb("c_sb", [128, KC], F32)
        sc = sb("sc", [128, KC], BF16)
        nw_sb = sb("nw_sb", [128, KC], F32)
        ba_sb = sb("ba_sb", [128, 96], F32)
        mod = sb("mod", [128, 96], F32)
        a1 = sb("a1", [128, KC], F32)
        ones = sb("ones", [128, 128], BF16)
        xs = [sb(f"xs{i}", [128, T], F32) for i in range(2)]
        sq = [sb(f"sq{i}", [128, T], BF16) for i in range(2)]
        rstd = sb("rstd", [128, T], F32)
        ot = [sb(f"ot{i}", [128, T], BF16) for i in range(2)]
        ps = [st.enter_context(nc.psum_tensor(f"ps{i}", [128, 512], F32)) for i in range(8)]
        P = Prog(nc)
        ws = WStream(P, nc, st, KC, WB)
        P.dma("sp", lambda e: e.dma_start(out=nw_sb[:, :], in_=nw[:, :]), dkey=("prm", "nw"), writes=["nw"])
        P.dma("sp", lambda e: e.dma_start(out=mod[:, :], in_=modT[:, :]), dkey=("prm", "mod"), writes=["mod"])
        P.op("dve", lambda e: e.memset(ones[:, :], 1.0), writes=["ones"])
        P.op("dve", lambda e: e.scalar_tensor_tensor(out=a1[:, :], in0=mod[:, KC:2 * KC], scalar=1.0, in1=nw_sb[:, :],
                                                     op0=ALU.add, op1=ALU.mult), reads=["mod", "nw"], writes=["a1"])
        xv = xT.rearrange("(kc p) t -> p kc t", p=128)
        for kc in range(KC):
            s = kc % 2
            P.dma("sp", lambda e, s=s, kc=kc: e.dma_start(out=xs[s][:, :], in_=xv[:, kc, :]), dkey=("xs", s),
                  writes=[("xs", s)])
            P.op("act", lambda e, s=s: e.activation(out=sq[s][:, :], in_=xs[s][:, :], func=AF.Square),
                 reads=[("xs", s)], writes=[("sq", s)])
            for tt in range(NT):
                P.op("pe", lambda e, s=s, tt=tt, kc=kc: e.matmul(
                    ps[tt][:, 0:TT], ones[:, :], sq[s][:, tt * TT:(tt + 1) * TT], start=(kc == 0), stop=(kc == KC - 1)),
                    reads=[("sq", s), "ones"], writes=[("ps", tt)])
        eps_t = sb("eps_t", [128, 1], F32)
        P.op("dve", lambda e: e.memset(eps_t[:, :], EPS), writes=["eps"])
        for tt in range(NT):
            P.op("act", lambda e, tt=tt: e.activation(out=rstd[:, tt * TT:(tt + 1) * TT], in_=ps[tt][:, 0:TT],
                                                     func=AF.Sqrt, scale=1.0 / D, bias=eps_t[:, 0:1]),
                 reads=[("ps", tt), "eps"], writes=[("rs", tt)])
            P.op("dve", lambda e, tt=tt: e.reciprocal(out=rstd[:, tt * TT:(tt + 1) * TT],
                                                      in_=rstd[:, tt * TT:(tt + 1) * TT]),
                 reads=[("rs", tt)], writes=[("rs", tt)])
        for kc in range(KC):
            s = kc % 2
            P.dma("sp", lambda e, s=s, kc=kc: e.dma_start(out=xs[s][:, :], in_=xv[:, kc, :]), dkey=("xs", s),
                  writes=[("xs", s)])
            P.op("dve", lambda e, s=s: e.tensor_tensor(out=xs[s][:, :], in0=xs[s][:, :], in1=rstd[:, :], op=ALU.mult),
                 reads=[("xs", s)] + [("rs", tt) for tt in range(NT)], writes=[("xs", s)])
            P.op("act", lambda e, s=s, kc=kc: e.activation(out=hT[:, kc, :], in_=xs[s][:, :], func=AF.Identity,
                                                          scale=a1[:, kc:kc + 1], bias=mod[:, kc:kc + 1]),
                 reads=[("xs", s), "a1", "mod"], writes=[("h", kc)])
        hkeys = [("h", kc) for kc in range(KC)]
        pi = 0
        oi = 0
        outs = []
        for blk in range(NOUT // WB):
            buf, key = ws.load(w_in, 0, KC, blk * WB, WB)
            for o2 in range(WB // 128):
                osl = oi % 2
                oi += 1
                for tt in range(NT):
                    pb = pi % 8
                    pi += 1
                    for kc in range(KC):
                        P.op("pe", lambda e, pb=pb, buf=buf, kc=kc, o2=o2, tt=tt: e.matmul(
                            ps[pb][:, 0:TT], buf[:, kc, o2 * 128:(o2 + 1) * 128], hT[:, kc, tt * TT:(tt + 1) * TT],
                            start=(kc == 0), stop=(kc == KC - 1)),
                            reads=[key] + (hkeys if kc == 0 else []), writes=[("ps", pb) if pb < 4 else ("psx", pb)] )
                    pk = ("ps", pb) if pb < 4 else ("psx", pb)
                    if pi % 2 == 0:
                        P.op("act", lambda e, pb=pb, osl=osl, tt=tt: e.activation(
                            out=ot[osl][:, tt * TT:(tt + 1) * TT], in_=ps[pb][:, 0:TT], func=AF.Copy),
                            reads=[pk], writes=[("ot", osl, tt)])
                    else:
                        P.op("dve", lambda e, pb=pb, osl=osl, tt=tt: e.tensor_copy(
                            out=ot[osl][:, tt * TT:(tt + 1) * TT], in_=ps[pb][:, 0:TT]),
                            reads=[pk], writes=[("ot", osl, tt)])
                r0 = blk * WB + o2 * 128
                d = P.dma("sp", lambda e, osl=osl, r0=r0: e.dma_start(out=uz[r0:r0 + 128, :], in_=ot[osl][:, :]),
                          dkey=("ost", osl), reads=[("ot", osl, tt) for tt in range(NT)], writes=[("uz", r0)])
                outs.append(d)
        P.emit(final_wait_ops=outs[-2:])
    return nc


TWO_PI = 2.0 * np.pi


def build_stage_s5(LC, LX, NT8=8, SB=64, CH=256, SES=True):
    L = LC + LX
    NG = 4 * NT8
    nc = bass.Bass("TRN2", target_bir_lowering=False)
    din = lambda n, s, dt=F32: nc.dram_tensor(n, s, dt, kind="ExternalInput").ap()
    u = din("u", [NT8, 128, 2, L], BF16)
    BpT = [din("BpT_re", [2, 128, NT8, 128]), din("BpT_im", [2, 128, NT8, 128])]
    lamR = [din("lamR_re", [2, 128, NT8, 128]), din("lamR_im", [2, 128, NT8, 128]), din("lsR", [2, 128, NT8, 128])]
    Cm = [din("Cm_re", [2, 128, NG, 32]), din("Cm_im", [2, 128, NG, 32])]
    lamM = [din("lamM_re", [2, 128, NG]), din("lamM_im", [2, 128, NG]), din("lsM", [2, 128, NG])]
    dskip = din("dskip", [128, NT8])
    ydir = [nc.dram_tensor(f"yd{d}", [NT8, 128, 2, LX], F32, kind="ExternalOutput").ap() for d in range(2)]
    yg = nc.dram_tensor("yg", [NT8, 128, 2, LX], BF16, kind="ExternalOutput").ap()
    st = contextlib.ExitStack()
    with st:
        sb = lambda name, shape, dt=F32: st.enter_context(nc.sbuf_tensor(name, shape, dt))
        P = Prog(nc, same_engine_sync=SES)
        MAGIC = 12582912.0

        def lam_bar(pref, lre, lim, ls, n, keys):
            t = {k: sb(f"{pref}_{k}", [128, n]) for k in ("dt", "a", "th", "mag", "cs", "sn", "are", "aim", "k")}
            P.op("act", lambda e: e.activation(out=t["dt"][:], in_=ls, func=AF.Exp), reads=keys, writes=[pref + "dt"])
            P.op("dve", lambda e: e.tensor_tensor(out=t["a"][:], in0=lre, in1=t["dt"][:], op=ALU.mult),
                 reads=keys + [pref + "dt"], writes=[pref + "a"])
            P.op("dve", lambda e: e.tensor_tensor(out=t["th"][:], in0=lim, in1=t["dt"][:], op=ALU.mult),
                 reads=keys + [pref + "dt"], writes=[pref + "th"])
            P.op("act", lambda e: e.activation(out=t["mag"][:], in_=t["a"][:], func=AF.Exp), reads=[pref + "a"],
                 writes=[pref + "mag"])
            for nm, off in (("sn", 0.0), ("cs", 0.25)):
                P.op("dve", lambda e, nm=nm, off=off: e.tensor_scalar(out=t[nm][:], in0=t["th"][:], scalar1=1.0 / TWO_PI,
                                                                     scalar2=off, op0=ALU.mult, op1=ALU.add),
                     reads=[pref + "th"], writes=[pref + nm])
                P.op("dve", lambda e, nm=nm: e.tensor_scalar(out=t["k"][:], in0=t[nm][:], scalar1=MAGIC, scalar2=None,
                                                             op0=ALU.add), reads=[pref + nm], writes=[pref + "k"])
                P.op("dve", lambda e, nm=nm: e.tensor_scalar(out=t["k"][:], in0=t["k"][:], scalar1=-MAGIC, scalar2=None,
                                                             op0=ALU.add), reads=[pref + "k"], writes=[pref + "k"])
                P.op("dve", lambda e, nm=nm: e.tensor_tensor(out=t[nm][:], in0=t[nm][:], in1=t["k"][:], op=ALU.subtract),
                     reads=[pref + nm, pref + "k"], writes=[pref + nm])
                P.op("act", lambda e, nm=nm: e.activation(out=t[nm][:], in_=t[nm][:], func=AF.Sin, scale=TWO_PI),
                     reads=[pref + nm], writes=[pref + nm])
            P.op("dve", lambda e: e.tensor_tensor(out=t["are"][:], in0=t["mag"][:], in1=t["cs"][:], op=ALU.mult),
                 reads=[pref + "mag", pref + "cs"], writes=[pref + "are"])
            P.op("dve", lambda e: e.tensor_tensor(out=t["aim"][:], in0=t["mag"][:], in1=t["sn"][:], op=ALU.mult),
                 reads=[pref + "mag", pref + "sn"], writes=[pref + "aim"])
            return t["are"], t["aim"]

        LBz = [sb(f"LBz{d}", [128, NT8, 4, 2, 128], BF16) for d in range(2)]
        Cz = [sb(f"Cz{d}", [128, NG, 2, 128], BF16) for d in range(2)]
        A2 = [sb(f"A2_{d}", [128, 2, 2, NG]) for d in range(2)]
        dsk = sb("dsk", [128, NT8])
        P.dma("sp", lambda e: e.dma_start(out=dsk[:, :], in_=dskip[:, :]), dkey=("prm", "dsk"), writes=["dsk"])
        cm = [sb(f"cm{i}", [128, NG, 32]) for i in range(2)]
        lr = [sb(f"lr{i}", [128, 128]) for i in range(3)]
        bp = [sb(f"bp{i}", [128, 128]) for i in range(2)]
        ktmp = {k: sb(f"k_{k}", [128, 128]) for k in ("nre", "den", "t1", "t2", "kre", "kim", "o1", "o2")}

        def dir_params(d):
            pf = f"d{d}"
            lm = [sb(f"{pf}lm{i}", [128, NG]) for i in range(3)]
            for i in range(3):
                P.dma("sp", lambda e, i=i: e.dma_start(out=lm[i][:, :], in_=lamM[i][d]), dkey=("prm", "lm", d, i),
                      writes=[pf + f"lm{i}"])
            are, aim = lam_bar(pf + "M", lm[0][:], lm[1][:], lm[2][:], NG, [pf + f"lm{i}" for i in range(3)])
            for c in range(2):
                P.op("dve", lambda e, c=c: e.tensor_copy(out=A2[d][:, 0, c, :], in_=are[:, :]), reads=[pf + "Mare"],
                     writes=[pf + "A2"])
            P.op("dve", lambda e: e.tensor_scalar(out=A2[d][:, 1, 0, :], in0=aim[:, :], scalar1=-1.0, scalar2=None,
                                                  op0=ALU.mult), reads=[pf + "Maim"], writes=[pf + "A2"])
            P.op("dve", lambda e: e.tensor_copy(out=A2[d][:, 1, 1, :], in_=aim[:, :]), reads=[pf + "Maim"],
                 writes=[pf + "A2"])
            for i in range(2):
                P.dma("sp", lambda e, i=i: e.dma_start(out=cm[i][:, :, :], in_=Cm[i][d]), dkey=("prm", "cm", i),
                      writes=[f"cm{i}"])
            P.op("pool", lambda e: e.memset(Cz[d][:].rearrange("p a b c -> p (a b c)"), 0.0), writes=[pf + "Cz"])
            for q in range(4):
                P.op("dve", lambda e, q=q: e.tensor_copy(out=Cz[d][:, q::4, 0, 32 * q:32 * q + 32], in_=cm[0][:, q::4, :]),
                     reads=["cm0", "cm1"], writes=[pf + "Cz"])
                P.op("dve", lambda e, q=q: e.tensor_scalar(out=Cz[d][:, q::4, 1, 32 * q:32 * q + 32], in0=cm[1][:, q::4, :],
                                                           scalar1=-1.0, scalar2=None, op0=ALU.mult),
                     reads=["cm0", "cm1"], writes=[pf + "Cz"])
            P.op("pool", lambda e: e.memset(LBz[d][:].rearrange("p a b c m -> p (a b c m)"), 0.0), writes=[pf + "LBz"])
            for t in range(NT8):
                for i in range(3):
                    P.dma("sp", lambda e, i=i, t=t: e.dma_start(out=lr[i][:, :], in_=lamR[i][d, :, t, :]),
                          dkey=("prm", "lr", i), writes=[f"lr{i}"])
                for i in range(2):
                    P.dma("sp", lambda e, i=i, t=t: e.dma_start(out=bp[i][:, :], in_=BpT[i][d, :, t, :]),
                          dkey=("prm", "bp", i), writes=[f"bp{i}"])
                lam_bar_again("R", lr, ["lr0", "lr1", "lr2"])
                rre, rim = Rt["are"], Rt["aim"]
                K = lambda k: ktmp[k][:]
                ops = [
                    lambda e: e.tensor_scalar(out=K("nre"), in0=rre[:], scalar1=-1.0, scalar2=None, op0=ALU.add),
                    lambda e: e.tensor_tensor(out=K("t1"), in0=lr[0][:], in1=lr[0][:], op=ALU.mult),
                    lambda e: e.tensor_tensor(out=K("t2"), in0=lr[1][:], in1=lr[1][:], op=ALU.mult),
                    lambda e: e.tensor_tensor(out=K("den"), in0=K("t1"), in1=K("t2"), op=ALU.add),
                    lambda e: e.reciprocal(out=K("den"), in_=K("den")),
                    lambda e: e.tensor_tensor(out=K("t1"), in0=K("nre"), in1=lr[0][:], op=ALU.mult),
                    lambda e: e.tensor_tensor(out=K("t2"), in0=rim[:], in1=lr[1][:], op=ALU.mult),
                    lambda e: e.tensor_tensor(out=K("t1"), in0=K("t1"), in1=K("t2"), op=ALU.add),
                    lambda e: e.tensor_tensor(out=K("kre"), in0=K("t1"), in1=K("den"), op=ALU.mult),
                    lambda e: e.tensor_tensor(out=K("t1"), in0=rim[:], in1=lr[0][:], op=ALU.mult),
                    lambda e: e.tensor_tensor(out=K("t2"), in0=K("nre"), in1=lr[1][:], op=ALU.mult),
                    lambda e: e.tensor_tensor(out=K("t1"), in0=K("t1"), in1=K("t2"), op=ALU.subtract),
                    lambda e: e.tensor_tensor(out=K("kim"), in0=K("t1"), in1=K("den"), op=ALU.mult),
                    lambda e: e.tensor_tensor(out=K("t1"), in0=K("kre"), in1=bp[0][:], op=ALU.mult),
                    lambda e: e.tensor_tensor(out=K("t2"), in0=K("kim"), in1=bp[1][:], op=ALU.mult),
                    lambda e: e.tensor_tensor(out=K("o1"), in0=K("t1"), in1=K("t2"), op=ALU.subtract),
                    lambda e: e.tensor_tensor(out=K("t1"), in0=K("kre"), in1=bp[1][:], op=ALU.mult),
                    lambda e: e.tensor_tensor(out=K("t2"), in0=K("kim"), in1=bp[0][:], op=ALU.mult),
                    lambda e: e.tensor_tensor(out=K("o2"), in0=K("t1"), in1=K("t2"), op=ALU.add),
                ]
                for f in ops:
                    P.op("dve", f, reads=["kap", "lr0", "lr1", "lr2", "bp0", "bp1", "Rare", "Raim"], writes=["kap"])
                for q in range(3):
                    for c, nm in ((0, "o1"), (1, "o2")):
                        P.op("dve", lambda e, q=q, c=c, nm=nm, t=t: e.tensor_copy(
                            out=LBz[d][32 * q:32 * q + 32, t, q, c, :], in_=ktmp[nm][32 * q:32 * q + 32, :]),
                            reads=["kap"], writes=[pf + "LBz"])
                for c, nm in ((0, "o1"), (1, "o2")):
                    P.op("dve", lambda e, c=c, nm=nm, t=t: e.tensor_copy(
                        out=LBz[d][64:128, t, 3, c, :], in_=ktmp[nm][64:128, :]), reads=["kap"], writes=[pf + "LBz"])
                    P.op("dve", lambda e, c=c, t=t: e.memset(LBz[d][64:96, t, 3, c, :], 0.0), reads=["kap"],
                         writes=[pf + "LBz"])

        RR = []
        Rt = {}

        def lam_bar_again(pref, lr_, keys):
            t = Rt
            lre, lim, ls = lr_[0][:], lr_[1][:], lr_[2][:]
            P.op("act", lambda e: e.activation(out=t["dt"][:], in_=ls, func=AF.Exp), reads=keys, writes=[pref + "dt"])
            P.op("dve", lambda e: e.tensor_tensor(out=t["a"][:], in0=lre, in1=t["dt"][:], op=ALU.mult),
                 reads=keys + [pref + "dt"], writes=[pref + "a"])
            P.op("dve", lambda e: e.tensor_tensor(out=t["th"][:], in0=lim, in1=t["dt"][:], op=ALU.mult),
                 reads=keys + [pref + "dt"], writes=[pref + "th"])
            P.op("act", lambda e: e.activation(out=t["mag"][:], in_=t["a"][:], func=AF.Exp), reads=[pref + "a"],
                 writes=[pref + "mag"])
            for nm, off in (("sn", 0.0), ("cs", 0.25)):
                P.op("dve", lambda e, nm=nm, off=off: e.tensor_scalar(out=t[nm][:], in0=t["th"][:], scalar1=1.0 / TWO_PI,
                                                                     scalar2=off, op0=ALU.mult, op1=ALU.add),
                     reads=[pref + "th"], writes=[pref + nm])
                P.op("dve", lambda e, nm=nm: e.tensor_scalar(out=t["k"][:], in0=t[nm][:], scalar1=MAGIC, scalar2=None,
                                                             op0=ALU.add), reads=[pref + nm], writes=[pref + "k"])
                P.op("dve", lambda e, nm=nm: e.tensor_scalar(out=t["k"][:], in0=t["k"][:], scalar1=-MAGIC, scalar2=None,
                                                             op0=ALU.add), reads=[pref + "k"], writes=[pref + "k"])
                P.op("dve", lambda e, nm=nm: e.tensor_tensor(out=t[nm][:], in0=t[nm][:], in1=t["k"][:], op=ALU.subtract),
                     reads=[pref + nm, pref + "k"], writes=[pref + nm])
                P.op("act", lambda e, nm=nm: e.activation(out=t[nm][:], in_=t[nm][:], func=AF.Sin, scale=TWO_PI),
                     reads=[pref + nm], writes=[pref + nm])
            P.op("dve", lambda e: e.tensor_tensor(out=t["are"][:], in0=t["mag"][:], in1=t["cs"][:], op=ALU.mult),
                 reads=[pref + "mag", pref + "cs"], writes=[pref + "are"])
            P.op("dve", lambda e: e.tensor_tensor(out=t["aim"][:], in0=t["mag"][:], in1=t["sn"][:], op=ALU.mult),
                 reads=[pref + "mag", pref + "sn"], writes=[pref + "aim"])

        for k_ in ("dt", "a", "th", "mag", "cs", "sn", "are", "aim", "k"):
            Rt[k_] = sb(f"Rt_{k_}", [128, 128])
        for d in range(2):
            dir_params(d)

        ub = [[sb(f"ub{d}_{i}", [128, NT8, 2, SB], BF16) for i in range(2)] for d in range(2)]
        S_all = sb("S_all", [128, 2, 2, NG, 2, SB], BF16)
        hb_all = sb("hb_all", [128, 2, 2, NG, 2, SB], BF16)
        NH = 2 * 2 * NG * 2
        Hs = [sb(f"H_{i}", [128, 2, 2, NG * 2]) for i in range(2)]
        A1r = sb("A1r", [128, 2, 2, NG, 2])
        A2r = sb("A2r", [128, 2, 2, NG, 2])
        T1 = sb("T1", [128, 2, 2, NG * 2])
        T2 = sb("T2", [128, 2, 2, NG * 2])
        ybuf = [[sb(f"yb{d}_{i}", [128, 2, SB]) for i in range(2)] for d in range(2)]
        psS = [[st.enter_context(nc.psum_tensor(f"psS{d}_{i}", [128, 4, 2, 2, SB], F32)) for i in range(2)] for d in range(2)]
        for d in range(2):
            pf = f"d{d}"
            for c in range(2):
                for b_ in range(2):
                    P.op("dve", lambda e, d=d, c=c, b_=b_: e.tensor_copy(out=A1r[:, c, d, :, b_], in_=A2[d][:, 0, 0, :]),
                         reads=[pf + "A2"], writes=["A1r"])
                    P.op("dve", lambda e, d=d, c=c, b_=b_: e.tensor_copy(out=A2r[:, c, d, :, b_], in_=A2[d][:, 1, c, :]),
                         reads=[pf + "A2"], writes=["A2r"])
        nblk = L // SB
        ncb = LC // SB
        orders = [list(range(nblk)), list(range(ncb - 1, -1, -1)) + list(range(nblk - 1, ncb - 1, -1))]
        outs = []
        SPP = 2 * 2 * NG * 2 * SB
        CST = NG * 2 * SB
        DST = 2 * CST
        stt = dict(ui=0, yi=0, hcur=0)

        def step_ap(tens, j):
            return bass.AP(tens, j, [[SPP, 128], [CST, 2], [DST + SB - 1 - 2 * j, 2], [SB, NG * 2]])

        fl3 = lambda t_: t_[:].rearrange("p c d n -> p (c d n)")
        P.op("dve", lambda e: e.memset(fl3(Hs[0]), 0.0), writes=[("H", 0)])

        def do_iter(i):
            is_x = i >= ncb
            us = stt["ui"] % 2
            stt["ui"] += 1
            blks = [orders[d][i] for d in range(2)]
            for d in range(2):
                pf = f"d{d}"
                t0 = blks[d] * SB
                ubk = (pf, "ub", us)
                for b_ in range(2):
                    P.dma("sp", lambda e, b_=b_, d=d, t0=t0: e.dma_start(out=ub[d][us][:, :, b_, :],
                                                                       in_=u[:, :, b_, t0:t0 + SB].rearrange("t p s -> p t s")),
                          dkey=ubk, writes=[ubk])
                for t in range(NT8):
                    pss = psS[d][t % 2]
                    for q in range(4):
                        for c in range(2):
                            P.op("pe", lambda e, pss=pss, t=t, q=q, c=c, d=d: e.matmul(
                                pss[:, q, c, :, :], LBz[d][:, t, q, c, :], ub[d][us][:, t, :, :], start=True, stop=True),
                                reads=[ubk, pf + "LBz"], writes=[(pf, "psS", t % 2, q, c)])
                    for c in range(2):
                        P.op("act", lambda e, pss=pss, t=t, c=c, d=d: e.activation(
                            out=S_all[:, d, c, 4 * t:4 * t + 4, :, :], in_=pss[:, :, c, :, :], func=AF.Copy),
                            reads=[(pf, "psS", t % 2, q, c) for q in range(4)], writes=[("S", d, t, c)])
            skeys = [("S", d, t, c) for d in range(2) for t in range(NT8) for c in range(2)]
            for j in range(SB):
                hcur = stt["hcur"]
                Hc, Hn = Hs[hcur], Hs[1 - hcur]
                hk, hn = ("H", hcur), ("H", 1 - hcur)
                P.op("dve", lambda e, Hc=Hc: e.tensor_tensor(out=fl3(T1), in0=fl3(Hc), in1=A1r[:].rearrange("p c d g b -> p (c d g b)"), op=ALU.mult),
                     reads=[hk, "A1r"], writes=["T1"])
                P.op("dve", lambda e, Hc=Hc: e.tensor_tensor(out=T2[:, 0, :, :], in0=Hc[:, 1, :, :],
                                                             in1=A2r[:, 0, :, :, :].rearrange("p d g b -> p d (g b)"), op=ALU.mult),
                     reads=[hk, "A2r"], writes=["T2a"])
                P.op("dve", lambda e, Hc=Hc: e.tensor_tensor(out=T2[:, 1, :, :], in0=Hc[:, 0, :, :],
                                                             in1=A2r[:, 1, :, :, :].rearrange("p d g b -> p d (g b)"), op=ALU.mult),
                     reads=[hk, "A2r"], writes=["T2b"])
                P.op("dve", lambda e: e.tensor_tensor(out=fl3(T1), in0=fl3(T1), in1=fl3(T2), op=ALU.add),
                     reads=["T1", "T2a", "T2b"], writes=["T1"])
                P.op("dve", lambda e, Hn=Hn, j=j: e.tensor_tensor(out=Hn[:], in0=T1[:], in1=step_ap(S_all, j), op=ALU.add),
                     reads=["T1"] + (skeys if j in (0, SB - 1) else []), writes=[hn])
                if is_x:
                    P.op("act", lambda e, Hn=Hn, j=j: e.activation(out=step_ap(hb_all, j), in_=Hn[:], func=AF.Copy),
                         reads=[hn], writes=[("hb", j)])
                stt["hcur"] = 1 - hcur
            if not is_x:
                return
            hbk = [("hb", j) for j in range(SB)]
            for d in range(2):
                pf = f"d{d}"
                x0 = blks[d] * SB - LC
                for t in range(NT8):
                    ys = stt["yi"] % 2
                    stt["yi"] += 1
                    yps = psS[d][t % 2][:, 0, 0, :, :]
                    first = True
                    for q in range(4):
                        g = 4 * t + q
                        for c in range(2):
                            P.op("pe", lambda e, yps=yps, g=g, c=c, d=d, first=first, last=(q == 3 and c == 1): e.matmul(
                                yps, Cz[d][:, g, c, :], hb_all[:, d, c, g, :, :], start=first, stop=last),
                                reads=hbk + [pf + "Cz"], writes=[(pf, "psS", t % 2, 0, 0)])
                            first = False
                    P.op("act", lambda e, ys=ys, yps=yps, d=d: e.activation(out=ybuf[d][ys][:, :, :], in_=yps, func=AF.Copy),
                         reads=[(pf, "psS", t % 2, 0, 0)], writes=[(pf, "yb", ys)])
                    dd = P.dma("sp", lambda e, ys=ys, t=t, d=d, x0=x0: e.dma_start(out=ydir[d][t, :, :, x0:x0 + SB], in_=ybuf[d][ys][:, :, :]),
                               dkey=(pf, "yo", ys), reads=[(pf, "yb", ys)], writes=[("yd", d, t)])
                    outs.append(dd)

        for i in range(nblk):
            do_iter(i)

        cy = [[sb(f"cy{k}_{i}", [128, 2, CH]) for i in range(2)] for k in range(2)]
        cu = [sb(f"cu{i}", [128, 2, CH], BF16) for i in range(2)]
        co = [sb(f"co{i}", [128, 2, CH], BF16) for i in range(2)]
        ci = 0
        fin = []
        for t in range(NT8):
            for x0 in range(0, LX, CH):
                s_ = ci % 2
                ci += 1
                for k in range(2):
                    P.dma("sp", lambda e, k=k, s_=s_, t=t, x0=x0: e.dma_start(out=cy[k][s_][:, :, :], in_=ydir[k][t, :, :, x0:x0 + CH]),
                          dkey=("cy", k, s_), reads=[("yd", k, t)], writes=[("cy", k, s_)])
                P.dma("sp", lambda e, s_=s_, t=t, x0=x0: e.dma_start(out=cu[s_][:, :, :], in_=u[t, :, :, LC + x0:LC + x0 + CH]),
                      dkey=("cu", s_), writes=[("cu", s_)])
                P.op("dve", lambda e, s_=s_: e.tensor_tensor(out=cy[0][s_][:], in0=cy[0][s_][:], in1=cy[1][s_][:], op=ALU.add),
                     reads=[("cy", 0, s_), ("cy", 1, s_)], writes=[("cy", 0, s_)])
                P.op("dve", lambda e, s_=s_, t=t: e.scalar_tensor_tensor(out=cy[0][s_][:], in0=cu[s_][:], scalar=dsk[:, t:t + 1],
                                                                      in1=cy[0][s_][:], op0=ALU.mult, op1=ALU.add),
                     reads=[("cy", 0, s_), ("cu", s_), "dsk"], writes=[("cy", 0, s_)])
                P.op("act", lambda e, s_=s_: e.activation(out=co[s_][:], in_=cy[0][s_][:], func=AF.Gelu),
                     reads=[("cy", 0, s_)], writes=[("co", s_)])
                fin.append(P.dma("sp", lambda e, s_=s_, t=t, x0=x0: e.dma_start(out=yg[t, :, :, x0:x0 + CH], in_=co[s_][:]),
                                 dkey=("cog", s_), reads=[("co", s_)], writes=[("yg", t, x0)]))
        P.emit(final_wait_ops=fin[-2:] + outs[-4:])
        print("s5 ops", len(P.ops), "sems", P.n_sems)
    return nc


class PsumRot:
    def __init__(self, nc, st, n=8):
        self.t = [st.enter_context(nc.psum_tensor(f"psr{i}", [128, 512], F32)) for i in range(n)]
        self.i = 0

    def next(self):
        i = self.i % len(self.t)
        self.i += 1
        return self.t[i], ("psr", i)


def outproj_residual(P, nc, ws, pr, v, vkeys, w_out, gate_sb, gate_key, xsrc, dst, c0, TT, xs, os_, cnt):
    last = []
    for oc2 in range(KC):
        buf, key = ws.load(w_out, 0, EC, oc2 * 128, 128)
        ps, pk = pr.next()
        for kc in range(EC):
            P.op("pe", lambda e, ps=ps, buf=buf, kc=kc: e.matmul(ps[:, 0:TT], buf[:, kc, 0:128], v[:, kc, :],
                                                                start=(kc == 0), stop=(kc == EC - 1)),
                 reads=[key] + (vkeys if kc in (0, EC - 1) else []), writes=[pk])
        s = cnt[0] % 2
        cnt[0] += 1
        P.dma("sp", lambda e, s=s, oc2=oc2: e.dma_start(out=xs[s][:, :], in_=xsrc[oc2 * 128:(oc2 + 1) * 128, c0:c0 + TT]),
              dkey=("xs", s), writes=[("xs", s)])
        P.op("dve", lambda e, s=s, ps=ps, oc2=oc2: e.scalar_tensor_tensor(
            out=os_[s][:, :], in0=ps[:, 0:TT], scalar=gate_sb[:, oc2:oc2 + 1], in1=xs[s][:, :], op0=ALU.mult, op1=ALU.add),
            reads=[pk, ("xs", s), gate_key], writes=[("os", s)])
        d = P.dma("sp", lambda e, s=s, oc2=oc2: e.dma_start(out=dst[oc2 * 128:(oc2 + 1) * 128, c0:c0 + TT], in_=os_[s][:, :]),
                  dkey=("oso", s), reads=[("os", s)], writes=[("dst", oc2, c0)])
        last.append(d)
    return last[-2:]


def outproj_residual2(P, pr, wviews, v, vkeys, w_out, gate_sb, gate_key, xsrc, dst, c0, TT, xs, os_, cnt, wcnt, ogc, first_extra):
    wv = w_out.rearrange("(kc p) n -> p kc n", p=128)
    last = []
    for og in range(KC // 4):
        bs = (ogc[0] % 2) * 4
        ogc[0] += 1
        for kg in range(8):
            i = wcnt[0] % 4
            wcnt[0] += 1
            buf = wviews[i]
            key = ("wb", i)
            for hq in range(2):
                P.dma("pool", lambda e, buf=buf, hq=hq, kg=kg, og=og: e.dma_start(
                    out=buf[:, hq * 4:(hq + 1) * 4, :], in_=wv[:, kg * 8 + hq * 4:kg * 8 + (hq + 1) * 4, og * 512:(og + 1) * 512]),
                    dkey=key, writes=[key] + first_extra)
            for k8 in range(8):
                kc = kg * 8 + k8
                for o in range(4):
                    P.op("pe", lambda e, buf=buf, k8=k8, kc=kc, o=o, bs=bs: e.matmul(
                        pr.t[bs + o][:, 0:TT], buf[:, k8, o * 128:(o + 1) * 128], v[:, kc, :],
                        start=(kc == 0), stop=(kc == EC - 1)), reads=[key] + vkeys, writes=[("psr", bs + o)])
        for o in range(4):
            oc2 = og * 4 + o
            ps, pk = pr.t[bs + o], ("psr", bs + o)
            s = cnt[0] % 2
            cnt[0] += 1
            P.dma("sp", lambda e, s=s, oc2=oc2: e.dma_start(out=xs[s][:, :], in_=xsrc[oc2 * 128:(oc2 + 1) * 128, c0:c0 + TT]),
                  dkey=("xs", s), writes=[("xs", s)])
            P.op("dve", lambda e, s=s, ps=ps, oc2=oc2: e.scalar_tensor_tensor(
                out=os_[s][:, :], in0=ps[:, 0:TT], scalar=gate_sb[:, oc2:oc2 + 1], in1=xs[s][:, :], op0=ALU.mult, op1=ALU.add),
                reads=[pk, ("xs", s), gate_key], writes=[("os", s)])
            d = P.dma("sp", lambda e, s=s, oc2=oc2: e.dma_start(out=dst[oc2 * 128:(oc2 + 1) * 128, c0:c0 + TT], in_=os_[s][:, :]),
                      dkey=("oso", s), reads=[("os", s)], writes=[("dst", oc2, c0)])
            last.append(d)
    return last[-2:]


def build_stage_c(T, TT=512):
    nc = bass.Bass("TRN2", target_bir_lowering=False)
    din = lambda n, s, dt=F32: nc.dram_tensor(n, s, dt, kind="ExternalInput").ap()
    ygT = din("ygT", [E, T], BF16)
    zT = din("zT", [E, T], BF16)
    xT = din("xT", [D, T])
    gate = din("gate", [128, KC])
    w_glu = din("w_glu", [E, E])
    b_glu = din("b_glu", [128, EC])
    w_out = din("w_out", [E, D])
    x1T = nc.dram_tensor("x1T", [D, T], F32, kind="ExternalOutput").ap()
    st = contextlib.ExitStack()
    with st:
        sb = lambda name, shape, dt=F32: st.enter_context(nc.sbuf_tensor(name, shape, dt))
        P = Prog(nc)
        ws = WStream(P, nc, st, EC, 128)
        pr = PsumRot(nc, st)
        ygs = sb("ygs", [128, EC, TT], BF16)
        v = sb("v", [128, EC, TT], BF16)
        g_sb = sb("g_sb", [128, KC])
        bg_sb = sb("bg_sb", [128, EC])
        t1 = [sb(f"t1_{i}", [128, TT]) for i in range(2)]
        t2 = [sb(f"t2_{i}", [128, TT]) for i in range(2)]
        zs = [sb(f"zs{i}", [128, TT], BF16) for i in range(2)]
        xs = [sb(f"xs{i}", [128, TT]) for i in range(2)]
        os_ = [sb(f"os{i}", [128, TT]) for i in range(2)]
        P.dma("sp", lambda e: e.dma_start(out=g_sb[:, :], in_=gate[:, :]), dkey=("prm", "g"), writes=["gate"])
        P.dma("sp", lambda e: e.dma_start(out=bg_sb[:, :], in_=b_glu[:, :]), dkey=("prm", "bg"), writes=["bg"])
        ygv = ygT.rearrange("(kc p) t -> p kc t", p=128)
        cnt = [0]
        zi = 0
        fin = []
        for tt in range(T // TT):
            c0 = tt * TT
            for q in range(4):
                P.dma("sp", lambda e, q=q, c0=c0: e.dma_start(out=ygs[:, q * 16:(q + 1) * 16, :], in_=ygv[:, q * 16:(q + 1) * 16, c0:c0 + TT]),
                      dkey="ygs", writes=["ygs"])
            for oc in range(EC):
                buf, key = ws.load(w_glu, 0, EC, oc * 128, 128)
                ps, pk = pr.next()
                for kc in range(EC):
                    P.op("pe", lambda e, ps=ps, buf=buf, kc=kc: e.matmul(ps[:, 0:TT], buf[:, kc, 0:128], ygs[:, kc, :],
                                                                        start=(kc == 0), stop=(kc == EC - 1)),
                         reads=[key, "ygs"], writes=[pk])
                s = zi % 2
                zi += 1
                P.dma("sp", lambda e, s=s, oc=oc, c0=c0: e.dma_start(out=zs[s][:, :], in_=zT[oc * 128:(oc + 1) * 128, c0:c0 + TT]),
                      dkey=("zs", s), writes=[("zs", s)])
                P.op("act", lambda e, s=s, ps=ps, oc=oc: e.activation(out=t1[s][:, :], in_=ps[:, 0:TT], func=AF.Sigmoid,
                                                                     bias=bg_sb[:, oc:oc + 1]),
                     reads=[pk, "bg"], writes=[("t1", s)])
                P.op("act", lambda e, s=s: e.activation(out=t2[s][:, :], in_=zs[s][:, :], func=AF.Silu),
                     reads=[("zs", s)], writes=[("t2", s)])
                P.op("dve", lambda e, s=s, oc=oc: e.tensor_tensor(out=t1[s][:, :], in0=t1[s][:, :], in1=ygs[:, oc, :], op=ALU.mult),
                     reads=[("t1", s), "ygs"], writes=[("t1", s)])
                P.op("dve", lambda e, s=s, oc=oc: e.tensor_tensor(out=v[:, oc, :], in0=t1[s][:, :], in1=t2[s][:, :], op=ALU.mult),
                     reads=[("t1", s), ("t2", s)], writes=[("v", oc)])
            fin = outproj_residual(P, nc, ws, pr, v, [("v", oc) for oc in range(EC)], w_out, g_sb, "gate", xT, x1T, c0, TT,
                                   xs, os_, cnt)
        P.emit(final_wait_ops=fin)
        print("stage c ops", len(P.ops), "sems", P.n_sems)
    return nc


def build_stage_ada():
    nc = bass.Bass("TRN2", target_bir_lowering=False)
    cT3 = nc.dram_tensor("cT3", [128, KC, 3], F32, kind="ExternalInput").ap()
    wa = nc.dram_tensor("wa", [2, D, 1536], F32, kind="ExternalInput").ap()
    ba = nc.dram_tensor("ba", [2, 128, 12], F32, kind="ExternalInput").ap()
    modp = nc.dram_tensor("modp", [2, 128, 12, 3], F32, kind="ExternalOutput").ap()
    st = contextlib.ExitStack()
    with st:
        sb = lambda name, shape, dt=F32: st.enter_context(nc.sbuf_tensor(name, shape, dt))
        P = Prog(nc)
        ws = WStream(P, nc, st, KC, 256)
        c_sb = sb("c_sb", [128, KC, 3])
        sc = sb("sc", [128, KC, 3], BF16)
        ba_sb = sb("ba_sb", [128, 2, 12])
        res = sb("res", [128, 2, 12, 3])
        ps = st.enter_context(nc.psum_tensor("psA", [128, 2, 12, 4], F32))
        P.dma("sp", lambda e: e.dma_start(out=c_sb[:], in_=cT3[:]), dkey=("prm", "c"), writes=["c"])
        for l in range(2):
            P.dma("sp", lambda e, l=l: e.dma_start(out=ba_sb[:, l, :], in_=ba[l]), dkey=("prm", "ba", l), writes=[("ba", l)])
        P.op("act", lambda e: e.activation(out=sc[:].rearrange("p a b -> p (a b)"), in_=c_sb[:].rearrange("p a b -> p (a b)"),
                                           func=AF.Silu), reads=["c"], writes=["sc"])
        for l in range(2):
            for blk in range(6):
                buf, key = ws.load(wa[l], 0, KC, blk * 256, 256)
                for o2 in range(2):
                    oc = blk * 2 + o2
                    for kc in range(KC):
                        P.op("pe", lambda e, buf=buf, kc=kc, o2=o2, oc=oc, l=l: e.matmul(
                            ps[:, l, oc, 0:3], buf[:, kc, o2 * 128:(o2 + 1) * 128], sc[:, kc, :],
                            start=(kc == 0), stop=(kc == KC - 1)), reads=[key, "sc"], writes=["psA"])
        P.op("dve", lambda e: e.tensor_tensor(out=res[:], in0=ps[:, :, :, 0:3],
                                              in1=ba_sb[:].unsqueeze(3).broadcast_to([128, 2, 12, 3]), op=ALU.add),
             reads=["psA", ("ba", 0), ("ba", 1)], writes=["res"])
        od = P.dma("sp", lambda e: e.dma_start(out=modp.rearrange("l p o c -> p l o c"), in_=res[:]), dkey="out",
                   reads=["res"], writes=["out"])
        P.emit(final_wait_ops=[od])
    return nc


def build_stage_d(T=2048, TT=512):
    R, W = 32, 64
    RH = R + 16
    nc = bass.Bass("TRN2", target_bir_lowering=False)
    din = lambda n, s, dt=F32: nc.dram_tensor(n, s, dt, kind="ExternalInput").ap()
    uh = din("uh", [E, RH * W], BF16)
    zT = din("zT", [E, T], BF16)
    xT = din("xT", [D, T])
    gate = din("gate", [128, KC])
    pool_w = din("pool_w", [4, 2048, 2048])
    pscale = din("pscale", [128, EC])
    w_out = din("w_out", [E, D])
    fnw = din("fnw", [128, KC])
    invc = din("invc", [4, 128, T])
    vT = nc.dram_tensor("vT", [E, T], BF16, kind="ExternalOutput").ap()
    x2T = nc.dram_tensor("x2T", [D, T], F32, kind="ExternalOutput").ap()
    outT = nc.dram_tensor("outT", [D, T], F32, kind="ExternalOutput").ap()
    NTT = T // TT
    st = contextlib.ExitStack()
    with st:
        sb = lambda name, shape, dt=F32: st.enter_context(nc.sbuf_tensor(name, shape, dt))
        P = Prog(nc)
        ws = WStream(P, nc, st, EC, 128)
        pr = PsumRot(nc, st)
        big = sb("big", [128, 16 * T], BF16)
        dl = big[:].rearrange("p (i t) -> p i t", i=16)
        vt = big[:].rearrange("p (k t) -> p k t", k=EC)
        assert 16 * T == EC * TT
        g_sb = sb("g_sb", [128, KC])
        ps_sb = sb("ps_sb", [128, EC])
        fn_sb = sb("fn_sb", [128, KC])
        inv_sb = sb("inv_sb", [128, T])
        ut = [sb(f"ut{i}", [128, RH, W], BF16) for i in range(2)]
        WP = W + 16
        Wk = [sb(f"Wk{i}", [128, RH, WP]) for i in range(2)]
        t2 = [sb(f"t2_{i}", [128, TT]) for i in range(2)]
        zs = [sb(f"zs{i}", [128, TT], BF16) for i in range(2)]
        vst = [sb(f"vst{i}", [128, TT], BF16) for i in range(2)]
        xs = [sb(f"xs{i}", [128, TT]) for i in range(2)]
        os_ = [sb(f"os{i}", [128, TT]) for i in range(2)]
        x3 = [sb(f"x3_{i}", [128, T]) for i in range(2)]
        sq = [sb(f"sq{i}", [128, T], BF16) for i in range(2)]
        rstd = sb("rstd", [128, T])
        ones = sb("ones", [128, 128], BF16)
        eps_t = sb("eps_t", [128, 1])
        P.op("dve", lambda e: e.memset(ones[:, :], 1.0), writes=["ones"])
        P.op("dve", lambda e: e.memset(eps_t[:, :], EPS), writes=["eps"])
        P.dma("sp", lambda e: e.dma_start(out=g_sb[:, :], in_=gate[:, :]), dkey=("prm", "g"), writes=["gate"])
        P.dma("sp", lambda e: e.dma_start(out=ps_sb[:, :], in_=pscale[:, :]), dkey=("prm", "ps"), writes=["pscale"])
        P.dma("sp", lambda e: e.dma_start(out=fn_sb[:, :], in_=fnw[:, :]), dkey=("prm", "fn"), writes=["fnw"])
        uhv = uh.rearrange("(kc p) (r w) -> p kc r w", p=128, w=W)
        ui = 0
        zi = 0
        for k in range(4):
            P.dma("sp", lambda e, k=k: e.dma_start(out=inv_sb[:, :], in_=invc[k]), dkey=("prm", "inv"), writes=["inv"])
            for i in range(16):
                kc = 16 * k + i
                s = ui % 2
                ui += 1
                P.dma("sp", lambda e, s=s, kc=kc: e.dma_start(out=ut[s][:, :, :], in_=uhv[:, kc, :, :]), dkey=("ut", s),
                      writes=[("ut", s)])
                for lo_ in (0, WP - 8):
                    P.op("pool", lambda e, lo_=lo_: e.memset(Wk[0][:, :, lo_:lo_ + 8], 0.0), writes=["W0"])
                P.op("act", lambda e, s=s: e.activation(out=Wk[0][:, :, 8:8 + W], in_=ut[s][:, :, :], func=AF.Copy),
                     reads=[("ut", s)], writes=["W0"])
                cur = 0

                def step(fns, cur):
                    src, dst = Wk[cur], Wk[1 - cur]
                    for f in fns:
                        P.op("dve", lambda e, f=f, src=src, dst=dst: f(e, src, dst), reads=[f"W{cur}"], writes=[f"W{1 - cur}"])
                    return 1 - cur
                cur = step([lambda e, a, b: e.tensor_tensor(out=b[:, :, 1:WP], in0=a[:, :, 0:WP - 1], in1=a[:, :, 1:WP], op=ALU.add)], cur)
                for l in range(1, k + 1):
                    sh = 2 ** (l - 1)
                    cur = step([lambda e, a, b, sh=sh: e.tensor_tensor(out=b[:, :, sh:WP - sh], in0=a[:, :, 0:WP - 2 * sh], in1=a[:, :, 2 * sh:WP], op=ALU.add)], cur)
                cur = step([lambda e, a, b: e.tensor_tensor(out=b[:, 1:RH, :], in0=a[:, 0:RH - 1, :], in1=a[:, 1:RH, :], op=ALU.add)], cur)
                for l in range(1, k + 1):
                    sh = 2 ** (l - 1)
                    cur = step([lambda e, a, b, sh=sh: e.tensor_tensor(out=b[:, sh:RH - sh, :], in0=a[:, 0:RH - 2 * sh, :], in1=a[:, 2 * sh:RH, :], op=ALU.add)], cur)
                src, dst = Wk[cur], Wk[1 - cur]
                P.op("dve", lambda e, src=src, dst=dst: e.tensor_tensor(out=dst[:, 8:8 + R, 8:8 + W], in0=src[:, 8:8 + R, 8:8 + W],
                                                                        in1=inv_sb[:].rearrange("p (r w) -> p r w", w=W), op=ALU.mult),
                     reads=[f"W{cur}", "inv"], writes=[f"W{1 - cur}"])
                P.op("dve", lambda e, dst=dst, s=s, i=i: e.tensor_tensor(out=dl[:, i, :].rearrange("p (r w) -> p r w", w=W),
                                                                       in0=dst[:, 8:8 + R, 8:8 + W], in1=ut[s][:, 8:8 + R, :], op=ALU.subtract),
                     reads=[f"W{1 - cur}", ("ut", s)], writes=[("dl", i), "W0", "W1"])
            dkeys = [("dl", i) for i in range(16)]
            for oc in range(16):
                ocg = 16 * k + oc
                buf, key = ws.load(pool_w[k], 0, 16, oc * 128, 128)
                for tt in range(NTT):
                    ps, pk = pr.next()
                    for kc in range(16):
                        P.op("pe", lambda e, ps=ps, buf=buf, kc=kc, tt=tt: e.matmul(ps[:, 0:TT], buf[:, kc, 0:128],
                                                                                  dl[:, kc, tt * TT:(tt + 1) * TT],
                                                                                  start=(kc == 0), stop=(kc == 15)),
                             reads=[key] + dkeys, writes=[pk])
                    s = zi % 2
                    zi += 1
                    P.dma("sp", lambda e, s=s, ocg=ocg, tt=tt: e.dma_start(out=zs[s][:, :], in_=zT[ocg * 128:(ocg + 1) * 128, tt * TT:(tt + 1) * TT]),
                          dkey=("zs", s), writes=[("zs", s)])
                    P.op("act", lambda e, s=s: e.activation(out=t2[s][:, :], in_=zs[s][:, :], func=AF.Silu),
                         reads=[("zs", s)], writes=[("t2", s)])
                    P.op("dve", lambda e, s=s, ps=ps, ocg=ocg: e.scalar_tensor_tensor(
                        out=vst[s][:, :], in0=ps[:, 0:TT], scalar=ps_sb[:, ocg:ocg + 1], in1=t2[s][:, :], op0=ALU.mult, op1=ALU.mult),
                        reads=[pk, ("t2", s), "pscale"], writes=[("vst", s)])
                    P.dma("sp", lambda e, s=s, ocg=ocg, tt=tt: e.dma_start(out=vT[ocg * 128:(ocg + 1) * 128, tt * TT:(tt + 1) * TT], in_=vst[s][:, :]),
                          dkey=("vso", s), reads=[("vst", s)], writes=[("vT", ocg, tt)])
        vTv = vT.rearrange("(kc p) t -> p kc t", p=128)
        cnt = [0]
        for tt in range(NTT):
            c0 = tt * TT
            for q in range(4):
                P.dma("sp", lambda e, q=q, c0=c0: e.dma_start(out=vt[:, q * 16:(q + 1) * 16, :], in_=vTv[:, q * 16:(q + 1) * 16, c0:c0 + TT]),
                      dkey="vtl", reads=[("vT", ocg, tt) for ocg in range(EC)], writes=["vtile"] + [("dl", i) for i in range(16)])
            outproj_residual(P, nc, ws, pr, vt, ["vtile"], w_out, g_sb, "gate", xT, x2T, c0, TT, xs, os_, cnt)
        x2v = x2T.rearrange("(kc p) t -> p kc t", p=128)
        outv = outT.rearrange("(kc p) t -> p kc t", p=128)
        dst_keys = lambda kc: [("dst", kc, tt * TT) for tt in range(NTT)]
        for kc in range(KC):
            s = kc % 2
            P.dma("sp", lambda e, s=s, kc=kc: e.dma_start(out=x3[s][:, :], in_=x2v[:, kc, :]), dkey=("x3", s),
                  reads=dst_keys(kc), writes=[("x3", s)])
            P.op("act", lambda e, s=s: e.activation(out=sq[s][:, :], in_=x3[s][:, :], func=AF.Square),
                 reads=[("x3", s)], writes=[("sq", s)])
            for tt in range(NTT):
                P.op("pe", lambda e, s=s, tt=tt, kc=kc: e.matmul(pr.t[tt][:, 0:TT], ones[:, :], sq[s][:, tt * TT:(tt + 1) * TT],
                                                              start=(kc == 0), stop=(kc == KC - 1)),
                     reads=[("sq", s), "ones"], writes=[("psr", tt)])
        for tt in range(NTT):
            P.op("act", lambda e, tt=tt: e.activation(out=rstd[:, tt * TT:(tt + 1) * TT], in_=pr.t[tt][:, 0:TT],
                                                     func=AF.Sqrt, scale=1.0 / D, bias=eps_t[:, 0:1]),
                 reads=[("psr", tt), "eps"], writes=[("rs", tt)])
            P.op("dve", lambda e, tt=tt: e.reciprocal(out=rstd[:, tt * TT:(tt + 1) * TT], in_=rstd[:, tt * TT:(tt + 1) * TT]),
                 reads=[("rs", tt)], writes=[("rs", tt)])
        fin = []
        for kc in range(KC):
            s = kc % 2
            P.dma("sp", lambda e, s=s, kc=kc: e.dma_start(out=x3[s][:, :], in_=x2v[:, kc, :]), dkey=("x3", s),
                  reads=dst_keys(kc), writes=[("x3", s)])
            P.op("dve", lambda e, s=s: e.tensor_tensor(out=x3[s][:, :], in0=x3[s][:, :], in1=rstd[:, :], op=ALU.mult),
                 reads=[("x3", s)] + [("rs", tt) for tt in range(NTT)], writes=[("x3", s)])
            P.op("act", lambda e, s=s, kc=kc: e.activation(out=x3[s][:, :], in_=x3[s][:, :], func=AF.Copy, scale=fn_sb[:, kc:kc + 1]),
                 reads=[("x3", s), "fnw"], writes=[("x3", s)])
            fin.append(P.dma("sp", lambda e, s=s, kc=kc: e.dma_start(out=outv[:, kc, :], in_=x3[s][:, :]), dkey=("x3o", s),
                             reads=[("x3", s)], writes=[("out", kc)]))
        P.emit(final_wait_ops=fin[-2:])
        print("stage d ops", len(P.ops), "sems", P.n_sems)
    return nc


def build_stage_d2(T=2048, TT=512):
    R, W = 32, 64
    RH = R + 16
    nc = bass.Bass("TRN2", target_bir_lowering=False)
    din = lambda n, s, dt=F32: nc.dram_tensor(n, s, dt, kind="ExternalInput").ap()
    uh = din("uh", [E, RH * W], BF16)
    zT = din("zT", [E, T], BF16)
    xT = din("xT", [D, T])
    gate = din("gate", [128, KC])
    pool_w = din("pool_w", [4, 2048, 2048])
    pscale = din("pscale", [128, EC])
    w_out = din("w_out", [E, D])
    fnw = din("fnw", [128, KC])
    invc = din("invc", [4, 128, T])
    vT = nc.dram_tensor("vT", [E, T], BF16, kind="ExternalOutput").ap()
    x2T = nc.dram_tensor("x2T", [D, T], F32, kind="ExternalOutput").ap()
    outT = nc.dram_tensor("outT", [D, T], F32, kind="ExternalOutput").ap()
    NTT = T // TT
    st = contextlib.ExitStack()
    with st:
        sb = lambda name, shape, dt=F32: st.enter_context(nc.sbuf_tensor(name, shape, dt))
        P = Prog(nc)
        ws = WStream(P, nc, st, EC, 128)
        pr = PsumRot(nc, st)
        big = sb("big", [128, 16 * T], BF16)
        dl = big[:].rearrange("p (i t) -> p i t", i=16)
        vt = big[:].rearrange("p (k t) -> p k t", k=EC)
        assert 16 * T == EC * TT
        g_sb = sb("g_sb", [128, KC])
        ps_sb = sb("ps_sb", [128, EC])
        fn_sb = sb("fn_sb", [128, KC])
        inv_sb = sb("inv_sb", [128, T])
        ut = [sb(f"ut{i}", [128, RH, W], BF16) for i in range(2)]
        WP = W + 16
        Wk = [sb(f"Wk{i}", [128, RH, WP]) for i in range(2)]
        t2 = [sb(f"t2_{i}", [128, TT]) for i in range(2)]
        zs = [sb(f"zs{i}", [128, TT], BF16) for i in range(2)]
        vst = [sb(f"vst{i}", [128, TT], BF16) for i in range(2)]
        xs = [sb(f"xs{i}", [128, TT]) for i in range(2)]
        os_ = [sb(f"os{i}", [128, TT]) for i in range(2)]
        x3 = [sb(f"x3_{i}", [128, T]) for i in range(2)]
        sq = [sb(f"sq{i}", [128, T], BF16) for i in range(2)]
        rstd = sb("rstd", [128, T])
        ones = sb("ones", [128, 128], BF16)
        eps_t = sb("eps_t", [128, 1])
        P.op("dve", lambda e: e.memset(ones[:, :], 1.0), writes=["ones"])
        P.op("dve", lambda e: e.memset(eps_t[:, :], EPS), writes=["eps"])
        P.dma("sp", lambda e: e.dma_start(out=g_sb[:, :], in_=gate[:, :]), dkey=("prm", "g"), writes=["gate"])
        P.dma("sp", lambda e: e.dma_start(out=ps_sb[:, :], in_=pscale[:, :]), dkey=("prm", "ps"), writes=["pscale"])
        P.dma("sp", lambda e: e.dma_start(out=fn_sb[:, :], in_=fnw[:, :]), dkey=("prm", "fn"), writes=["fnw"])
        uhv = uh.rearrange("(kc p) (r w) -> p kc r w", p=128, w=W)
        ui = 0
        zi = 0
        for k in range(4):
            P.dma("sp", lambda e, k=k: e.dma_start(out=inv_sb[:, :], in_=invc[k]), dkey=("prm", "inv"), writes=["inv"])
            for i in range(16):
                kc = 16 * k + i
                s = ui % 2
                ui += 1
                P.dma("sp", lambda e, s=s, kc=kc: e.dma_start(out=ut[s][:, :, :], in_=uhv[:, kc, :, :]), dkey=("ut", s),
                      writes=[("ut", s)])
                for lo_ in (0, WP - 8):
                    P.op("pool", lambda e, lo_=lo_: e.memset(Wk[0][:, :, lo_:lo_ + 8], 0.0), writes=["W0"])
                P.op("act", lambda e, s=s: e.activation(out=Wk[0][:, :, 8:8 + W], in_=ut[s][:, :, :], func=AF.Copy),
                     reads=[("ut", s)], writes=["W0"])
                cur = 0

                def step(fns, cur):
                    src, dst = Wk[cur], Wk[1 - cur]
                    for f in fns:
                        P.op("dve", lambda e, f=f, src=src, dst=dst: f(e, src, dst), reads=[f"W{cur}"], writes=[f"W{1 - cur}"])
                    return 1 - cur
                cur = step([lambda e, a, b: e.tensor_tensor(out=b[:, :, 1:WP], in0=a[:, :, 0:WP - 1], in1=a[:, :, 1:WP], op=ALU.add)], cur)
                for l in range(1, k + 1):
                    sh = 2 ** (l - 1)
                    cur = step([lambda e, a, b, sh=sh: e.tensor_tensor(out=b[:, :, sh:WP - sh], in0=a[:, :, 0:WP - 2 * sh], in1=a[:, :, 2 * sh:WP], op=ALU.add)], cur)
                cur = step([lambda e, a, b: e.tensor_tensor(out=b[:, 1:RH, :], in0=a[:, 0:RH - 1, :], in1=a[:, 1:RH, :], op=ALU.add)], cur)
                for l in range(1, k + 1):
                    sh = 2 ** (l - 1)
                    cur = step([lambda e, a, b, sh=sh: e.tensor_tensor(out=b[:, sh:RH - sh, :], in0=a[:, 0:RH - 2 * sh, :], in1=a[:, 2 * sh:RH, :], op=ALU.add)], cur)
                src, dst = Wk[cur], Wk[1 - cur]
                P.op("dve", lambda e, src=src, dst=dst: e.tensor_tensor(out=dst[:, 8:8 + R, 8:8 + W], in0=src[:, 8:8 + R, 8:8 + W],
                                                                        in1=inv_sb[:].rearrange("p (r w) -> p r w", w=W), op=ALU.mult),
                     reads=[f"W{cur}", "inv"], writes=[f"W{1 - cur}"])
                P.op("dve", lambda e, dst=dst, s=s, i=i: e.tensor_tensor(out=dl[:, i, :].rearrange("p (r w) -> p r w", w=W),
                                                                       in0=dst[:, 8:8 + R, 8:8 + W], in1=ut[s][:, 8:8 + R, :], op=ALU.subtract),
                     reads=[f"W{1 - cur}", ("ut", s)], writes=[("dl", i), "W0", "W1"])
            dkeys = [("dl", i) for i in range(16)]
            for oc in range(16):
                ocg = 16 * k + oc
                buf, key = ws.load(pool_w[k], 0, 16, oc * 128, 128)
                for tt in range(NTT):
                    ps, pk = pr.next()
                    for kc in range(16):
                        P.op("pe", lambda e, ps=ps, buf=buf, kc=kc, tt=tt: e.matmul(ps[:, 0:TT], buf[:, kc, 0:128],
                                                                                  dl[:, kc, tt * TT:(tt + 1) * TT],
                                                                                  start=(kc == 0), stop=(kc == 15)),
                             reads=[key] + dkeys, writes=[pk])
                    s = zi % 2
                    zi += 1
                    P.dma("sp", lambda e, s=s, ocg=ocg, tt=tt: e.dma_start(out=zs[s][:, :], in_=zT[ocg * 128:(ocg + 1) * 128, tt * TT:(tt + 1) * TT]),
                          dkey=("zs", s), writes=[("zs", s)])
                    P.op("act", lambda e, s=s: e.activation(out=t2[s][:, :], in_=zs[s][:, :], func=AF.Silu),
                         reads=[("zs", s)], writes=[("t2", s)])
                    P.op("dve", lambda e, s=s, ps=ps, ocg=ocg: e.scalar_tensor_tensor(
                        out=vst[s][:, :], in0=ps[:, 0:TT], scalar=ps_sb[:, ocg:ocg + 1], in1=t2[s][:, :], op0=ALU.mult, op1=ALU.mult),
                        reads=[pk, ("t2", s), "pscale"], writes=[("vst", s)])
                    P.dma("sp", lambda e, s=s, ocg=ocg, tt=tt: e.dma_start(out=vT[ocg * 128:(ocg + 1) * 128, tt * TT:(tt + 1) * TT], in_=vst[s][:, :]),
                          dkey=("vso", s), reads=[("vst", s)], writes=[("vT", ocg, tt)])
        vTv = vT.rearrange("(kc p) t -> p kc t", p=128)
        cnt = [0]
        wcnt = [0]
        ogc = [0]
        wviews = []
        for wb_ in ws.bufs:
            flat = wb_[:].rearrange("p k n -> p (k n)")
            for hh in range(2):
                wviews.append(flat[:, hh * 4096:(hh + 1) * 4096].rearrange("p (k n) -> p k n", k=8))
        for tt in range(NTT):
            c0 = tt * TT
            for q in range(4):
                P.dma("sp", lambda e, q=q, c0=c0: e.dma_start(out=vt[:, q * 16:(q + 1) * 16, :], in_=vTv[:, q * 16:(q + 1) * 16, c0:c0 + TT]),
                      dkey="vtl", reads=[("vT", ocg, tt) for ocg in range(EC)], writes=["vtile"] + [("dl", i) for i in range(16)])
            outproj_residual2(P, pr, wviews, vt, ["vtile"], w_out, g_sb, "gate", xT, x2T, c0, TT, xs, os_, cnt, wcnt, ogc,
                              [("wt", 0), ("wt", 1)])
        x2v = x2T.rearrange("(kc p) t -> p kc t", p=128)
        outv = outT.rearrange("(kc p) t -> p kc t", p=128)
        dst_keys = lambda kc: [("dst", kc, tt * TT) for tt in range(NTT)]
        for kc in range(KC):
            s = kc % 2
            P.dma("sp", lambda e, s=s, kc=kc: e.dma_start(out=x3[s][:, :], in_=x2v[:, kc, :]), dkey=("x3", s),
                  reads=dst_keys(kc), writes=[("x3", s)])
            P.op("act", lambda e, s=s: e.activation(out=sq[s][:, :], in_=x3[s][:, :], func=AF.Square),
                 reads=[("x3", s)], writes=[("sq", s)])
            for tt in range(NTT):
                P.op("pe", lambda e, s=s, tt=tt, kc=kc: e.matmul(pr.t[tt][:, 0:TT], ones[:, :], sq[s][:, tt * TT:(tt + 1) * TT],
                                                              start=(kc == 0), stop=(kc == KC - 1)),
                     reads=[("sq", s), "ones"], writes=[("psr", tt)])
        for tt in range(NTT):
            P.op("act", lambda e, tt=tt: e.activation(out=rstd[:, tt * TT:(tt + 1) * TT], in_=pr.t[tt][:, 0:TT],
                                                     func=AF.Sqrt, scale=1.0 / D, bias=eps_t[:, 0:1]),
                 reads=[("psr", tt), "eps"], writes=[("rs", tt)])
            P.op("dve", lambda e, tt=tt: e.reciprocal(out=rstd[:, tt * TT:(tt + 1) * TT], in_=rstd[:, tt * TT:(tt + 1) * TT]),
                 reads=[("rs", tt)], writes=[("rs", tt)])
        fin = []
        for kc in range(KC):
            s = kc % 2
            P.dma("sp", lambda e, s=s, kc=kc: e.dma_start(out=x3[s][:, :], in_=x2v[:, kc, :]), dkey=("x3", s),
                  reads=dst_keys(kc), writes=[("x3", s)])
            P.op("dve", lambda e, s=s: e.tensor_tensor(out=x3[s][:, :], in0=x3[s][:, :], in1=rstd[:, :], op=ALU.mult),
                 reads=[("x3", s)] + [("rs", tt) for tt in range(NTT)], writes=[("x3", s)])
            P.op("act", lambda e, s=s, kc=kc: e.activation(out=x3[s][:, :], in_=x3[s][:, :], func=AF.Copy, scale=fn_sb[:, kc:kc + 1]),
                 reads=[("x3", s), "fnw"], writes=[("x3", s)])
            fin.append(P.dma("sp", lambda e, s=s, kc=kc: e.dma_start(out=outv[:, kc, :], in_=x3[s][:, :]), dkey=("x3o", s),
                             reads=[("x3", s)], writes=[("out", kc)]))
        P.emit(final_wait_ops=fin[-2:])
        print("stage d2 ops", len(P.ops), "sems", P.n_sems)
    return nc


def inv_counts(q):
    out = np.zeros((4, 32 * 64), np.float32)
    for k, w in enumerate((2, 4, 8, 16)):
        r = np.arange(32 * q, 32 * q + 32)
        c = np.arange(64)
        rc = np.clip(r + w - w // 2, 0, 128) - np.clip(r - w // 2, 0, 128)
        cc = np.clip(c + w - w // 2, 0, 64) - np.clip(c - w // 2, 0, 64)
        out[k] = (1.0 / (rc[:, None] * cc[None, :]).astype(np.float32)).reshape(-1)
    return out


def build_stage_c2(T, TT=1024):
    nc = bass.Bass("TRN2", target_bir_lowering=False)
    din = lambda n, s, dt=F32: nc.dram_tensor(n, s, dt, kind="ExternalInput").ap()
    ygT = din("ygT", [E, T], BF16)
    zT = din("zT", [E, T], BF16)
    xT = din("xT", [D, T])
    gate = din("gate", [128, KC])
    w_glu = din("w_glu", [E, E])
    b_glu = din("b_glu", [128, EC])
    w_out = din("w_out", [E, D])
    vT = nc.dram_tensor("vT", [E, T], BF16, kind="ExternalOutput").ap()
    x1T = nc.dram_tensor("x1T", [D, T], F32, kind="ExternalOutput").ap()
    HW = 512
    NH_ = TT // HW
    st = contextlib.ExitStack()
    with st:
        sb = lambda name, shape, dt=F32: st.enter_context(nc.sbuf_tensor(name, shape, dt))
        P = Prog(nc)
        ws = WStream(P, nc, st, EC, 128)
        pr = PsumRot(nc, st)
        big = sb("big", [128, EC, TT], BF16)
        g_sb = sb("g_sb", [128, KC])
        bg_sb = sb("bg_sb", [128, EC])
        t1 = [sb(f"t1_{i}", [128, HW]) for i in range(2)]
        t2 = [sb(f"t2_{i}", [128, HW]) for i in range(2)]
        zs = [sb(f"zs{i}", [128, HW], BF16) for i in range(2)]
        vst = [sb(f"vst{i}", [128, HW], BF16) for i in range(2)]
        xs = [sb(f"xs{i}", [128, HW]) for i in range(2)]
        os_ = [sb(f"os{i}", [128, HW]) for i in range(2)]
        P.dma("sp", lambda e: e.dma_start(out=g_sb[:, :], in_=gate[:, :]), dkey=("prm", "g"), writes=["gate"])
        P.dma("sp", lambda e: e.dma_start(out=bg_sb[:, :], in_=b_glu[:, :]), dkey=("prm", "bg"), writes=["bg"])
        ygv = ygT.rearrange("(kc p) t -> p kc t", p=128)
        vTv = vT.rearrange("(kc p) t -> p kc t", p=128)
        zi = 0
        for tt in range(T // TT):
            c0 = tt * TT
            for q in range(8):
                P.dma("sp", lambda e, q=q, c0=c0: e.dma_start(out=big[:, q * 8:(q + 1) * 8, :], in_=ygv[:, q * 8:(q + 1) * 8, c0:c0 + TT]),
                      dkey="bigl", writes=["big"])
            for oc in range(EC):
                buf, key = ws.load(w_glu, 0, EC, oc * 128, 128)
                for h in range(NH_):
                    ps, pk = pr.next()
                    for kc in range(EC):
                        P.op("pe", lambda e, ps=ps, buf=buf, kc=kc, h=h: e.matmul(ps[:, 0:HW], buf[:, kc, 0:128],
                                                                                big[:, kc, h * HW:(h + 1) * HW],
                                                                                start=(kc == 0), stop=(kc == EC - 1)),
                             reads=[key, "big"], writes=[pk])
                    s = zi % 2
                    zi += 1
                    cc = c0 + h * HW
                    P.dma("sp", lambda e, s=s, oc=oc, cc=cc: e.dma_start(out=zs[s][:, :], in_=zT[oc * 128:(oc + 1) * 128, cc:cc + HW]),
                          dkey=("zs", s), writes=[("zs", s)])
                    P.op("act", lambda e, s=s, ps=ps, oc=oc: e.activation(out=t1[s][:, :], in_=ps[:, 0:HW], func=AF.Sigmoid,
                                                                         bias=bg_sb[:, oc:oc + 1]),
                         reads=[pk, "bg"], writes=[("t1", s)])
                    P.op("act", lambda e, s=s: e.activation(out=t2[s][:, :], in_=zs[s][:, :], func=AF.Silu),
                         reads=[("zs", s)], writes=[("t2", s)])
                    P.op("dve", lambda e, s=s, oc=oc, h=h: e.tensor_tensor(out=t1[s][:, :], in0=t1[s][:, :],
                                                                          in1=big[:, oc, h * HW:(h + 1) * HW], op=ALU.mult),
                         reads=[("t1", s), "big"], writes=[("t1", s)])
                    P.op("pool", lambda e, s=s: e.tensor_tensor(out=vst[s][:, :], in0=t1[s][:, :], in1=t2[s][:, :], op=ALU.mult),
                         reads=[("t1", s), ("t2", s)], writes=[("vst", s)])
                    P.dma("sp", lambda e, s=s, oc=oc, cc=cc: e.dma_start(out=vT[oc * 128:(oc + 1) * 128, cc:cc + HW], in_=vst[s][:, :]),
                          dkey=("vso", s), reads=[("vst", s)], writes=[("vT", oc, cc)])
        fin = []
        xi = 0
        for tt in range(T // TT):
            c0 = tt * TT
            for q in range(8):
                P.dma("sp", lambda e, q=q, c0=c0: e.dma_start(out=big[:, q * 8:(q + 1) * 8, :], in_=vTv[:, q * 8:(q + 1) * 8, c0:c0 + TT]),
                      dkey="bigl", reads=[("vT", oc, c0 + h * HW) for oc in range(EC) for h in range(NH_)], writes=["big"])
            for oc2 in range(KC):
                buf, key = ws.load(w_out, 0, EC, oc2 * 128, 128)
                for h in range(NH_):
                    ps, pk = pr.next()
                    for kc in range(EC):
                        P.op("pe", lambda e, ps=ps, buf=buf, kc=kc, h=h: e.matmul(ps[:, 0:HW], buf[:, kc, 0:128],
                                                                                big[:, kc, h * HW:(h + 1) * HW],
                                                                                start=(kc == 0), stop=(kc == EC - 1)),
                             reads=[key, "big"], writes=[pk])
                    s = xi % 2
                    xi += 1
                    cc = c0 + h * HW
                    P.dma("sp", lambda e, s=s, oc2=oc2, cc=cc: e.dma_start(out=xs[s][:, :], in_=xT[oc2 * 128:(oc2 + 1) * 128, cc:cc + HW]),
                          dkey=("xs", s), writes=[("xs", s)])
                    P.op("dve", lambda e, s=s, ps=ps, oc2=oc2: e.scalar_tensor_tensor(
                        out=os_[s][:, :], in0=ps[:, 0:HW], scalar=g_sb[:, oc2:oc2 + 1], in1=xs[s][:, :], op0=ALU.mult, op1=ALU.add),
                        reads=[pk, ("xs", s), "gate"], writes=[("os", s)])
                    fin.append(P.dma("sp", lambda e, s=s, oc2=oc2, cc=cc: e.dma_start(out=x1T[oc2 * 128:(oc2 + 1) * 128, cc:cc + HW], in_=os_[s][:, :]),
                                     dkey=("oso", s), reads=[("os", s)], writes=[("x1", oc2, cc)]))
        P.emit(final_wait_ops=fin[-2:])
        print("stage c2 ops", len(P.ops), "sems", P.n_sems)
    return nc


def build_stage_c3(T, TT=1024):
    nc = bass.Bass("TRN2", target_bir_lowering=False)
    din = lambda n, s, dt=F32: nc.dram_tensor(n, s, dt, kind="ExternalInput").ap()
    ygT = din("ygT", [E, T], BF16)
    zT = din("zT", [E, T], BF16)
    xT = din("xT", [D, T])
    gate = din("gate", [128, KC])
    w_glu = din("w_glu", [E, E])
    b_glu = din("b_glu", [128, EC])
    w_out = din("w_out", [E, D])
    vT = nc.dram_tensor("vT", [E, T], BF16, kind="ExternalOutput").ap()
    x1T = nc.dram_tensor("x1T", [D, T], F32, kind="ExternalOutput").ap()
    HW = 512
    NH_ = TT // HW
    st = contextlib.ExitStack()
    with st:
        sb = lambda name, shape, dt=F32: st.enter_context(nc.sbuf_tensor(name, shape, dt))
        P = Prog(nc)
        wbufs = [sb(f"wb{i}", [128, 8, 512], BF16) for i in range(4)]
        wcnt = [0]

        def wload(w, kg, n0):
            i = wcnt[0] % 4
            wcnt[0] += 1
            wv = w.rearrange("(kc p) n -> p kc n", p=128)
            key = ("wb", i)
            for hq in range(2):
                P.dma("pool", lambda e, i=i, hq=hq: e.dma_start(out=wbufs[i][:, hq * 4:(hq + 1) * 4, :],
                                                               in_=wv[:, kg * 8 + hq * 4:kg * 8 + (hq + 1) * 4, n0:n0 + 512]),
                      dkey=key, writes=[key])
            return wbufs[i], key
        pr = PsumRot(nc, st)
        big = sb("big", [128, EC, TT], BF16)
        g_sb = sb("g_sb", [128, KC])
        bg_sb = sb("bg_sb", [128, EC])
        t1 = [sb(f"t1_{i}", [128, HW]) for i in range(2)]
        t2 = [sb(f"t2_{i}", [128, HW]) for i in range(2)]
        zs = [sb(f"zs{i}", [128, HW], BF16) for i in range(2)]
        vst = [sb(f"vst{i}", [128, HW], BF16) for i in range(2)]
        xs = [sb(f"xs{i}", [128, HW]) for i in range(2)]
        os_ = [sb(f"os{i}", [128, HW]) for i in range(2)]
        P.dma("sp", lambda e: e.dma_start(out=g_sb[:, :], in_=gate[:, :]), dkey=("prm", "g"), writes=["gate"])
        P.dma("sp", lambda e: e.dma_start(out=bg_sb[:, :], in_=b_glu[:, :]), dkey=("prm", "bg"), writes=["bg"])
        ygv = ygT.rearrange("(kc p) t -> p kc t", p=128)
        vTv = vT.rearrange("(kc p) t -> p kc t", p=128)
        zi = 0
        for tt in range(T // TT):
            c0 = tt * TT
            for q in range(8):
                P.dma("sp", lambda e, q=q, c0=c0: e.dma_start(out=big[:, q * 8:(q + 1) * 8, :], in_=ygv[:, q * 8:(q + 1) * 8, c0:c0 + TT]),
                      dkey="bigl", writes=["big"])
            for og in range(EC // 4):
                for kg in range(8):
                    buf, key = wload(w_glu, kg, og * 512)
                    for k8 in range(8):
                        kc = kg * 8 + k8
                        for o in range(4):
                            for h in range(NH_):
                                P.op("pe", lambda e, buf=buf, k8=k8, kc=kc, o=o, h=h: e.matmul(
                                    pr.t[o * 2 + h][:, 0:HW], buf[:, k8, o * 128:(o + 1) * 128], big[:, kc, h * HW:(h + 1) * HW],
                                    start=(kc == 0), stop=(kc == EC - 1)), reads=[key, "big"], writes=[("psr", o * 2 + h)])
                for o in range(4):
                    oc = og * 4 + o
                    for h in range(NH_):
                        ps, pk = pr.t[o * 2 + h], ("psr", o * 2 + h)
                        s = zi % 2
                        zi += 1
                        cc = c0 + h * HW
                        P.dma("sp", lambda e, s=s, oc=oc, cc=cc: e.dma_start(out=zs[s][:, :], in_=zT[oc * 128:(oc + 1) * 128, cc:cc + HW]),
                              dkey=("zs", s), writes=[("zs", s)])
                        P.op("act", lambda e, s=s, ps=ps, oc=oc: e.activation(out=t1[s][:, :], in_=ps[:, 0:HW], func=AF.Sigmoid,
                                                                             bias=bg_sb[:, oc:oc + 1]),
                             reads=[pk, "bg"], writes=[("t1", s)])
                        P.op("act", lambda e, s=s: e.activation(out=t2[s][:, :], in_=zs[s][:, :], func=AF.Silu),
                             reads=[("zs", s)], writes=[("t2", s)])
                        P.op("dve", lambda e, s=s, oc=oc, h=h: e.tensor_tensor(out=t1[s][:, :], in0=t1[s][:, :],
                                                                              in1=big[:, oc, h * HW:(h + 1) * HW], op=ALU.mult),
                             reads=[("t1", s), "big"], writes=[("t1", s)])
                        P.op("pool", lambda e, s=s: e.tensor_tensor(out=vst[s][:, :], in0=t1[s][:, :], in1=t2[s][:, :], op=ALU.mult),
                             reads=[("t1", s), ("t2", s)], writes=[("vst", s)])
                        P.dma("sp", lambda e, s=s, oc=oc, cc=cc: e.dma_start(out=vT[oc * 128:(oc + 1) * 128, cc:cc + HW], in_=vst[s][:, :]),
                              dkey=("vso", s), reads=[("vst", s)], writes=[("vT", oc, cc)])
        fin = []
        xi = 0
        for tt in range(T // TT):
            c0 = tt * TT
            for q in range(8):
                P.dma("sp", lambda e, q=q, c0=c0: e.dma_start(out=big[:, q * 8:(q + 1) * 8, :], in_=vTv[:, q * 8:(q + 1) * 8, c0:c0 + TT]),
                      dkey="bigl", reads=[("vT", oc, c0 + h * HW) for oc in range(EC) for h in range(NH_)], writes=["big"])
            for og in range(KC // 4):
                for kg in range(8):
                    buf, key = wload(w_out, kg, og * 512)
                    for k8 in range(8):
                        kc = kg * 8 + k8
                        for o in range(4):
                            for h in range(NH_):
                                P.op("pe", lambda e, buf=buf, k8=k8, kc=kc, o=o, h=h: e.matmul(
                                    pr.t[o * 2 + h][:, 0:HW], buf[:, k8, o * 128:(o + 1) * 128], big[:, kc, h * HW:(h + 1) * HW],
                                    start=(kc == 0), stop=(kc == EC - 1)), reads=[key, "big"], writes=[("psr", o * 2 + h)])
                for o in range(4):
                    oc2 = og * 4 + o
                    for h in range(NH_):
                        ps, pk = pr.t[o * 2 + h], ("psr", o * 2 + h)
                        s = xi % 2
                        xi += 1
                        cc = c0 + h * HW
                        P.dma("sp", lambda e, s=s, oc2=oc2, cc=cc: e.dma_start(out=xs[s][:, :], in_=xT[oc2 * 128:(oc2 + 1) * 128, cc:cc + HW]),
                              dkey=("xs", s), writes=[("xs", s)])
                        P.op("dve", lambda e, s=s, ps=ps, oc2=oc2: e.scalar_tensor_tensor(
                            out=os_[s][:, :], in0=ps[:, 0:HW], scalar=g_sb[:, oc2:oc2 + 1], in1=xs[s][:, :], op0=ALU.mult, op1=ALU.add),
                            reads=[pk, ("xs", s), "gate"], writes=[("os", s)])
                        fin.append(P.dma("sp", lambda e, s=s, oc2=oc2, cc=cc: e.dma_start(out=x1T[oc2 * 128:(oc2 + 1) * 128, cc:cc + HW], in_=os_[s][:, :]),
                                         dkey=("oso", s), reads=[("os", s)], writes=[("x1", oc2, cc)]))
        P.emit(final_wait_ops=fin[-2:])
        print("stage c3 ops", len(P.ops), "sems", P.n_sems)
    return nc


def s5_host_params(lam_re, lam_im, ls, b_re, b_im, c_re, c_im, g0, NT8):
    NG = 4 * NT8
    BpT = np.zeros((2, 2, 128, NT8, 128), np.float32)
    lamR = np.zeros((3, 2, 128, NT8, 128), np.float32)
    Cm = np.zeros((2, 2, 128, NG, 32), np.float32)
    lamM = np.zeros((3, 2, 128, NG), np.float32)
    for d in range(2):
        for g in range(NG):
            t, q = g // 4, g % 4
            for h in range(2):
                G = 2 * (g0 + g) + h
                ms = slice(64 * h, 64 * h + 64)
                rows = slice(32 * q + 16 * h, 32 * q + 16 * h + 16)
                BpT[0, d, rows, t, ms] = b_re[d, G].T
                BpT[1, d, rows, t, ms] = b_im[d, G].T
                lamR[0, d, 32 * q:32 * q + 32, t, ms] = lam_re[d, G][None, :]
                lamR[1, d, 32 * q:32 * q + 32, t, ms] = lam_im[d, G][None, :]
                lamR[2, d, 32 * q:32 * q + 32, t, ms] = ls[d, G]
                Cm[0, d, ms, g, 16 * h:16 * h + 16] = c_re[d, G].T
                Cm[1, d, ms, g, 16 * h:16 * h + 16] = c_im[d, G].T
                lamM[0, d, ms, g] = lam_re[d, G]
                lamM[1, d, ms, g] = lam_im[d, G]
                lamM[2, d, ms, g] = ls[d, G]
    return dict(BpT_re=BpT[0], BpT_im=BpT[1], lamR_re=lamR[0], lamR_im=lamR[1], lsR=lamR[2],
                Cm_re=Cm[0], Cm_im=Cm[1], lamM_re=lamM[0], lamM_im=lamM[1], lsM=lamM[2])


from concourse.bass_utils import run_bass_kernel_spmd

_BF = ml_dtypes.bfloat16


def _run(nc, ins):
    res = run_bass_kernel_spmd(nc, ins, core_ids=list(range(8)))
    return res.results


def kernel(x, c, ctx, c_ctx, norm_w, w_ada, b_ada, w_in, w_out, s5_lam_re, s5_lam_im, s5_log_step, s5_b_re, s5_b_im,
           s5_c_re, s5_c_im, s5_d, s5_w_glu, s5_b_glu, pool_w, pool_scale, final_norm_w):
    f32 = lambda a: np.asarray(a, np.float32)
    x, c, ctx, c_ctx, norm_w, w_ada, b_ada, w_in, w_out = map(f32, (x, c, ctx, c_ctx, norm_w, w_ada, b_ada, w_in, w_out))
    T = 2048
    nc_ada = build_stage_ada()
    cv = np.stack([colv(c[0]), colv(c[1]), colv(c_ctx)], axis=-1)
    ins = []
    for j in range(8):
        wa = np.ascontiguousarray(w_ada.reshape(2, D, 8, 1536)[:, :, j, :])
        ba = np.stack([colv(b_ada[l].reshape(8, 1536)[j]) for l in range(2)])
        ins.append(dict(cT3=cv, wa=wa, ba=ba))
    r = _run(nc_ada, ins)
    mp = np.stack([np.asarray(q["modp"]) for q in r])
    mod = np.ascontiguousarray(mp.transpose(1, 4, 2, 0, 3).reshape(2, 3, 128, 96))

    nc_a = build_stage_a(T, 2 * E)
    ins = []
    for core in range(8):
        b, q = core // 4, core % 4
        ins.append(dict(xT=np.ascontiguousarray(x[b, q * T:(q + 1) * T, :].T), modT=mod[0, b], nw=colv(norm_w[0]),
                        w_in=w_in[0]))
    r = _run(nc_a, ins)
    uz0 = [np.asarray(q["uz"]) for q in r]
    nc_ac = build_stage_a(64, E)
    w_in0_u = np.ascontiguousarray(w_in[0][:, :E])
    ins = []
    for core in range(8):
        b, q = core // 4, core % 4
        ins.append(dict(xT=np.ascontiguousarray(ctx[b, q * 64:(q + 1) * 64, :].T), modT=mod[0, 2], nw=colv(norm_w[0]),
                        w_in=w_in0_u))
    r = _run(nc_ac, ins)
    uc0 = [np.asarray(q["uz"]) for q in r]
    del w_in0_u

    LC, LX = 256, 8192
    U = [np.concatenate([uc0[4 * b + q] for q in range(4)] + [uz0[4 * b + q][:E] for q in range(4)], axis=1) for b in range(2)]
    nc_s5 = build_stage_s5(LC, LX, 8)
    ins = []
    lam_re, lam_im, ls = f32(s5_lam_re)[0], f32(s5_lam_im)[0], f32(s5_log_step)[0]
    b_re, b_im, c_re, c_im = f32(s5_b_re)[0], f32(s5_b_im)[0], f32(s5_c_re)[0], f32(s5_c_im)[0]
    for j in range(8):
        uj = np.stack([U[b][j * 1024:(j + 1) * 1024].reshape(8, 128, LC + LX) for b in range(2)], axis=2)
        prm = s5_host_params(lam_re, lam_im, ls, b_re, b_im, c_re, c_im, j * 32, 8)
        ins.append(dict(u=np.ascontiguousarray(uj), dskip=colv(f32(s5_d)[0][j * 1024:(j + 1) * 1024]), **prm))
    r = _run(nc_s5, ins)
    del U
    YG = [np.concatenate([np.asarray(r[j]["yg"])[:, :, b, :].reshape(1024, LX) for j in range(8)], axis=0) for b in range(2)]

    nc_c = build_stage_c3(T)
    ins = []
    for core in range(8):
        b, q = core // 4, core % 4
        ins.append(dict(ygT=np.ascontiguousarray(YG[b][:, q * T:(q + 1) * T]), zT=np.ascontiguousarray(uz0[core][E:]),
                        xT=np.ascontiguousarray(x[b, q * T:(q + 1) * T, :].T), gate=np.ascontiguousarray(mod[0, b][:, 64:96]),
                        w_glu=f32(s5_w_glu)[0], b_glu=colv(f32(s5_b_glu)[0]), w_out=w_out[0]))
    r = _run(nc_c, ins)
    x1T = [np.asarray(q["x1T"]) for q in r]
    del YG, uz0, uc0

    ins = []
    for core in range(8):
        b = core // 4
        ins.append(dict(xT=x1T[core], modT=mod[1, b], nw=colv(norm_w[1]), w_in=w_in[1]))
    r = _run(nc_a, ins)
    uz1 = [np.asarray(q["uz"]) for q in r]

    nc_d = build_stage_d2()
    ins = []
    for core in range(8):
        b, q = core // 4, core % 4
        uh = np.zeros((E, 48, 64), _BF)
        uh[:, 8:40, :] = uz1[core][:E].reshape(E, 32, 64)
        if q > 0:
            uh[:, 0:8, :] = uz1[core - 1][:E].reshape(E, 32, 64)[:, 24:32, :]
        if q < 3:
            uh[:, 40:48, :] = uz1[core + 1][:E].reshape(E, 32, 64)[:, 0:8, :]
        inv = np.ascontiguousarray(np.broadcast_to(inv_counts(q)[:, None, :], (4, 128, T)))
        ins.append(dict(uh=uh.reshape(E, 48 * 64), zT=np.ascontiguousarray(uz1[core][E:]), xT=x1T[core],
                        gate=np.ascontiguousarray(mod[1, b][:, 64:96]), pool_w=f32(pool_w)[0], pscale=colv(f32(pool_scale)[0]),
                        w_out=w_out[1], fnw=colv(f32(final_norm_w)), invc=inv))
    r = _run(nc_d, ins)
    out = np.empty((2, 8192, D), np.float32)
    for core in range(8):
        b, q = core // 4, core % 4
        out[b, q * T:(q + 1) * T, :] = np.asarray(r[core]["outT"]).T
    return out
```

```python
import numpy as np
import concourse.bass as bass
import concourse.mybir as mybir

F32 = mybir.dt.float32
BF16 = mybir.dt.bfloat16
ALU = mybir.AluOpType
AF = mybir.ActivationFunctionType

ENGS = ("pe", "act", "dve", "pool", "sp")
SEM_EPOCH = 30000


class Op:
    __slots__ = ("eng", "fn", "deps", "is_dma", "dkey", "sig", "idx", "dma_wait")

    def __init__(self, eng, fn, is_dma=False, dkey=None):
        self.eng = eng
        self.fn = fn
        self.deps = []
        self.is_dma = is_dma
        self.dkey = dkey
        self.sig = False
        self.idx = None
        self.dma_wait = None


class Prog:
    def __init__(self, nc, same_engine_sync=True):
        self.nc = nc
        self.ops = []
        self.last_w = {}
        self.readers = {}
        self.same_engine_sync = same_engine_sync
        self.dma_count = {}
        self.ctx = []

    def _add(self, op, reads, writes):
        deps = []
        for k in reads:
            w = self.last_w.get(k)
            if w is not None:
                deps.append(w)
        for k in writes:
            w = self.last_w.get(k)
            if w is not None:
                deps.append(w)
            for r in self.readers.get(k, ()):
                deps.append(r)
        seen = set()
        for d in deps:
            if id(d) in seen or d is op:
                continue
            seen.add(id(d))
            if (not d.is_dma) and d.eng == op.eng and not op.is_dma:
                ses = self.same_engine_sync
                if d.eng == "pe" or ses is False or (ses is not True and d.eng not in ses):
                    continue
            op.deps.append(d)
            d.sig = True
        for k in reads:
            self.readers.setdefault(k, []).append(op)
        for k in writes:
            self.last_w[k] = op
            self.readers[k] = []
        self.ops.append(op)
        return op

    def op(self, eng, fn, reads=(), writes=()):
        return self._add(Op(eng, fn), reads, writes)

    def dma(self, eng, fn, dkey, reads=(), writes=()):
        o = Op(eng, fn, is_dma=True, dkey=dkey)
        ep, cnt = self.dma_count.get(dkey, (0, 0))
        if cnt + 16 > SEM_EPOCH:
            ep, cnt = ep + 1, 0
        cnt += 16
        self.dma_count[dkey] = (ep, cnt)
        o.dma_wait = (dkey, ep, cnt)
        return self._add(o, reads, writes)

    def emit(self, final_wait_ops=()):
        nc = self.nc
        counters = {e: [0, 0] for e in ENGS}
        sig_of = {}
        for o in self.ops:
            if o.is_dma:
                continue
            if o.sig:
                c = counters[o.eng]
                if c[1] + 1 > SEM_EPOCH:
                    c[0] += 1
                    c[1] = 0
                c[1] += 1
                sig_of[id(o)] = (("eng", o.eng), c[0], c[1])
        semkeys = set()
        for o in self.ops:
            if o.is_dma:
                semkeys.add((("dma", o.dkey), o.dma_wait[1]))
            elif o.sig:
                k = sig_of[id(o)]
                semkeys.add((k[0], k[1]))
        semkeys = sorted(semkeys, key=repr)
        self.n_sems = len(semkeys)
        sems = {}
        import contextlib
        stack = contextlib.ExitStack()
        for i, k in enumerate(semkeys):
            sems[k] = stack.enter_context(nc.semaphore(f"s{i}"))
        per_eng = {e: [] for e in ENGS}
        for o in self.ops:
            per_eng[o.eng].append(o)

        def wait_target(d):
            if d.is_dma:
                return (("dma", d.dkey), d.dma_wait[1]), d.dma_wait[2]
            k = sig_of[id(d)]
            return (k[0], k[1]), k[2]

        def run(engname, eng):
            waited = {}
            for o in per_eng[engname]:
                for d in o.deps:
                    sk, val = wait_target(d)
                    if waited.get(sk, 0) >= val:
                        continue
                    waited[sk] = val
                    eng.wait_ge(sems[sk], val)
                ins = o.fn(eng)
                if o.is_dma:
                    ins.then_inc(sems[(("dma", o.dkey), o.dma_wait[1])], 16)
                elif o.sig:
                    k = sig_of[id(o)]
                    ins.then_inc(sems[(k[0], k[1])], 1)
            if engname == "sp":
                for d in final_wait_ops:
                    sk, val = wait_target(d)
                    eng.wait_ge(sems[sk], val)

        with stack:
            with nc.Block() as block:
                @block.tensor
                def _(e):
                    run("pe", e)

                @block.scalar
                def _(e):
                    run("act", e)

                @block.vector
                def _(e):
                    run("dve", e)

                @block.gpsimd
                def _(e):
                    run("pool", e)

                @block.sync
                def _(e):
                    run("sp", e)


import contextlib
import ml_dtypes

D = 4096
E = 8192
KC = D // 128
EC = E // 128
EPS = 1e-6


def colv(v):
    v = np.ascontiguousarray(v, dtype=np.float32)
    return np.ascontiguousarray(v.reshape(-1, 128).T)


class WStream:
    def __init__(self, P, nc, st, kmax, wb, name="wt", nbuf=2):
        self.P, self.nc = P, nc
        self.wb = wb
        self.kmax = kmax
        self.bufs = [st.enter_context(nc.sbuf_tensor(f"{name}{i}", [128, kmax, wb], BF16)) for i in range(nbuf)]
        self.i = 0
        self.name = name

    def load(self, w, kc0, nkc, n0, ncols):
        s = self.i % len(self.bufs)
        self.i += 1
        buf = self.bufs[s]
        key = (self.name, s)
        wv = w.rearrange("(kc p) n -> p kc n", p=128)
        step = 8
        for q in range(0, nkc, step):
            qn = min(step, nkc - q)
            self.P.dma("pool", lambda e, q=q, qn=qn: e.dma_start(
                out=buf[:, q:q + qn, 0:ncols], in_=wv[:, kc0 + q:kc0 + q + qn, n0:n0 + ncols]),
                dkey=key, writes=[key])
        return buf, key


def build_stage_a(T, NOUT, with_gate=True):
    nc = bass.Bass("TRN2", target_bir_lowering=False)
    xT = nc.dram_tensor("xT", [D, T], F32, kind="ExternalInput").ap()
    modT = nc.dram_tensor("modT", [128, 96], F32, kind="ExternalInput").ap()
    nw = nc.dram_tensor("nw", [128, KC], F32, kind="ExternalInput").ap()
    w_in = nc.dram_tensor("w_in", [D, NOUT], F32, kind="ExternalInput").ap()
    uz = nc.dram_tensor("uz", [NOUT, T], BF16, kind="ExternalOutput").ap()
    TT = min(512, T)
    NT = T // TT
    WB = 256
    st = contextlib.ExitStack()
    with st:
        sb = lambda name, shape, dt: st.enter_context(nc.sbuf_tensor(name, shape, dt))
        hT = sb("hT", [128, KC, T], BF16)
        c_sb = sb("c_sb", [128, KC], F32)
        sc = sb("sc", [128, KC], BF16)
        nw_sb = sb("nw_sb", [128, KC], F32)
        ba_sb = sb("ba_sb", [128, 96], F32)
        mod = sb("mod", [128, 96], F32)
        a1 = sb("a1", [128, KC], F32)
        ones = sb("ones", [128, 128], BF16)
        xs = [sb(f"xs{i}", [128, T], F32) for i in range(2)]
        sq = [sb(f"sq{i}", [128, T], BF16) for i in range(2)]
        rstd = sb("rstd", [128, T], F32)
        ot = [sb(f"ot{i}", [128, T], BF16) for i in range(2)]
        ps = [st.enter_context(nc.psum_tensor(f"ps{i}", [128, 512], F32)) for i in range(8)]
        P = Prog(nc)
        ws = WStream(P, nc, st, KC, WB)
        P.dma("sp", lambda e: e.dma_start(out=nw_sb[:, :], in_=nw[:, :]), dkey=("prm", "nw"), writes=["nw"])
        P.dma("sp", lambda e: e.dma_start(out=mod[:, :], in_=modT[:, :]), dkey=("prm", "mod"), writes=["mod"])
        P.op("dve", lambda e: e.memset(ones[:, :], 1.0), writes=["ones"])
        P.op("dve", lambda e: e.scalar_tensor_tensor(out=a1[:, :], in0=mod[:, KC:2 * KC], scalar=1.0, in1=nw_sb[:, :],
                                                     op0=ALU.add, op1=ALU.mult), reads=["mod", "nw"], writes=["a1"])
        xv = xT.rearrange("(kc p) t -> p kc t", p=128)
        for kc in range(KC):
            s = kc % 2
            P.dma("sp", lambda e, s=s, kc=kc: e.dma_start(out=xs[s][:, :], in_=xv[:, kc, :]), dkey=("xs", s),
                  writes=[("xs", s)])
            P.op("act", lambda e, s=s: e.activation(out=sq[s][:, :], in_=xs[s][:, :], func=AF.Square),
                 reads=[("xs", s)], writes=[("sq", s)])
            for tt in range(NT):
                P.op("pe", lambda e, s=s, tt=tt, kc=kc: e.matmul(
                    ps[tt][:, 0:TT], ones[:, :], sq[s][:, tt * TT:(tt + 1) * TT], start=(kc == 0), stop=(kc == KC - 1)),
                    reads=[("sq", s), "ones"], writes=[("ps", tt)])
        eps_t = sb("eps_t", [128, 1], F32)
        P.op("dve", lambda e: e.memset(eps_t[:, :], EPS), writes=["eps"])
        for tt in range(NT):
            P.op("act", lambda e, tt=tt: e.activation(out=rstd[:, tt * TT:(tt + 1) * TT], in_=ps[tt][:, 0:TT],
                                                     func=AF.Sqrt, scale=1.0 / D, bias=eps_t[:, 0:1]),
                 reads=[("ps", tt), "eps"], writes=[("rs", tt)])
            P.op("dve", lambda e, tt=tt: e.reciprocal(out=rstd[:, tt * TT:(tt + 1) * TT],
                                                      in_=rstd[:, tt * TT:(tt + 1) * TT]),
                 reads=[("rs", tt)], writes=[("rs", tt)])
        for kc in range(KC):
            s = kc % 2
            P.dma("sp", lambda e, s=s, kc=kc: e.dma_start(out=xs[s][:, :], in_=xv[:, kc, :]), dkey=("xs", s),
                  writes=[("xs", s)])
            P.op("dve", lambda e, s=s: e.tensor_tensor(out=xs[s][:, :], in0=xs[s][:, :], in1=rstd[:, :], op=ALU.mult),
                 reads=[("xs", s)] + [("rs", tt) for tt in range(NT)], writes=[("xs", s)])
            P.op("act", lambda e, s=s, kc=kc: e.activation(out=hT[:, kc, :], in_=xs[s][:, :], func=AF.Identity,
                                                          scale=a1[:, kc:kc + 1], bias=mod[:, kc:kc + 1]),
                 reads=[("xs", s), "a1", "mod"], writes=[("h", kc)])
        hkeys = [("h", kc) for kc in range(KC)]
        pi = 0
        oi = 0
        outs = []
        for blk in range(NOUT // WB):
            buf, key = ws.load(w_in, 0, KC, blk * WB, WB)
            for o2 in range(WB // 128):
                osl = oi % 2
                oi += 1
                for tt in range(NT):
                    pb = pi % 8
                    pi += 1
                    for kc in range(KC):
                        P.op("pe", lambda e, pb=pb, buf=buf, kc=kc, o2=o2, tt=tt: e.matmul(
                            ps[pb][:, 0:TT], buf[:, kc, o2 * 128:(o2 + 1) * 128], hT[:, kc, tt * TT:(tt + 1) * TT],
                            start=(kc == 0), stop=(kc == KC - 1)),
                            reads=[key] + (hkeys if kc == 0 else []), writes=[("ps", pb) if pb < 4 else ("psx", pb)] )
                    pk = ("ps", pb) if pb < 4 else ("psx", pb)
                    if pi % 2 == 0:
                        P.op("act", lambda e, pb=pb, osl=osl, tt=tt: e.activation(
                            out=ot[osl][:, tt * TT:(tt + 1) * TT], in_=ps[pb][:, 0:TT], func=AF.Copy),
                            reads=[pk], writes=[("ot", osl, tt)])
                    else:
                        P.op("dve", lambda e, pb=pb, osl=osl, tt=tt: e.tensor_copy(
                            out=ot[osl][:, tt * TT:(tt + 1) * TT], in_=ps[pb][:, 0:TT]),
                            reads=[pk], writes=[("ot", osl, tt)])
                r0 = blk * WB + o2 * 128
                d = P.dma("sp", lambda e, osl=osl, r0=r0: e.dma_start(out=uz[r0:r0 + 128, :], in_=ot[osl][:, :]),
                          dkey=("ost", osl), reads=[("ot", osl, tt) for tt in range(NT)], writes=[("uz", r0)])
                outs.append(d)
        P.emit(final_wait_ops=outs[-2:])
    return nc


TWO_PI = 2.0 * np.pi


def build_stage_s5(LC, LX, NT8=8, SB=64, CH=256, SES=True):
    L = LC + LX
    NG = 4 * NT8
    nc = bass.Bass("TRN2", target_bir_lowering=False)
    din = lambda n, s, dt=F32: nc.dram_tensor(n, s, dt, kind="ExternalInput").ap()
    u = din("u", [NT8, 128, 2, L], BF16)
    BpT = [din("BpT_re", [2, 128, NT8, 128]), din("BpT_im", [2, 128, NT8, 128])]
    lamR = [din("lamR_re", [2, 128, NT8, 128]), din("lamR_im", [2, 128, NT8, 128]), din("lsR", [2, 128, NT8, 128])]
    Cm = [din("Cm_re", [2, 128, NG, 32]), din("Cm_im", [2, 128, NG, 32])]
    lamM = [din("lamM_re", [2, 128, NG]), din("lamM_im", [2, 128, NG]), din("lsM", [2, 128, NG])]
    dskip = din("dskip", [128, NT8])
    ydir = [nc.dram_tensor(f"yd{d}", [NT8, 128, 2, LX], F32, kind="ExternalOutput").ap() for d in range(2)]
    yg = nc.dram_tensor("yg", [NT8, 128, 2, LX], BF16, kind="ExternalOutput").ap()
    st = contextlib.ExitStack()
    with st:
        sb = lambda name, shape, dt=F32: st.enter_context(nc.sbuf_tensor(name, shape, dt))
        P = Prog(nc, same_engine_sync=SES)
        MAGIC = 12582912.0

        def lam_bar(pref, lre, lim, ls, n, keys):
            t = {k: sb(f"{pref}_{k}", [128, n]) for k in ("dt", "a", "th", "mag", "cs", "sn", "are", "aim", "k")}
            P.op("act", lambda e: e.activation(out=t["dt"][:], in_=ls, func=AF.Exp), reads=keys, writes=[pref + "dt"])
            P.op("dve", lambda e: e.tensor_tensor(out=t["a"][:], in0=lre, in1=t["dt"][:], op=ALU.mult),
                 reads=keys + [pref + "dt"], writes=[pref + "a"])
            P.op("dve", lambda e: e.tensor_tensor(out=t["th"][:], in0=lim, in1=t["dt"][:], op=ALU.mult),
                 reads=keys + [pref + "dt"], writes=[pref + "th"])
            P.op("act", lambda e: e.activation(out=t["mag"][:], in_=t["a"][:], func=AF.Exp), reads=[pref + "a"],
                 writes=[pref + "mag"])
            for nm, off in (("sn", 0.0), ("cs", 0.25)):
                P.op("dve", lambda e, nm=nm, off=off: e.tensor_scalar(out=t[nm][:], in0=t["th"][:], scalar1=1.0 / TWO_PI,
                                                                     scalar2=off, op0=ALU.mult, op1=ALU.add),
                     reads=[pref + "th"], writes=[pref + nm])
                P.op("dve", lambda e, nm=nm: e.tensor_scalar(out=t["k"][:], in0=t[nm][:], scalar1=MAGIC, scalar2=None,
                                                             op0=ALU.add), reads=[pref + nm], writes=[pref + "k"])
                P.op("dve", lambda e, nm=nm: e.tensor_scalar(out=t["k"][:], in0=t["k"][:], scalar1=-MAGIC, scalar2=None,
                                                             op0=ALU.add), reads=[pref + "k"], writes=[pref + "k"])
                P.op("dve", lambda e, nm=nm: e.tensor_tensor(out=t[nm][:], in0=t[nm][:], in1=t["k"][:], op=ALU.subtract),
                     reads=[pref + nm, pref + "k"], writes=[pref + nm])
                P.op("act", lambda e, nm=nm: e.activation(out=t[nm][:], in_=t[nm][:], func=AF.Sin, scale=TWO_PI),
                     reads=[pref + nm], writes=[pref + nm])
            P.op("dve", lambda e: e.tensor_tensor(out=t["are"][:], in0=t["mag"][:], in1=t["cs"][:], op=ALU.mult),
                 reads=[pref + "mag", pref + "cs"], writes=[pref + "are"])
            P.op("dve", lambda e: e.tensor_tensor(out=t["aim"][:], in0=t["mag"][:], in1=t["sn"][:], op=ALU.mult),
                 reads=[pref + "mag", pref + "sn"], writes=[pref + "aim"])
            return t["are"], t["aim"]

        LBz = [sb(f"LBz{d}", [128, NT8, 4, 2, 128], BF16) for d in range(2)]
        Cz = [sb(f"Cz{d}", [128, NG, 2, 128], BF16) for d in range(2)]
        A2 = [sb(f"A2_{d}", [128, 2, 2, NG]) for d in range(2)]
        dsk = sb("dsk", [128, NT8])
        P.dma("sp", lambda e: e.dma_start(out=dsk[:, :], in_=dskip[:, :]), dkey=("prm", "dsk"), writes=["dsk"])
        cm = [sb(f"cm{i}", [128, NG, 32]) for i in range(2)]
        lr = [sb(f"lr{i}", [128, 128]) for i in range(3)]
        bp = [sb(f"bp{i}", [128, 128]) for i in range(2)]
        ktmp = {k: sb(f"k_{k}", [128, 128]) for k in ("nre", "den", "t1", "t2", "kre", "kim", "o1", "o2")}

        def dir_params(d):
            pf = f"d{d}"
            lm = [sb(f"{pf}lm{i}", [128, NG]) for i in range(3)]
            for i in range(3):
                P.dma("sp", lambda e, i=i: e.dma_start(out=lm[i][:, :], in_=lamM[i][d]), dkey=("prm", "lm", d, i),
                      writes=[pf + f"lm{i}"])
            are, aim = lam_bar(pf + "M", lm[0][:], lm[1][:], lm[2][:], NG, [pf + f"lm{i}" for i in range(3)])
            for c in range(2):
                P.op("dve", lambda e, c=c: e.tensor_copy(out=A2[d][:, 0, c, :], in_=are[:, :]), reads=[pf + "Mare"],
                     writes=[pf + "A2"])
            P.op("dve", lambda e: e.tensor_scalar(out=A2[d][:, 1, 0, :], in0=aim[:, :], scalar1=-1.0, scalar2=None,
                                                  op0=ALU.mult), reads=[pf + "Maim"], writes=[pf + "A2"])
            P.op("dve", lambda e: e.tensor_copy(out=A2[d][:, 1, 1, :], in_=aim[:, :]), reads=[pf + "Maim"],
                 writes=[pf + "A2"])
            for i in range(2):
                P.dma("sp", lambda e, i=i: e.dma_start(out=cm[i][:, :, :], in_=Cm[i][d]), dkey=("prm", "cm", i),
                      writes=[f"cm{i}"])
            P.op("pool", lambda e: e.memset(Cz[d][:].rearrange("p a b c -> p (a b c)"), 0.0), writes=[pf + "Cz"])
            for q in range(4):
                P.op("dve", lambda e, q=q: e.tensor_copy(out=Cz[d][:, q::4, 0, 32 * q:32 * q + 32], in_=cm[0][:, q::4, :]),
                     reads=["cm0", "cm1"], writes=[pf + "Cz"])
                P.op("dve", lambda e, q=q: e.tensor_scalar(out=Cz[d][:, q::4, 1, 32 * q:32 * q + 32], in0=cm[1][:, q::4, :],
                                                           scalar1=-1.0, scalar2=None, op0=ALU.mult),
                     reads=["cm0", "cm1"], writes=[pf + "Cz"])
            P.op("pool", lambda e: e.memset(LBz[d][:].rearrange("p a b c m -> p (a b c m)"), 0.0), writes=[pf + "LBz"])
            for t in range(NT8):
                for i in range(3):
                    P.dma("sp", lambda e, i=i, t=t: e.dma_start(out=lr[i][:, :], in_=lamR[i][d, :, t, :]),
                          dkey=("prm", "lr", i), writes=[f"lr{i}"])
                for i in range(2):
                    P.dma("sp", lambda e, i=i, t=t: e.dma_start(out=bp[i][:, :], in_=BpT[i][d, :, t, :]),
                          dkey=("prm", "bp", i), writes=[f"bp{i}"])
                lam_bar_again("R", lr, ["lr0", "lr1", "lr2"])
                rre, rim = Rt["are"], Rt["aim"]
                K = lambda k: ktmp[k][:]
                ops = [
                    lambda e: e.tensor_scalar(out=K("nre"), in0=rre[:], scalar1=-1.0, scalar2=None, op0=ALU.add),
                    lambda e: e.tensor_tensor(out=K("t1"), in0=lr[0][:], in1=lr[0][:], op=ALU.mult),
                    lambda e: e.tensor_tensor(out=K("t2"), in0=lr[1][:], in1=lr[1][:], op=ALU.mult),
                    lambda e: e.tensor_tensor(out=K("den"), in0=K("t1"), in1=K("t2"), op=ALU.add),
                    lambda e: e.reciprocal(out=K("den"), in_=K("den")),
                    lambda e: e.tensor_tensor(out=K("t1"), in0=K("nre"), in1=lr[0][:], op=ALU.mult),
                    lambda e: e.tensor_tensor(out=K("t2"), in0=rim[:], in1=lr[1][:], op=ALU.mult),
                    lambda e: e.tensor_tensor(out=K("t1"), in0=K("t1"), in1=K("t2"), op=ALU.add),
                    lambda e: e.tensor_tensor(out=K("kre"), in0=K("t1"), in1=K("den"), op=ALU.mult),
                    lambda e: e.tensor_tensor(out=K("t1"), in0=rim[:], in1=lr[0][:], op=ALU.mult),
                    lambda e: e.tensor_tensor(out=K("t2"), in0=K("nre"), in1=lr[1][:], op=ALU.mult),
                    lambda e: e.tensor_tensor(out=K("t1"), in0=K("t1"), in1=K("t2"), op=ALU.subtract),
                    lambda e: e.tensor_tensor(out=K("kim"), in0=K("t1"), in1=K("den"), op=ALU.mult),
                    lambda e: e.tensor_tensor(out=K("t1"), in0=K("kre"), in1=bp[0][:], op=ALU.mult),
                    lambda e: e.tensor_tensor(out=K("t2"), in0=K("kim"), in1=bp[1][:], op=ALU.mult),
                    lambda e: e.tensor_tensor(out=K("o1"), in0=K("t1"), in1=K("t2"), op=ALU.subtract),
                    lambda e: e.tensor_tensor(out=K("t1"), in0=K("kre"), in1=bp[1][:], op=ALU.mult),
                    lambda e: e.tensor_tensor(out=K("t2"), in0=K("kim"), in1=bp[0][:], op=ALU.mult),
                    lambda e: e.tensor_tensor(out=K("o2"), in0=K("t1"), in1=K("t2"), op=ALU.add),
                ]
                for f in ops:
                    P.op("dve", f, reads=["kap", "lr0", "lr1", "lr2", "bp0", "bp1", "Rare", "Raim"], writes=["kap"])
                for q in range(3):
                    for c, nm in ((0, "o1"), (1, "o2")):
                        P.op("dve", lambda e, q=q, c=c, nm=nm, t=t: e.tensor_copy(
                            out=LBz[d][32 * q:32 * q + 32, t, q, c, :], in_=ktmp[nm][32 * q:32 * q + 32, :]),
                            reads=["kap"], writes=[pf + "LBz"])
                for c, nm in ((0, "o1"), (1, "o2")):
                    P.op("dve", lambda e, c=c, nm=nm, t=t: e.tensor_copy(
                        out=LBz[d][64:128, t, 3, c, :], in_=ktmp[nm][64:128, :]), reads=["kap"], writes=[pf + "LBz"])
                    P.op("dve", lambda e, c=c, t=t: e.memset(LBz[d][64:96, t, 3, c, :], 0.0), reads=["kap"],
                         writes=[pf + "LBz"])

        RR = []
        Rt = {}

        def lam_bar_again(pref, lr_, keys):
            t = Rt
            lre, lim, ls = lr_[0][:], lr_[1][:], lr_[2][:]
            P.op("act", lambda e: e.activation(out=t["dt"][:], in_=ls, func=AF.Exp), reads=keys, writes=[pref + "dt"])
            P.op("dve", lambda e: e.tensor_tensor(out=t["a"][:], in0=lre, in1=t["dt"][:], op=ALU.mult),
                 reads=keys + [pref + "dt"], writes=[pref + "a"])
            P.op("dve", lambda e: e.tensor_tensor(out=t["th"][:], in0=lim, in1=t["dt"][:], op=ALU.mult),
                 reads=keys + [pref + "dt"], writes=[pref + "th"])
            P.op("act", lambda e: e.activation(out=t["mag"][:], in_=t["a"][:], func=AF.Exp), reads=[pref + "a"],
                 writes=[pref + "mag"])
            for nm, off in (("sn", 0.0), ("cs", 0.25)):
                P.op("dve", lambda e, nm=nm, off=off: e.tensor_scalar(out=t[nm][:], in0=t["th"][:], scalar1=1.0 / TWO_PI,
                                                                     scalar2=off, op0=ALU.mult, op1=ALU.add),
                     reads=[pref + "th"], writes=[pref + nm])
                P.op("dve", lambda e, nm=nm: e.tensor_scalar(out=t["k"][:], in0=t[nm][:], scalar1=MAGIC, scalar2=None,
                                                             op0=ALU.add), reads=[pref + nm], writes=[pref + "k"])
                P.op("dve", lambda e, nm=nm: e.tensor_scalar(out=t["k"][:], in0=t["k"][:], scalar1=-MAGIC, scalar2=None,
                                                             op0=ALU.add), reads=[pref + "k"], writes=[pref + "k"])
                P.op("dve", lambda e, nm=nm: e.tensor_tensor(out=t[nm][:], in0=t[nm][:], in1=t["k"][:], op=ALU.subtract),
                     reads=[pref + nm, pref + "k"], writes=[pref + nm])
                P.op("act", lambda e, nm=nm: e.activation(out=t[nm][:], in_=t[nm][:], func=AF.Sin, scale=TWO_PI),
                     reads=[pref + nm], writes=[pref + nm])
            P.op("dve", lambda e: e.tensor_tensor(out=t["are"][:], in0=t["mag"][:], in1=t["cs"][:], op=ALU.mult),
                 reads=[pref + "mag", pref + "cs"], writes=[pref + "are"])
            P.op("dve", lambda e: e.tensor_tensor(out=t["aim"][:], in0=t["mag"][:], in1=t["sn"][:], op=ALU.mult),
                 reads=[pref + "mag", pref + "sn"], writes=[pref + "aim"])

        for k_ in ("dt", "a", "th", "mag", "cs", "sn", "are", "aim", "k"):
            Rt[k_] = sb(f"Rt_{k_}", [128, 128])
        for d in range(2):
            dir_params(d)

        ub = [[sb(f"ub{d}_{i}", [128, NT8, 2, SB], BF16) for i in range(2)] for d in range(2)]
        S_all = sb("S_all", [128, 2, 2, NG, 2, SB], BF16)
        hb_all = sb("hb_all", [128, 2, 2, NG, 2, SB], BF16)
        NH = 2 * 2 * NG * 2
        Hs = [sb(f"H_{i}", [128, 2, 2, NG * 2]) for i in range(2)]
        A1r = sb("A1r", [128, 2, 2, NG, 2])
        A2r = sb("A2r", [128, 2, 2, NG, 2])
        T1 = sb("T1", [128, 2, 2, NG * 2])
        T2 = sb("T2", [128, 2, 2, NG * 2])
        ybuf = [[sb(f"yb{d}_{i}", [128, 2, SB]) for i in range(2)] for d in range(2)]
        psS = [[st.enter_context(nc.psum_tensor(f"psS{d}_{i}", [128, 4, 2, 2, SB], F32)) for i in range(2)] for d in range(2)]
        for d in range(2):
            pf = f"d{d}"
            for c in range(2):
                for b_ in range(2):
                    P.op("dve", lambda e, d=d, c=c, b_=b_: e.tensor_copy(out=A1r[:, c, d, :, b_], in_=A2[d][:, 0, 0, :]),
                         reads=[pf + "A2"], writes=["A1r"])
                    P.op("dve", lambda e, d=d, c=c, b_=b_: e.tensor_copy(out=A2r[:, c, d, :, b_], in_=A2[d][:, 1, c, :]),
                         reads=[pf + "A2"], writes=["A2r"])
        nblk = L // SB
        ncb = LC // SB
        orders = [list(range(nblk)), list(range(ncb - 1, -1, -1)) + list(range(nblk - 1, ncb - 1, -1))]
        outs = []
        SPP = 2 * 2 * NG * 2 * SB
        CST = NG * 2 * SB
        DST = 2 * CST
        stt = dict(ui=0, yi=0, hcur=0)

        def step_ap(tens, j):
            return bass.AP(tens, j, [[SPP, 128], [CST, 2], [DST + SB - 1 - 2 * j, 2], [SB, NG * 2]])

        fl3 = lambda t_: t_[:].rearrange("p c d n -> p (c d n)")
        P.op("dve", lambda e: e.memset(fl3(Hs[0]), 0.0), writes=[("H", 0)])

        def do_iter(i):
            is_x = i >= ncb
            us = stt["ui"] % 2
            stt["ui"] += 1
            blks = [orders[d][i] for d in range(2)]
            for d in range(2):
                pf = f"d{d}"
                t0 = blks[d] * SB
                ubk = (pf, "ub", us)
                for b_ in range(2):
                    P.dma("sp", lambda e, b_=b_, d=d, t0=t0: e.dma_start(out=ub[d][us][:, :, b_, :],
                                                                       in_=u[:, :, b_, t0:t0 + SB].rearrange("t p s -> p t s")),
                          dkey=ubk, writes=[ubk])
                for t in range(NT8):
                    pss = psS[d][t % 2]
                    for q in range(4):
                        for c in range(2):
                            P.op("pe", lambda e, pss=pss, t=t, q=q, c=c, d=d: e.matmul(
                                pss[:, q, c, :, :], LBz[d][:, t, q, c, :], ub[d][us][:, t, :, :], start=True, stop=True),
                                reads=[ubk, pf + "LBz"], writes=[(pf, "psS", t % 2, q, c)])
                    for c in range(2):
                        P.op("act", lambda e, pss=pss, t=t, c=c, d=d: e.activation(
                            out=S_all[:, d, c, 4 * t:4 * t + 4, :, :], in_=pss[:, :, c, :, :], func=AF.Copy),
                            reads=[(pf, "psS", t % 2, q, c) for q in range(4)], writes=[("S", d, t, c)])
            skeys = [("S", d, t, c) for d in range(2) for t in range(NT8) for c in range(2)]
            for j in range(SB):
                hcur = stt["hcur"]
                Hc, Hn = Hs[hcur], Hs[1 - hcur]
                hk, hn = ("H", hcur), ("H", 1 - hcur)
                P.op("dve", lambda e, Hc=Hc: e.tensor_tensor(out=fl3(T1), in0=fl3(Hc), in1=A1r[:].rearrange("p c d g b -> p (c d g b)"), op=ALU.mult),
                     reads=[hk, "A1r"], writes=["T1"])
                P.op("dve", lambda e, Hc=Hc: e.tensor_tensor(out=T2[:, 0, :, :], in0=Hc[:, 1, :, :],
                                                             in1=A2r[:, 0, :, :, :].rearrange("p d g b -> p d (g b)"), op=ALU.mult),
                     reads=[hk, "A2r"], writes=["T2a"])
                P.op("dve", lambda e, Hc=Hc: e.tensor_tensor(out=T2[:, 1, :, :], in0=Hc[:, 0, :, :],
                                                             in1=A2r[:, 1, :, :, :].rearrange("p d g b -> p d (g b)"), op=ALU.mult),
                     reads=[hk, "A2r"], writes=["T2b"])
                P.op("dve", lambda e: e.tensor_tensor(out=fl3(T1), in0=fl3(T1), in1=fl3(T2), op=ALU.add),
                     reads=["T1", "T2a", "T2b"], writes=["T1"])
                P.op("dve", lambda e, Hn=Hn, j=j: e.tensor_tensor(out=Hn[:], in0=T1[:], in1=step_ap(S_all, j), op=ALU.add),
                     reads=["T1"] + (skeys if j in (0, SB - 1) else []), writes=[hn])
                if is_x:
                    P.op("act", lambda e, Hn=Hn, j=j: e.activation(out=step_ap(hb_all, j), in_=Hn[:], func=AF.Copy),
                         reads=[hn], writes=[("hb", j)])
                stt["hcur"] = 1 - hcur
            if not is_x:
                return
            hbk = [("hb", j) for j in range(SB)]
            for d in range(2):
                pf = f"d{d}"
                x0 = blks[d] * SB - LC
                for t in range(NT8):
                    ys = stt["yi"] % 2
                    stt["yi"] += 1
                    yps = psS[d][t % 2][:, 0, 0, :, :]
                    first = True
                    for q in range(4):
                        g = 4 * t + q
                        for c in range(2):
                            P.op("pe", lambda e, yps=yps, g=g, c=c, d=d, first=first, last=(q == 3 and c == 1): e.matmul(
                                yps, Cz[d][:, g, c, :], hb_all[:, d, c, g, :, :], start=first, stop=last),
                                reads=hbk + [pf + "Cz"], writes=[(pf, "psS", t % 2, 0, 0)])
                            first = False
                    P.op("act", lambda e, ys=ys, yps=yps, d=d: e.activation(out=ybuf[d][ys][:, :, :], in_=yps, func=AF.Copy),
                         reads=[(pf, "psS", t % 2, 0, 0)], writes=[(pf, "yb", ys)])
                    dd = P.dma("sp", lambda e, ys=ys, t=t, d=d, x0=x0: e.dma_start(out=ydir[d][t, :, :, x0:x0 + SB], in_=ybuf[d][ys][:, :, :]),
                               dkey=(pf, "yo", ys), reads=[(pf, "yb", ys)], writes=[("yd", d, t)])
                    outs.append(dd)

        for i in range(nblk):
            do_iter(i)

        cy = [[sb(f"cy{k}_{i}", [128, 2, CH]) for i in range(2)] for k in range(2)]
        cu = [sb(f"cu{i}", [128, 2, CH], BF16) for i in range(2)]
        co = [sb(f"co{i}", [128, 2, CH], BF16) for i in range(2)]
        ci = 0
        fin = []
        for t in range(NT8):
            for x0 in range(0, LX, CH):
                s_ = ci % 2
                ci += 1
                for k in range(2):
                    P.dma("sp", lambda e, k=k, s_=s_, t=t, x0=x0: e.dma_start(out=cy[k][s_][:, :, :], in_=ydir[k][t, :, :, x0:x0 + CH]),
                          dkey=("cy", k, s_), reads=[("yd", k, t)], writes=[("cy", k, s_)])
                P.dma("sp", lambda e, s_=s_, t=t, x0=x0: e.dma_start(out=cu[s_][:, :, :], in_=u[t, :, :, LC + x0:LC + x0 + CH]),
                      dkey=("cu", s_), writes=[("cu", s_)])
                P.op("dve", lambda e, s_=s_: e.tensor_tensor(out=cy[0][s_][:], in0=cy[0][s_][:], in1=cy[1][s_][:], op=ALU.add),
                     reads=[("cy", 0, s_), ("cy", 1, s_)], writes=[("cy", 0, s_)])
                P.op("dve", lambda e, s_=s_, t=t: e.scalar_tensor_tensor(out=cy[0][s_][:], in0=cu[s_][:], scalar=dsk[:, t:t + 1],
                                                                      in1=cy[0][s_][:], op0=ALU.mult, op1=ALU.add),
                     reads=[("cy", 0, s_), ("cu", s_), "dsk"], writes=[("cy", 0, s_)])
                P.op("act", lambda e, s_=s_: e.activation(out=co[s_][:], in_=cy[0][s_][:], func=AF.Gelu),
                     reads=[("cy", 0, s_)], writes=[("co", s_)])
                fin.append(P.dma("sp", lambda e, s_=s_, t=t, x0=x0: e.dma_start(out=yg[t, :, :, x0:x0 + CH], in_=co[s_][:]),
                                 dkey=("cog", s_), reads=[("co", s_)], writes=[("yg", t, x0)]))
        P.emit(final_wait_ops=fin[-2:] + outs[-4:])
        print("s5 ops", len(P.ops), "sems", P.n_sems)
    return nc


class PsumRot:
    def __init__(self, nc, st, n=8):
        self.t = [st.enter_context(nc.psum_tensor(f"psr{i}", [128, 512], F32)) for i in range(n)]
        self.i = 0

    def next(self):
        i = self.i % len(self.t)
        self.i += 1
        return self.t[i], ("psr", i)


def outproj_residual(P, nc, ws, pr, v, vkeys, w_out, gate_sb, gate_key, xsrc, dst, c0, TT, xs, os_, cnt):
    last = []
    for oc2 in range(KC):
        buf, key = ws.load(w_out, 0, EC, oc2 * 128, 128)
        ps, pk = pr.next()
        for kc in range(EC):
            P.op("pe", lambda e, ps=ps, buf=buf, kc=kc: e.matmul(ps[:, 0:TT], buf[:, kc, 0:128], v[:, kc, :],
                                                                start=(kc == 0), stop=(kc == EC - 1)),
                 reads=[key] + (vkeys if kc in (0, EC - 1) else []), writes=[pk])
        s = cnt[0] % 2
        cnt[0] += 1
        P.dma("sp", lambda e, s=s, oc2=oc2: e.dma_start(out=xs[s][:, :], in_=xsrc[oc2 * 128:(oc2 + 1) * 128, c0:c0 + TT]),
              dkey=("xs", s), writes=[("xs", s)])
        P.op("dve", lambda e, s=s, ps=ps, oc2=oc2: e.scalar_tensor_tensor(
            out=os_[s][:, :], in0=ps[:, 0:TT], scalar=gate_sb[:, oc2:oc2 + 1], in1=xs[s][:, :], op0=ALU.mult, op1=ALU.add),
            reads=[pk, ("xs", s), gate_key], writes=[("os", s)])
        d = P.dma("sp", lambda e, s=s, oc2=oc2: e.dma_start(out=dst[oc2 * 128:(oc2 + 1) * 128, c0:c0 + TT], in_=os_[s][:, :]),
                  dkey=("oso", s), reads=[("os", s)], writes=[("dst", oc2, c0)])
        last.append(d)
    return last[-2:]


def outproj_residual2(P, pr, wviews, v, vkeys, w_out, gate_sb, gate_key, xsrc, dst, c0, TT, xs, os_, cnt, wcnt, ogc, first_extra,
                      weng="pool", wreads=()):
    wv = w_out.rearrange("(kc p) n -> p kc n", p=128)
    last = []
    for og in range(KC // 4):
        bs = (ogc[0] % 2) * 4
        ogc[0] += 1
        for kg in range(8):
            i = wcnt[0] % 4
            wcnt[0] += 1
            buf = wviews[i]
            key = ("wb", i)
            for hq in range(2):
                P.dma(weng, lambda e, buf=buf, hq=hq, kg=kg, og=og: e.dma_start(
                    out=buf[:, hq * 4:(hq + 1) * 4, :], in_=wv[:, kg * 8 + hq * 4:kg * 8 + (hq + 1) * 4, og * 512:(og + 1) * 512]),
                    dkey=key, reads=list(wreads), writes=[key] + first_extra)
            for k8 in range(8):
                kc = kg * 8 + k8
                for o in range(4):
                    P.op("pe", lambda e, buf=buf, k8=k8, kc=kc, o=o, bs=bs: e.matmul(
                        pr.t[bs + o][:, 0:TT], buf[:, k8, o * 128:(o + 1) * 128], v[:, kc, :],
                        start=(kc == 0), stop=(kc == EC - 1)), reads=[key] + vkeys, writes=[("psr", bs + o)])
        for o in range(4):
            oc2 = og * 4 + o
            ps, pk = pr.t[bs + o], ("psr", bs + o)
            s = cnt[0] % 2
            cnt[0] += 1
            P.dma("sp", lambda e, s=s, oc2=oc2: e.dma_start(out=xs[s][:, :], in_=xsrc[oc2 * 128:(oc2 + 1) * 128, c0:c0 + TT]),
                  dkey=("xs", s), writes=[("xs", s)])
            P.op("dve", lambda e, s=s, ps=ps, oc2=oc2: e.scalar_tensor_tensor(
                out=os_[s][:, :], in0=ps[:, 0:TT], scalar=gate_sb[:, oc2:oc2 + 1], in1=xs[s][:, :], op0=ALU.mult, op1=ALU.add),
                reads=[pk, ("xs", s), gate_key], writes=[("os", s)])
            d = P.dma("sp", lambda e, s=s, oc2=oc2: e.dma_start(out=dst[oc2 * 128:(oc2 + 1) * 128, c0:c0 + TT], in_=os_[s][:, :]),
                      dkey=("oso", s), reads=[("os", s)], writes=[("dst", oc2, c0)])
            last.append(d)
    return last[-2:]


def build_stage_c(T, TT=512):
    nc = bass.Bass("TRN2", target_bir_lowering=False)
    din = lambda n, s, dt=F32: nc.dram_tensor(n, s, dt, kind="ExternalInput").ap()
    ygT = din("ygT", [E, T], BF16)
    zT = din("zT", [E, T], BF16)
    xT = din("xT", [D, T])
    gate = din("gate", [128, KC])
    w_glu = din("w_glu", [E, E])
    b_glu = din("b_glu", [128, EC])
    w_out = din("w_out", [E, D])
    x1T = nc.dram_tensor("x1T", [D, T], F32, kind="ExternalOutput").ap()
    st = contextlib.ExitStack()
    with st:
        sb = lambda name, shape, dt=F32: st.enter_context(nc.sbuf_tensor(name, shape, dt))
        P = Prog(nc)
        ws = WStream(P, nc, st, EC, 128)
        pr = PsumRot(nc, st)
        ygs = sb("ygs", [128, EC, TT], BF16)
        v = sb("v", [128, EC, TT], BF16)
        g_sb = sb("g_sb", [128, KC])
        bg_sb = sb("bg_sb", [128, EC])
        t1 = [sb(f"t1_{i}", [128, TT]) for i in range(2)]
        t2 = [sb(f"t2_{i}", [128, TT]) for i in range(2)]
        zs = [sb(f"zs{i}", [128, TT], BF16) for i in range(2)]
        xs = [sb(f"xs{i}", [128, TT]) for i in range(2)]
        os_ = [sb(f"os{i}", [128, TT]) for i in range(2)]
        P.dma("sp", lambda e: e.dma_start(out=g_sb[:, :], in_=gate[:, :]), dkey=("prm", "g"), writes=["gate"])
        P.dma("sp", lambda e: e.dma_start(out=bg_sb[:, :], in_=b_glu[:, :]), dkey=("prm", "bg"), writes=["bg"])
        ygv = ygT.rearrange("(kc p) t -> p kc t", p=128)
        cnt = [0]
        zi = 0
        fin = []
        for tt in range(T // TT):
            c0 = tt * TT
            for q in range(4):
                P.dma("sp", lambda e, q=q, c0=c0: e.dma_start(out=ygs[:, q * 16:(q + 1) * 16, :], in_=ygv[:, q * 16:(q + 1) * 16, c0:c0 + TT]),
                      dkey="ygs", writes=["ygs"])
            for oc in range(EC):
                buf, key = ws.load(w_glu, 0, EC, oc * 128, 128)
                ps, pk = pr.next()
                for kc in range(EC):
                    P.op("pe", lambda e, ps=ps, buf=buf, kc=kc: e.matmul(ps[:, 0:TT], buf[:, kc, 0:128], ygs[:, kc, :],
                                                                        start=(kc == 0), stop=(kc == EC - 1)),
                         reads=[key, "ygs"], writes=[pk])
                s = zi % 2
                zi += 1
                P.dma("sp", lambda e, s=s, oc=oc, c0=c0: e.dma_start(out=zs[s][:, :], in_=zT[oc * 128:(oc + 1) * 128, c0:c0 + TT]),
                      dkey=("zs", s), writes=[("zs", s)])
                P.op("act", lambda e, s=s, ps=ps, oc=oc: e.activation(out=t1[s][:, :], in_=ps[:, 0:TT], func=AF.Sigmoid,
                                                                     bias=bg_sb[:, oc:oc + 1]),
                     reads=[pk, "bg"], writes=[("t1", s)])
                P.op("act", lambda e, s=s: e.activation(out=t2[s][:, :], in_=zs[s][:, :], func=AF.Silu),
                     reads=[("zs", s)], writes=[("t2", s)])
                P.op("dve", lambda e, s=s, oc=oc: e.tensor_tensor(out=t1[s][:, :], in0=t1[s][:, :], in1=ygs[:, oc, :], op=ALU.mult),
                     reads=[("t1", s), "ygs"], writes=[("t1", s)])
                P.op("dve", lambda e, s=s, oc=oc: e.tensor_tensor(out=v[:, oc, :], in0=t1[s][:, :], in1=t2[s][:, :], op=ALU.mult),
                     reads=[("t1", s), ("t2", s)], writes=[("v", oc)])
            fin = outproj_residual(P, nc, ws, pr, v, [("v", oc) for oc in range(EC)], w_out, g_sb, "gate", xT, x1T, c0, TT,
                                   xs, os_, cnt)
        P.emit(final_wait_ops=fin)
        print("stage c ops", len(P.ops), "sems", P.n_sems)
    return nc


def build_stage_ada():
    nc = bass.Bass("TRN2", target_bir_lowering=False)
    cT3 = nc.dram_tensor("cT3", [128, KC, 3], F32, kind="ExternalInput").ap()
    wa = nc.dram_tensor("wa", [2, D, 1536], F32, kind="ExternalInput").ap()
    ba = nc.dram_tensor("ba", [2, 128, 12], F32, kind="ExternalInput").ap()
    modp = nc.dram_tensor("modp", [2, 128, 12, 3], F32, kind="ExternalOutput").ap()
    st = contextlib.ExitStack()
    with st:
        sb = lambda name, shape, dt=F32: st.enter_context(nc.sbuf_tensor(name, shape, dt))
        P = Prog(nc)
        ws = WStream(P, nc, st, KC, 256)
        c_sb = sb("c_sb", [128, KC, 3])
        sc = sb("sc", [128, KC, 3], BF16)
        ba_sb = sb("ba_sb", [128, 2, 12])
        res = sb("res", [128, 2, 12, 3])
        ps = st.enter_context(nc.psum_tensor("psA", [128, 2, 12, 4], F32))
        P.dma("sp", lambda e: e.dma_start(out=c_sb[:], in_=cT3[:]), dkey=("prm", "c"), writes=["c"])
        for l in range(2):
            P.dma("sp", lambda e, l=l: e.dma_start(out=ba_sb[:, l, :], in_=ba[l]), dkey=("prm", "ba", l), writes=[("ba", l)])
        P.op("act", lambda e: e.activation(out=sc[:].rearrange("p a b -> p (a b)"), in_=c_sb[:].rearrange("p a b -> p (a b)"),
                                           func=AF.Silu), reads=["c"], writes=["sc"])
        for l in range(2):
            for blk in range(6):
                buf, key = ws.load(wa[l], 0, KC, blk * 256, 256)
                for o2 in range(2):
                    oc = blk * 2 + o2
                    for kc in range(KC):
                        P.op("pe", lambda e, buf=buf, kc=kc, o2=o2, oc=oc, l=l: e.matmul(
                            ps[:, l, oc, 0:3], buf[:, kc, o2 * 128:(o2 + 1) * 128], sc[:, kc, :],
                            start=(kc == 0), stop=(kc == KC - 1)), reads=[key, "sc"], writes=["psA"])
        P.op("dve", lambda e: e.tensor_tensor(out=res[:], in0=ps[:, :, :, 0:3],
                                              in1=ba_sb[:].unsqueeze(3).broadcast_to([128, 2, 12, 3]), op=ALU.add),
             reads=["psA", ("ba", 0), ("ba", 1)], writes=["res"])
        od = P.dma("sp", lambda e: e.dma_start(out=modp.rearrange("l p o c -> p l o c"), in_=res[:]), dkey="out",
                   reads=["res"], writes=["out"])
        P.emit(final_wait_ops=[od])
    return nc


def build_stage_d(T=2048, TT=512):
    R, W = 32, 64
    RH = R + 16
    nc = bass.Bass("TRN2", target_bir_lowering=False)
    din = lambda n, s, dt=F32: nc.dram_tensor(n, s, dt, kind="ExternalInput").ap()
    uh = din("uh", [E, RH * W], BF16)
    zT = din("zT", [E, T], BF16)
    xT = din("xT", [D, T])
    gate = din("gate", [128, KC])
    pool_w = din("pool_w", [4, 2048, 2048])
    pscale = din("pscale", [128, EC])
    w_out = din("w_out", [E, D])
    fnw = din("fnw", [128, KC])
    invc = din("invc", [4, 128, T])
    vT = nc.dram_tensor("vT", [E, T], BF16, kind="ExternalOutput").ap()
    x2T = nc.dram_tensor("x2T", [D, T], F32, kind="ExternalOutput").ap()
    outT = nc.dram_tensor("outT", [D, T], F32, kind="ExternalOutput").ap()
    NTT = T // TT
    st = contextlib.ExitStack()
    with st:
        sb = lambda name, shape, dt=F32: st.enter_context(nc.sbuf_tensor(name, shape, dt))
        P = Prog(nc)
        ws = WStream(P, nc, st, EC, 128)
        pr = PsumRot(nc, st)
        big = sb("big", [128, 16 * T], BF16)
        dl = big[:].rearrange("p (i t) -> p i t", i=16)
        vt = big[:].rearrange("p (k t) -> p k t", k=EC)
        assert 16 * T == EC * TT
        g_sb = sb("g_sb", [128, KC])
        ps_sb = sb("ps_sb", [128, EC])
        fn_sb = sb("fn_sb", [128, KC])
        inv_sb = sb("inv_sb", [128, T])
        ut = [sb(f"ut{i}", [128, RH, W], BF16) for i in range(2)]
        WP = W + 16
        Wk = [sb(f"Wk{i}", [128, RH, WP]) for i in range(2)]
        t2 = [sb(f"t2_{i}", [128, TT]) for i in range(2)]
        zs = [sb(f"zs{i}", [128, TT], BF16) for i in range(2)]
        vst = [sb(f"vst{i}", [128, TT], BF16) for i in range(2)]
        xs = [sb(f"xs{i}", [128, TT]) for i in range(2)]
        os_ = [sb(f"os{i}", [128, TT]) for i in range(2)]
        x3 = [sb(f"x3_{i}", [128, T]) for i in range(2)]
        sq = [sb(f"sq{i}", [128, T], BF16) for i in range(2)]
        rstd = sb("rstd", [128, T])
        ones = sb("ones", [128, 128], BF16)
        eps_t = sb("eps_t", [128, 1])
        P.op("dve", lambda e: e.memset(ones[:, :], 1.0), writes=["ones"])
        P.op("dve", lambda e: e.memset(eps_t[:, :], EPS), writes=["eps"])
        P.dma("sp", lambda e: e.dma_start(out=g_sb[:, :], in_=gate[:, :]), dkey=("prm", "g"), writes=["gate"])
        P.dma("sp", lambda e: e.dma_start(out=ps_sb[:, :], in_=pscale[:, :]), dkey=("prm", "ps"), writes=["pscale"])
        P.dma("sp", lambda e: e.dma_start(out=fn_sb[:, :], in_=fnw[:, :]), dkey=("prm", "fn"), writes=["fnw"])
        uhv = uh.rearrange("(kc p) (r w) -> p kc r w", p=128, w=W)
        ui = 0
        zi = 0
        for k in range(4):
            P.dma("sp", lambda e, k=k: e.dma_start(out=inv_sb[:, :], in_=invc[k]), dkey=("prm", "inv"), writes=["inv"])
            for i in range(16):
                kc = 16 * k + i
                s = ui % 2
                ui += 1
                P.dma("sp", lambda e, s=s, kc=kc: e.dma_start(out=ut[s][:, :, :], in_=uhv[:, kc, :, :]), dkey=("ut", s),
                      writes=[("ut", s)])
                for lo_ in (0, WP - 8):
                    P.op("pool", lambda e, lo_=lo_: e.memset(Wk[0][:, :, lo_:lo_ + 8], 0.0), writes=["W0"])
                P.op("act", lambda e, s=s: e.activation(out=Wk[0][:, :, 8:8 + W], in_=ut[s][:, :, :], func=AF.Copy),
                     reads=[("ut", s)], writes=["W0"])
                cur = 0

                def step(fns, cur):
                    src, dst = Wk[cur], Wk[1 - cur]
                    for f in fns:
                        P.op("dve", lambda e, f=f, src=src, dst=dst: f(e, src, dst), reads=[f"W{cur}"], writes=[f"W{1 - cur}"])
                    return 1 - cur
                cur = step([lambda e, a, b: e.tensor_tensor(out=b[:, :, 1:WP], in0=a[:, :, 0:WP - 1], in1=a[:, :, 1:WP], op=ALU.add)], cur)
                for l in range(1, k + 1):
                    sh = 2 ** (l - 1)
                    cur = step([lambda e, a, b, sh=sh: e.tensor_tensor(out=b[:, :, sh:WP - sh], in0=a[:, :, 0:WP - 2 * sh], in1=a[:, :, 2 * sh:WP], op=ALU.add)], cur)
                cur = step([lambda e, a, b: e.tensor_tensor(out=b[:, 1:RH, :], in0=a[:, 0:RH - 1, :], in1=a[:, 1:RH, :], op=ALU.add)], cur)
                for l in range(1, k + 1):
                    sh = 2 ** (l - 1)
                    cur = step([lambda e, a, b, sh=sh: e.tensor_tensor(out=b[:, sh:RH - sh, :], in0=a[:, 0:RH - 2 * sh, :], in1=a[:, 2 * sh:RH, :], op=ALU.add)], cur)
                src, dst = Wk[cur], Wk[1 - cur]
                P.op("dve", lambda e, src=src, dst=dst: e.tensor_tensor(out=dst[:, 8:8 + R, 8:8 + W], in0=src[:, 8:8 + R, 8:8 + W],
                                                                        in1=inv_sb[:].rearrange("p (r w) -> p r w", w=W), op=ALU.mult),
                     reads=[f"W{cur}", "inv"], writes=[f"W{1 - cur}"])
                P.op("dve", lambda e, dst=dst, s=s, i=i: e.tensor_tensor(out=dl[:, i, :].rearrange("p (r w) -> p r w", w=W),
                                                                       in0=dst[:, 8:8 + R, 8:8 + W], in1=ut[s][:, 8:8 + R, :], op=ALU.subtract),
                     reads=[f"W{1 - cur}", ("ut", s)], writes=[("dl", i), "W0", "W1"])
            dkeys = [("dl", i) for i in range(16)]
            for oc in range(16):
                ocg = 16 * k + oc
                buf, key = ws.load(pool_w[k], 0, 16, oc * 128, 128)
                for tt in range(NTT):
                    ps, pk = pr.next()
                    for kc in range(16):
                        P.op("pe", lambda e, ps=ps, buf=buf, kc=kc, tt=tt: e.matmul(ps[:, 0:TT], buf[:, kc, 0:128],
                                                                                  dl[:, kc, tt * TT:(tt + 1) * TT],
                                                                                  start=(kc == 0), stop=(kc == 15)),
                             reads=[key] + dkeys, writes=[pk])
                    s = zi % 2
                    zi += 1
                    P.dma("sp", lambda e, s=s, ocg=ocg, tt=tt: e.dma_start(out=zs[s][:, :], in_=zT[ocg * 128:(ocg + 1) * 128, tt * TT:(tt + 1) * TT]),
                          dkey=("zs", s), writes=[("zs", s)])
                    P.op("act", lambda e, s=s: e.activation(out=t2[s][:, :], in_=zs[s][:, :], func=AF.Silu),
                         reads=[("zs", s)], writes=[("t2", s)])
                    P.op("dve", lambda e, s=s, ps=ps, ocg=ocg: e.scalar_tensor_tensor(
                        out=vst[s][:, :], in0=ps[:, 0:TT], scalar=ps_sb[:, ocg:ocg + 1], in1=t2[s][:, :], op0=ALU.mult, op1=ALU.mult),
                        reads=[pk, ("t2", s), "pscale"], writes=[("vst", s)])
                    P.dma("sp", lambda e, s=s, ocg=ocg, tt=tt: e.dma_start(out=vT[ocg * 128:(ocg + 1) * 128, tt * TT:(tt + 1) * TT], in_=vst[s][:, :]),
                          dkey=("vso", s), reads=[("vst", s)], writes=[("vT", ocg, tt)])
        vTv = vT.rearrange("(kc p) t -> p kc t", p=128)
        cnt = [0]
        for tt in range(NTT):
            c0 = tt * TT
            for q in range(4):
                P.dma("sp", lambda e, q=q, c0=c0: e.dma_start(out=vt[:, q * 16:(q + 1) * 16, :], in_=vTv[:, q * 16:(q + 1) * 16, c0:c0 + TT]),
                      dkey="vtl", reads=[("vT", ocg, tt) for ocg in range(EC)], writes=["vtile"] + [("dl", i) for i in range(16)])
            outproj_residual(P, nc, ws, pr, vt, ["vtile"], w_out, g_sb, "gate", xT, x2T, c0, TT, xs, os_, cnt)
        x2v = x2T.rearrange("(kc p) t -> p kc t", p=128)
        outv = outT.rearrange("(kc p) t -> p kc t", p=128)
        dst_keys = lambda kc: [("dst", kc, tt * TT) for tt in range(NTT)]
        for kc in range(KC):
            s = kc % 2
            P.dma("sp", lambda e, s=s, kc=kc: e.dma_start(out=x3[s][:, :], in_=x2v[:, kc, :]), dkey=("x3", s),
                  reads=dst_keys(kc), writes=[("x3", s)])
            P.op("act", lambda e, s=s: e.activation(out=sq[s][:, :], in_=x3[s][:, :], func=AF.Square),
                 reads=[("x3", s)], writes=[("sq", s)])
            for tt in range(NTT):
                P.op("pe", lambda e, s=s, tt=tt, kc=kc: e.matmul(pr.t[tt][:, 0:TT], ones[:, :], sq[s][:, tt * TT:(tt + 1) * TT],
                                                              start=(kc == 0), stop=(kc == KC - 1)),
                     reads=[("sq", s), "ones"], writes=[("psr", tt)])
        for tt in range(NTT):
            P.op("act", lambda e, tt=tt: e.activation(out=rstd[:, tt * TT:(tt + 1) * TT], in_=pr.t[tt][:, 0:TT],
                                                     func=AF.Sqrt, scale=1.0 / D, bias=eps_t[:, 0:1]),
                 reads=[("psr", tt), "eps"], writes=[("rs", tt)])
            P.op("dve", lambda e, tt=tt: e.reciprocal(out=rstd[:, tt * TT:(tt + 1) * TT], in_=rstd[:, tt * TT:(tt + 1) * TT]),
                 reads=[("rs", tt)], writes=[("rs", tt)])
        fin = []
        for kc in range(KC):
            s = kc % 2
            P.dma("sp", lambda e, s=s, kc=kc: e.dma_start(out=x3[s][:, :], in_=x2v[:, kc, :]), dkey=("x3", s),
                  reads=dst_keys(kc), writes=[("x3", s)])
            P.op("dve", lambda e, s=s: e.tensor_tensor(out=x3[s][:, :], in0=x3[s][:, :], in1=rstd[:, :], op=ALU.mult),
                 reads=[("x3", s)] + [("rs", tt) for tt in range(NTT)], writes=[("x3", s)])
            P.op("act", lambda e, s=s, kc=kc: e.activation(out=x3[s][:, :], in_=x3[s][:, :], func=AF.Copy, scale=fn_sb[:, kc:kc + 1]),
                 reads=[("x3", s), "fnw"], writes=[("x3", s)])
            fin.append(P.dma("sp", lambda e, s=s, kc=kc: e.dma_start(out=outv[:, kc, :], in_=x3[s][:, :]), dkey=("x3o", s),
                             reads=[("x3", s)], writes=[("out", kc)]))
        P.emit(final_wait_ops=fin[-2:])
        print("stage d ops", len(P.ops), "sems", P.n_sems)
    return nc


def build_stage_d2(T=2048, TT=512):
    R, W = 32, 64
    RH = R + 16
    nc = bass.Bass("TRN2", target_bir_lowering=False)
    din = lambda n, s, dt=F32: nc.dram_tensor(n, s, dt, kind="ExternalInput").ap()
    uh = din("uh", [E, RH * W], BF16)
    zT = din("zT", [E, T], BF16)
    xT = din("xT", [D, T])
    gate = din("gate", [128, KC])
    pool_w = din("pool_w", [4, 2048, 2048])
    pscale = din("pscale", [128, EC])
    w_out = din("w_out", [E, D])
    fnw = din("fnw", [128, KC])
    invc = din("invc", [4, 128, T])
    vT = nc.dram_tensor("vT", [E, T], BF16, kind="ExternalOutput").ap()
    x2T = nc.dram_tensor("x2T", [D, T], F32, kind="ExternalOutput").ap()
    outT = nc.dram_tensor("outT", [D, T], F32, kind="ExternalOutput").ap()
    NTT = T // TT
    st = contextlib.ExitStack()
    with st:
        sb = lambda name, shape, dt=F32: st.enter_context(nc.sbuf_tensor(name, shape, dt))
        P = Prog(nc)
        ws = WStream(P, nc, st, EC, 128)
        pr = PsumRot(nc, st)
        big = sb("big", [128, 16 * T], BF16)
        dl = big[:].rearrange("p (i t) -> p i t", i=16)
        vt = big[:].rearrange("p (k t) -> p k t", k=EC)
        assert 16 * T == EC * TT
        g_sb = sb("g_sb", [128, KC])
        ps_sb = sb("ps_sb", [128, EC])
        fn_sb = sb("fn_sb", [128, KC])
        inv_sb = sb("inv_sb", [128, T])
        ut = [sb(f"ut{i}", [128, RH, W], BF16) for i in range(2)]
        WP = W + 16
        Wk = [sb(f"Wk{i}", [128, RH, WP]) for i in range(2)]
        t2 = [sb(f"t2_{i}", [128, TT]) for i in range(2)]
        zs = [sb(f"zs{i}", [128, TT], BF16) for i in range(2)]
        vst = [sb(f"vst{i}", [128, TT], BF16) for i in range(2)]
        xs = [sb(f"xs{i}", [128, TT]) for i in range(2)]
        os_ = [sb(f"os{i}", [128, TT]) for i in range(2)]
        x3 = [sb(f"x3_{i}", [128, T]) for i in range(2)]
        sq = [sb(f"sq{i}", [128, T], BF16) for i in range(2)]
        rstd = sb("rstd", [128, T])
        ones = sb("ones", [128, 128], BF16)
        eps_t = sb("eps_t", [128, 1])
        P.op("dve", lambda e: e.memset(ones[:, :], 1.0), writes=["ones"])
        P.op("dve", lambda e: e.memset(eps_t[:, :], EPS), writes=["eps"])
        P.dma("sp", lambda e: e.dma_start(out=g_sb[:, :], in_=gate[:, :]), dkey=("prm", "g"), writes=["gate"])
        P.dma("sp", lambda e: e.dma_start(out=ps_sb[:, :], in_=pscale[:, :]), dkey=("prm", "ps"), writes=["pscale"])
        P.dma("sp", lambda e: e.dma_start(out=fn_sb[:, :], in_=fnw[:, :]), dkey=("prm", "fn"), writes=["fnw"])
        uhv = uh.rearrange("(kc p) (r w) -> p kc r w", p=128, w=W)
        ui = 0
        zi = 0
        for k in range(4):
            P.dma("sp", lambda e, k=k: e.dma_start(out=inv_sb[:, :], in_=invc[k]), dkey=("prm", "inv"), writes=["inv"])
            for i in range(16):
                kc = 16 * k + i
                s = ui % 2
                ui += 1
                P.dma("sp", lambda e, s=s, kc=kc: e.dma_start(out=ut[s][:, :, :], in_=uhv[:, kc, :, :]), dkey=("ut", s),
                      writes=[("ut", s)])
                for lo_ in (0, WP - 8):
                    P.op("pool", lambda e, lo_=lo_: e.memset(Wk[0][:, :, lo_:lo_ + 8], 0.0), writes=["W0"])
                P.op("act", lambda e, s=s: e.activation(out=Wk[0][:, :, 8:8 + W], in_=ut[s][:, :, :], func=AF.Copy),
                     reads=[("ut", s)], writes=["W0"])
                cur = 0

                def step(fns, cur):
                    src, dst = Wk[cur], Wk[1 - cur]
                    for f in fns:
                        P.op("dve", lambda e, f=f, src=src, dst=dst: f(e, src, dst), reads=[f"W{cur}"], writes=[f"W{1 - cur}"])
                    return 1 - cur
                cur = step([lambda e, a, b: e.tensor_tensor(out=b[:, :, 1:WP], in0=a[:, :, 0:WP - 1], in1=a[:, :, 1:WP], op=ALU.add)], cur)
                for l in range(1, k + 1):
                    sh = 2 ** (l - 1)
                    cur = step([lambda e, a, b, sh=sh: e.tensor_tensor(out=b[:, :, sh:WP - sh], in0=a[:, :, 0:WP - 2 * sh], in1=a[:, :, 2 * sh:WP], op=ALU.add)], cur)
                cur = step([lambda e, a, b: e.tensor_tensor(out=b[:, 1:RH, :], in0=a[:, 0:RH - 1, :], in1=a[:, 1:RH, :], op=ALU.add)], cur)
                for l in range(1, k + 1):
                    sh = 2 ** (l - 1)
                    cur = step([lambda e, a, b, sh=sh: e.tensor_tensor(out=b[:, sh:RH - sh, :], in0=a[:, 0:RH - 2 * sh, :], in1=a[:, 2 * sh:RH, :], op=ALU.add)], cur)
                src, dst = Wk[cur], Wk[1 - cur]
                P.op("dve", lambda e, src=src, dst=dst: e.tensor_tensor(out=dst[:, 8:8 + R, 8:8 + W], in0=src[:, 8:8 + R, 8:8 + W],
                                                                        in1=inv_sb[:].rearrange("p (r w) -> p r w", w=W), op=ALU.mult),
                     reads=[f"W{cur}", "inv"], writes=[f"W{1 - cur}"])
                P.op("dve", lambda e, dst=dst, s=s, i=i: e.tensor_tensor(out=dl[:, i, :].rearrange("p (r w) -> p r w", w=W),
                                                                       in0=dst[:, 8:8 + R, 8:8 + W], in1=ut[s][:, 8:8 + R, :], op=ALU.subtract),
                     reads=[f"W{1 - cur}", ("ut", s)], writes=[("dl", i), "W0", "W1"])
            dkeys = [("dl", i) for i in range(16)]
            for oc in range(16):
                ocg = 16 * k + oc
                buf, key = ws.load(pool_w[k], 0, 16, oc * 128, 128)
                for tt in range(NTT):
                    ps, pk = pr.next()
                    for kc in range(16):
                        P.op("pe", lambda e, ps=ps, buf=buf, kc=kc, tt=tt: e.matmul(ps[:, 0:TT], buf[:, kc, 0:128],
                                                                                  dl[:, kc, tt * TT:(tt + 1) * TT],
                                                                                  start=(kc == 0), stop=(kc == 15)),
                             reads=[key] + dkeys, writes=[pk])
                    s = zi % 2
                    zi += 1
                    P.dma("sp", lambda e, s=s, ocg=ocg, tt=tt: e.dma_start(out=zs[s][:, :], in_=zT[ocg * 128:(ocg + 1) * 128, tt * TT:(tt + 1) * TT]),
                          dkey=("zs", s), writes=[("zs", s)])
                    P.op("act", lambda e, s=s: e.activation(out=t2[s][:, :], in_=zs[s][:, :], func=AF.Silu),
                         reads=[("zs", s)], writes=[("t2", s)])
                    P.op("dve", lambda e, s=s, ps=ps, ocg=ocg: e.scalar_tensor_tensor(
                        out=vst[s][:, :], in0=ps[:, 0:TT], scalar=ps_sb[:, ocg:ocg + 1], in1=t2[s][:, :], op0=ALU.mult, op1=ALU.mult),
                        reads=[pk, ("t2", s), "pscale"], writes=[("vst", s)])
                    P.dma("sp", lambda e, s=s, ocg=ocg, tt=tt: e.dma_start(out=vT[ocg * 128:(ocg + 1) * 128, tt * TT:(tt + 1) * TT], in_=vst[s][:, :]),
                          dkey=("vso", s), reads=[("vst", s)], writes=[("vT", ocg, tt)])
        vTv = vT.rearrange("(kc p) t -> p kc t", p=128)
        cnt = [0]
        wcnt = [0]
        ogc = [0]
        wviews = []
        for wb_ in ws.bufs:
            flat = wb_[:].rearrange("p k n -> p (k n)")
            for hh in range(2):
                wviews.append(flat[:, hh * 4096:(hh + 1) * 4096].rearrange("p (k n) -> p k n", k=8))
        for tt in range(NTT):
            c0 = tt * TT
            for q in range(4):
                P.dma("sp", lambda e, q=q, c0=c0: e.dma_start(out=vt[:, q * 16:(q + 1) * 16, :], in_=vTv[:, q * 16:(q + 1) * 16, c0:c0 + TT]),
                      dkey="vtl", reads=[("vT", ocg, tt) for ocg in range(EC)], writes=["vtile"] + [("dl", i) for i in range(16)])
            outproj_residual2(P, pr, wviews, vt, ["vtile"], w_out, g_sb, "gate", xT, x2T, c0, TT, xs, os_, cnt, wcnt, ogc,
                              [("wt", 0), ("wt", 1)])
        x2v = x2T.rearrange("(kc p) t -> p kc t", p=128)
        outv = outT.rearrange("(kc p) t -> p kc t", p=128)
        dst_keys = lambda kc: [("dst", kc, tt * TT) for tt in range(NTT)]
        for kc in range(KC):
            s = kc % 2
            P.dma("sp", lambda e, s=s, kc=kc: e.dma_start(out=x3[s][:, :], in_=x2v[:, kc, :]), dkey=("x3", s),
                  reads=dst_keys(kc), writes=[("x3", s)])
            P.op("act", lambda e, s=s: e.activation(out=sq[s][:, :], in_=x3[s][:, :], func=AF.Square),
                 reads=[("x3", s)], writes=[("sq", s)])
            for tt in range(NTT):
                P.op("pe", lambda e, s=s, tt=tt, kc=kc: e.matmul(pr.t[tt][:, 0:TT], ones[:, :], sq[s][:, tt * TT:(tt + 1) * TT],
                                                              start=(kc == 0), stop=(kc == KC - 1)),
                     reads=[("sq", s), "ones"], writes=[("psr", tt)])
        for tt in range(NTT):
            P.op("act", lambda e, tt=tt: e.activation(out=rstd[:, tt * TT:(tt + 1) * TT], in_=pr.t[tt][:, 0:TT],
                                                     func=AF.Sqrt, scale=1.0 / D, bias=eps_t[:, 0:1]),
                 reads=[("psr", tt), "eps"], writes=[("rs", tt)])
            P.op("dve", lambda e, tt=tt: e.reciprocal(out=rstd[:, tt * TT:(tt + 1) * TT], in_=rstd[:, tt * TT:(tt + 1) * TT]),
                 reads=[("rs", tt)], writes=[("rs", tt)])
        fin = []
        for kc in range(KC):
            s = kc % 2
            P.dma("sp", lambda e, s=s, kc=kc: e.dma_start(out=x3[s][:, :], in_=x2v[:, kc, :]), dkey=("x3", s),
                  reads=dst_keys(kc), writes=[("x3", s)])
            P.op("dve", lambda e, s=s: e.tensor_tensor(out=x3[s][:, :], in0=x3[s][:, :], in1=rstd[:, :], op=ALU.mult),
                 reads=[("x3", s)] + [("rs", tt) for tt in range(NTT)], writes=[("x3", s)])
            P.op("act", lambda e, s=s, kc=kc: e.activation(out=x3[s][:, :], in_=x3[s][:, :], func=AF.Copy, scale=fn_sb[:, kc:kc + 1]),
                 reads=[("x3", s), "fnw"], writes=[("x3", s)])
            fin.append(P.dma("sp", lambda e, s=s, kc=kc: e.dma_start(out=outv[:, kc, :], in_=x3[s][:, :]), dkey=("x3o", s),
                             reads=[("x3", s)], writes=[("out", kc)]))
        P.emit(final_wait_ops=fin[-2:])
        print("stage d2 ops", len(P.ops), "sems", P.n_sems)
    return nc


def build_stage_d3(T=2048, TT=512):
    R, W = 32, 64
    RH = R + 16
    nc = bass.Bass("TRN2", target_bir_lowering=False)
    din = lambda n, s, dt=F32: nc.dram_tensor(n, s, dt, kind="ExternalInput").ap()
    uh = din("uh", [E, RH * W], BF16)
    zT = din("zT", [E, T], BF16)
    xT = din("xT", [D, T])
    gate = din("gate", [128, KC])
    pool_w = din("pool_w", [4, 2048, 2048])
    pscale = din("pscale", [128, EC])
    w_out = din("w_out", [E, D])
    fnw = din("fnw", [128, KC])
    invc = din("invc", [4, 128, T])
    vT = nc.dram_tensor("vT", [E, T], BF16, kind="ExternalOutput").ap()
    x2T = nc.dram_tensor("x2T", [D, T], F32, kind="ExternalOutput").ap()
    outT = nc.dram_tensor("outT", [D, T], F32, kind="ExternalOutput").ap()
    wob = nc.dram_tensor("wob", [E, D], BF16, kind="ExternalOutput").ap()
    NTT = T // TT
    st = contextlib.ExitStack()
    with st:
        sb = lambda name, shape, dt=F32: st.enter_context(nc.sbuf_tensor(name, shape, dt))
        P = Prog(nc)
        ws = WStream(P, nc, st, EC, 128)
        pr = PsumRot(nc, st)
        big = sb("big", [128, 16 * T], BF16)
        dl = big[:].rearrange("p (i t) -> p i t", i=16)
        vt = big[:].rearrange("p (k t) -> p k t", k=EC)
        assert 16 * T == EC * TT
        g_sb = sb("g_sb", [128, KC])
        ps_sb = sb("ps_sb", [128, EC])
        fn_sb = sb("fn_sb", [128, KC])
        inv_sb = sb("inv_sb", [128, T])
        ut = [sb(f"ut{i}", [128, RH, W], BF16) for i in range(2)]
        WP = W + 16
        Wk = [sb(f"Wk{i}", [128, RH, WP]) for i in range(2)]
        t2 = [sb(f"t2_{i}", [128, TT]) for i in range(2)]
        zs = [sb(f"zs{i}", [128, TT], BF16) for i in range(2)]
        vst = [sb(f"vst{i}", [128, TT], BF16) for i in range(2)]
        xs = [sb(f"xs{i}", [128, TT]) for i in range(2)]
        os_ = [sb(f"os{i}", [128, TT]) for i in range(2)]
        x3 = [sb(f"x3_{i}", [128, T]) for i in range(2)]
        sq = [sb(f"sq{i}", [128, T], BF16) for i in range(2)]
        rstd = sb("rstd", [128, T])
        ones = sb("ones", [128, 128], BF16)
        eps_t = sb("eps_t", [128, 1])
        P.op("dve", lambda e: e.memset(ones[:, :], 1.0), writes=["ones"])
        P.op("dve", lambda e: e.memset(eps_t[:, :], EPS), writes=["eps"])
        P.dma("sp", lambda e: e.dma_start(out=g_sb[:, :], in_=gate[:, :]), dkey=("prm", "g"), writes=["gate"])
        P.dma("sp", lambda e: e.dma_start(out=ps_sb[:, :], in_=pscale[:, :]), dkey=("prm", "ps"), writes=["pscale"])
        P.dma("sp", lambda e: e.dma_start(out=fn_sb[:, :], in_=fnw[:, :]), dkey=("prm", "fn"), writes=["fnw"])
        uhv = uh.rearrange("(kc p) (r w) -> p kc r w", p=128, w=W)
        for cq in range(16):
            P.dma("pool", lambda e, cq=cq: e.dma_start(out=wob[cq * 512:(cq + 1) * 512, :], in_=w_out[cq * 512:(cq + 1) * 512, :]),
                  dkey=("wcv", cq % 4), writes=[("wob", cq)])
        ui = 0
        zi = 0
        for k in range(4):
            P.dma("sp", lambda e, k=k: e.dma_start(out=inv_sb[:, :], in_=invc[k]), dkey=("prm", "inv"), writes=["inv"])
            for i in range(16):
                kc = 16 * k + i
                s = ui % 2
                ui += 1
                P.dma("sp", lambda e, s=s, kc=kc: e.dma_start(out=ut[s][:, :, :], in_=uhv[:, kc, :, :]), dkey=("ut", s),
                      writes=[("ut", s)])
                for lo_ in (0, WP - 8):
                    P.op("pool", lambda e, lo_=lo_: e.memset(Wk[0][:, :, lo_:lo_ + 8], 0.0), writes=["W0"])
                P.op("act", lambda e, s=s: e.activation(out=Wk[0][:, :, 8:8 + W], in_=ut[s][:, :, :], func=AF.Copy),
                     reads=[("ut", s)], writes=["W0"])
                cur = 0

                def step(fns, cur):
                    src, dst = Wk[cur], Wk[1 - cur]
                    for f in fns:
                        P.op("dve", lambda e, f=f, src=src, dst=dst: f(e, src, dst), reads=[f"W{cur}"], writes=[f"W{1 - cur}"])
                    return 1 - cur
                cur = step([lambda e, a, b: e.tensor_tensor(out=b[:, :, 1:WP], in0=a[:, :, 0:WP - 1], in1=a[:, :, 1:WP], op=ALU.add)], cur)
                for l in range(1, k + 1):
                    sh = 2 ** (l - 1)
                    cur = step([lambda e, a, b, sh=sh: e.tensor_tensor(out=b[:, :, sh:WP - sh], in0=a[:, :, 0:WP - 2 * sh], in1=a[:, :, 2 * sh:WP], op=ALU.add)], cur)
                cur = step([lambda e, a, b: e.tensor_tensor(out=b[:, 1:RH, :], in0=a[:, 0:RH - 1, :], in1=a[:, 1:RH, :], op=ALU.add)], cur)
                for l in range(1, k + 1):
                    sh = 2 ** (l - 1)
                    cur = step([lambda e, a, b, sh=sh: e.tensor_tensor(out=b[:, sh:RH - sh, :], in0=a[:, 0:RH - 2 * sh, :], in1=a[:, 2 * sh:RH, :], op=ALU.add)], cur)
                src, dst = Wk[cur], Wk[1 - cur]
                P.op("dve", lambda e, src=src, dst=dst: e.tensor_tensor(out=dst[:, 8:8 + R, 8:8 + W], in0=src[:, 8:8 + R, 8:8 + W],
                                                                        in1=inv_sb[:].rearrange("p (r w) -> p r w", w=W), op=ALU.mult),
                     reads=[f"W{cur}", "inv"], writes=[f"W{1 - cur}"])
                P.op("dve", lambda e, dst=dst, s=s, i=i: e.tensor_tensor(out=dl[:, i, :].rearrange("p (r w) -> p r w", w=W),
                                                                       in0=dst[:, 8:8 + R, 8:8 + W], in1=ut[s][:, 8:8 + R, :], op=ALU.subtract),
                     reads=[f"W{1 - cur}", ("ut", s)], writes=[("dl", i), "W0", "W1"])
            dkeys = [("dl", i) for i in range(16)]
            for oc in range(16):
                ocg = 16 * k + oc
                buf, key = ws.load(pool_w[k], 0, 16, oc * 128, 128)
                for tt in range(NTT):
                    ps, pk = pr.next()
                    for kc in range(16):
                        P.op("pe", lambda e, ps=ps, buf=buf, kc=kc, tt=tt: e.matmul(ps[:, 0:TT], buf[:, kc, 0:128],
                                                                                  dl[:, kc, tt * TT:(tt + 1) * TT],
                                                                                  start=(kc == 0), stop=(kc == 15)),
                             reads=[key] + dkeys, writes=[pk])
                    s = zi % 2
                    zi += 1
                    P.dma("sp", lambda e, s=s, ocg=ocg, tt=tt: e.dma_start(out=zs[s][:, :], in_=zT[ocg * 128:(ocg + 1) * 128, tt * TT:(tt + 1) * TT]),
                          dkey=("zs", s), writes=[("zs", s)])
                    P.op("act", lambda e, s=s: e.activation(out=t2[s][:, :], in_=zs[s][:, :], func=AF.Silu),
                         reads=[("zs", s)], writes=[("t2", s)])
                    P.op("dve", lambda e, s=s, ps=ps, ocg=ocg: e.scalar_tensor_tensor(
                        out=vst[s][:, :], in0=ps[:, 0:TT], scalar=ps_sb[:, ocg:ocg + 1], in1=t2[s][:, :], op0=ALU.mult, op1=ALU.mult),
                        reads=[pk, ("t2", s), "pscale"], writes=[("vst", s)])
                    P.dma("sp", lambda e, s=s, ocg=ocg, tt=tt: e.dma_start(out=vT[ocg * 128:(ocg + 1) * 128, tt * TT:(tt + 1) * TT], in_=vst[s][:, :]),
                          dkey=("vso", s), reads=[("vst", s)], writes=[("vT", ocg, tt)])
        vTv = vT.rearrange("(kc p) t -> p kc t", p=128)
        cnt = [0]
        wcnt = [0]
        ogc = [0]
        wviews = []
        for wb_ in ws.bufs:
            flat = wb_[:].rearrange("p k n -> p (k n)")
            for hh in range(2):
                wviews.append(flat[:, hh * 4096:(hh + 1) * 4096].rearrange("p (k n) -> p k n", k=8))
        for tt in range(NTT):
            c0 = tt * TT
            for q in range(4):
                P.dma("sp", lambda e, q=q, c0=c0: e.dma_start(out=vt[:, q * 16:(q + 1) * 16, :], in_=vTv[:, q * 16:(q + 1) * 16, c0:c0 + TT]),
                      dkey="vtl", reads=[("vT", ocg, tt) for ocg in range(EC)], writes=["vtile"] + [("dl", i) for i in range(16)])
            outproj_residual2(P, pr, wviews, vt, ["vtile"], wob, g_sb, "gate", xT, x2T, c0, TT, xs, os_, cnt, wcnt, ogc,
                              [("wt", 0), ("wt", 1)], weng="sp", wreads=[("wob", cq) for cq in range(16)])
        x2v = x2T.rearrange("(kc p) t -> p kc t", p=128)
        outv = outT.rearrange("(kc p) t -> p kc t", p=128)
        dst_keys = lambda kc: [("dst", kc, tt * TT) for tt in range(NTT)]
        for kc in range(KC):
            s = kc % 2
            P.dma("sp", lambda e, s=s, kc=kc: e.dma_start(out=x3[s][:, :], in_=x2v[:, kc, :]), dkey=("x3", s),
                  reads=dst_keys(kc), writes=[("x3", s)])
            P.op("act", lambda e, s=s: e.activation(out=sq[s][:, :], in_=x3[s][:, :], func=AF.Square),
                 reads=[("x3", s)], writes=[("sq", s)])
            for tt in range(NTT):
                P.op("pe", lambda e, s=s, tt=tt, kc=kc: e.matmul(pr.t[tt][:, 0:TT], ones[:, :], sq[s][:, tt * TT:(tt + 1) * TT],
                                                              start=(kc == 0), stop=(kc == KC - 1)),
                     reads=[("sq", s), "ones"], writes=[("psr", tt)])
        for tt in range(NTT):
            P.op("act", lambda e, tt=tt: e.activation(out=rstd[:, tt * TT:(tt + 1) * TT], in_=pr.t[tt][:, 0:TT],
                                                     func=AF.Sqrt, scale=1.0 / D, bias=eps_t[:, 0:1]),
                 reads=[("psr", tt), "eps"], writes=[("rs", tt)])
            P.op("dve", lambda e, tt=tt: e.reciprocal(out=rstd[:, tt * TT:(tt + 1) * TT], in_=rstd[:, tt * TT:(tt + 1) * TT]),
                 reads=[("rs", tt)], writes=[("rs", tt)])
        fin = []
        for kc in range(KC):
            s = kc % 2
            P.dma("sp", lambda e, s=s, kc=kc: e.dma_start(out=x3[s][:, :], in_=x2v[:, kc, :]), dkey=("x3", s),
                  reads=dst_keys(kc), writes=[("x3", s)])
            P.op("dve", lambda e, s=s: e.tensor_tensor(out=x3[s][:, :], in0=x3[s][:, :], in1=rstd[:, :], op=ALU.mult),
                 reads=[("x3", s)] + [("rs", tt) for tt in range(NTT)], writes=[("x3", s)])
            P.op("act", lambda e, s=s, kc=kc: e.activation(out=x3[s][:, :], in_=x3[s][:, :], func=AF.Copy, scale=fn_sb[:, kc:kc + 1]),
                 reads=[("x3", s), "fnw"], writes=[("x3", s)])
            fin.append(P.dma("sp", lambda e, s=s, kc=kc: e.dma_start(out=outv[:, kc, :], in_=x3[s][:, :]), dkey=("x3o", s),
                             reads=[("x3", s)], writes=[("out", kc)]))
        P.emit(final_wait_ops=fin[-2:])
        print("stage d3 ops", len(P.ops), "sems", P.n_sems)
    return nc


def inv_counts(q):
    out = np.zeros((4, 32 * 64), np.float32)
    for k, w in enumerate((2, 4, 8, 16)):
        r = np.arange(32 * q, 32 * q + 32)
        c = np.arange(64)
        rc = np.clip(r + w - w // 2, 0, 128) - np.clip(r - w // 2, 0, 128)
        cc = np.clip(c + w - w // 2, 0, 64) - np.clip(c - w // 2, 0, 64)
        out[k] = (1.0 / (rc[:, None] * cc[None, :]).astype(np.float32)).reshape(-1)
    return out


def build_stage_c2(T, TT=1024):
    nc = bass.Bass("TRN2", target_bir_lowering=False)
    din = lambda n, s, dt=F32: nc.dram_tensor(n, s, dt, kind="ExternalInput").ap()
    ygT = din("ygT", [E, T], BF16)
    zT = din("zT", [E, T], BF16)
    xT = din("xT", [D, T])
    gate = din("gate", [128, KC])
    w_glu = din("w_glu", [E, E])
    b_glu = din("b_glu", [128, EC])
    w_out = din("w_out", [E, D])
    vT = nc.dram_tensor("vT", [E, T], BF16, kind="ExternalOutput").ap()
    x1T = nc.dram_tensor("x1T", [D, T], F32, kind="ExternalOutput").ap()
    HW = 512
    NH_ = TT // HW
    st = contextlib.ExitStack()
    with st:
        sb = lambda name, shape, dt=F32: st.enter_context(nc.sbuf_tensor(name, shape, dt))
        P = Prog(nc)
        ws = WStream(P, nc, st, EC, 128)
        pr = PsumRot(nc, st)
        big = sb("big", [128, EC, TT], BF16)
        g_sb = sb("g_sb", [128, KC])
        bg_sb = sb("bg_sb", [128, EC])
        t1 = [sb(f"t1_{i}", [128, HW]) for i in range(2)]
        t2 = [sb(f"t2_{i}", [128, HW]) for i in range(2)]
        zs = [sb(f"zs{i}", [128, HW], BF16) for i in range(2)]
        vst = [sb(f"vst{i}", [128, HW], BF16) for i in range(2)]
        xs = [sb(f"xs{i}", [128, HW]) for i in range(2)]
        os_ = [sb(f"os{i}", [128, HW]) for i in range(2)]
        P.dma("sp", lambda e: e.dma_start(out=g_sb[:, :], in_=gate[:, :]), dkey=("prm", "g"), writes=["gate"])
        P.dma("sp", lambda e: e.dma_start(out=bg_sb[:, :], in_=b_glu[:, :]), dkey=("prm", "bg"), writes=["bg"])
        ygv = ygT.rearrange("(kc p) t -> p kc t", p=128)
        vTv = vT.rearrange("(kc p) t -> p kc t", p=128)
        zi = 0
        for tt in range(T // TT):
            c0 = tt * TT
            for q in range(8):
                P.dma("sp", lambda e, q=q, c0=c0: e.dma_start(out=big[:, q * 8:(q + 1) * 8, :], in_=ygv[:, q * 8:(q + 1) * 8, c0:c0 + TT]),
                      dkey="bigl", writes=["big"])
            for oc in range(EC):
                buf, key = ws.load(w_glu, 0, EC, oc * 128, 128)
                for h in range(NH_):
                    ps, pk = pr.next()
                    for kc in range(EC):
                        P.op("pe", lambda e, ps=ps, buf=buf, kc=kc, h=h: e.matmul(ps[:, 0:HW], buf[:, kc, 0:128],
                                                                                big[:, kc, h * HW:(h + 1) * HW],
                                                                                start=(kc == 0), stop=(kc == EC - 1)),
                             reads=[key, "big"], writes=[pk])
                    s = zi % 2
                    zi += 1
                    cc = c0 + h * HW
                    P.dma("sp", lambda e, s=s, oc=oc, cc=cc: e.dma_start(out=zs[s][:, :], in_=zT[oc * 128:(oc + 1) * 128, cc:cc + HW]),
                          dkey=("zs", s), writes=[("zs", s)])
                    P.op("act", lambda e, s=s, ps=ps, oc=oc: e.activation(out=t1[s][:, :], in_=ps[:, 0:HW], func=AF.Sigmoid,
                                                                         bias=bg_sb[:, oc:oc + 1]),
                         reads=[pk, "bg"], writes=[("t1", s)])
                    P.op("act", lambda e, s=s: e.activation(out=t2[s][:, :], in_=zs[s][:, :], func=AF.Silu),
                         reads=[("zs", s)], writes=[("t2", s)])
                    P.op("dve", lambda e, s=s, oc=oc, h=h: e.tensor_tensor(out=t1[s][:, :], in0=t1[s][:, :],
                                                                          in1=big[:, oc, h * HW:(h + 1) * HW], op=ALU.mult),
                         reads=[("t1", s), "big"], writes=[("t1", s)])
                    P.op("pool", lambda e, s=s: e.tensor_tensor(out=vst[s][:, :], in0=t1[s][:, :], in1=t2[s][:, :], op=ALU.mult),
                         reads=[("t1", s), ("t2", s)], writes=[("vst", s)])
                    P.dma("sp", lambda e, s=s, oc=oc, cc=cc: e.dma_start(out=vT[oc * 128:(oc + 1) * 128, cc:cc + HW], in_=vst[s][:, :]),
                          dkey=("vso", s), reads=[("vst", s)], writes=[("vT", oc, cc)])
        fin = []
        xi = 0
        for tt in range(T // TT):
            c0 = tt * TT
            for q in range(8):
                P.dma("sp", lambda e, q=q, c0=c0: e.dma_start(out=big[:, q * 8:(q + 1) * 8, :], in_=vTv[:, q * 8:(q + 1) * 8, c0:c0 + TT]),
                      dkey="bigl", reads=[("vT", oc, c0 + h * HW) for oc in range(EC) for h in range(NH_)], writes=["big"])
            for oc2 in range(KC):
                buf, key = ws.load(w_out, 0, EC, oc2 * 128, 128)
                for h in range(NH_):
                    ps, pk = pr.next()
                    for kc in range(EC):
                        P.op("pe", lambda e, ps=ps, buf=buf, kc=kc, h=h: e.matmul(ps[:, 0:HW], buf[:, kc, 0:128],
                                                                                big[:, kc, h * HW:(h + 1) * HW],
                                                                                start=(kc == 0), stop=(kc == EC - 1)),
                             reads=[key, "big"], writes=[pk])
                    s = xi % 2
                    xi += 1
                    cc = c0 + h * HW
                    P.dma("sp", lambda e, s=s, oc2=oc2, cc=cc: e.dma_start(out=xs[s][:, :], in_=xT[oc2 * 128:(oc2 + 1) * 128, cc:cc + HW]),
                          dkey=("xs", s), writes=[("xs", s)])
                    P.op("dve", lambda e, s=s, ps=ps, oc2=oc2: e.scalar_tensor_tensor(
                        out=os_[s][:, :], in0=ps[:, 0:HW], scalar=g_sb[:, oc2:oc2 + 1], in1=xs[s][:, :], op0=ALU.mult, op1=ALU.add),
                        reads=[pk, ("xs", s), "gate"], writes=[("os", s)])
                    fin.append(P.dma("sp", lambda e, s=s, oc2=oc2, cc=cc: e.dma_start(out=x1T[oc2 * 128:(oc2 + 1) * 128, cc:cc + HW], in_=os_[s][:, :]),
                                     dkey=("oso", s), reads=[("os", s)], writes=[("x1", oc2, cc)]))
        P.emit(final_wait_ops=fin[-2:])
        print("stage c2 ops", len(P.ops), "sems", P.n_sems)
    return nc


def build_stage_c3(T, TT=1024):
    nc = bass.Bass("TRN2", target_bir_lowering=False)
    din = lambda n, s, dt=F32: nc.dram_tensor(n, s, dt, kind="ExternalInput").ap()
    ygT = din("ygT", [E, T], BF16)
    zT = din("zT", [E, T], BF16)
    xT = din("xT", [D, T])
    gate = din("gate", [128, KC])
    w_glu = din("w_glu", [E, E])
    b_glu = din("b_glu", [128, EC])
    w_out = din("w_out", [E, D])
    vT = nc.dram_tensor("vT", [E, T], BF16, kind="ExternalOutput").ap()
    x1T = nc.dram_tensor("x1T", [D, T], F32, kind="ExternalOutput").ap()
    HW = 512
    NH_ = TT // HW
    st = contextlib.ExitStack()
    with st:
        sb = lambda name, shape, dt=F32: st.enter_context(nc.sbuf_tensor(name, shape, dt))
        P = Prog(nc)
        wbufs = [sb(f"wb{i}", [128, 8, 512], BF16) for i in range(4)]
        wcnt = [0]

        def wload(w, kg, n0):
            i = wcnt[0] % 4
            wcnt[0] += 1
            wv = w.rearrange("(kc p) n -> p kc n", p=128)
            key = ("wb", i)
            for hq in range(2):
                P.dma("pool", lambda e, i=i, hq=hq: e.dma_start(out=wbufs[i][:, hq * 4:(hq + 1) * 4, :],
                                                               in_=wv[:, kg * 8 + hq * 4:kg * 8 + (hq + 1) * 4, n0:n0 + 512]),
                      dkey=key, writes=[key])
            return wbufs[i], key
        pr = PsumRot(nc, st)
        big = sb("big", [128, EC, TT], BF16)
        g_sb = sb("g_sb", [128, KC])
        bg_sb = sb("bg_sb", [128, EC])
        t1 = [sb(f"t1_{i}", [128, HW]) for i in range(2)]
        t2 = [sb(f"t2_{i}", [128, HW]) for i in range(2)]
        zs = [sb(f"zs{i}", [128, HW], BF16) for i in range(2)]
        vst = [sb(f"vst{i}", [128, HW], BF16) for i in range(2)]
        xs = [sb(f"xs{i}", [128, HW]) for i in range(2)]
        os_ = [sb(f"os{i}", [128, HW]) for i in range(2)]
        P.dma("sp", lambda e: e.dma_start(out=g_sb[:, :], in_=gate[:, :]), dkey=("prm", "g"), writes=["gate"])
        P.dma("sp", lambda e: e.dma_start(out=bg_sb[:, :], in_=b_glu[:, :]), dkey=("prm", "bg"), writes=["bg"])
        ygv = ygT.rearrange("(kc p) t -> p kc t", p=128)
        vTv = vT.rearrange("(kc p) t -> p kc t", p=128)
        zi = 0
        for tt in range(T // TT):
            c0 = tt * TT
            for q in range(8):
                P.dma("sp", lambda e, q=q, c0=c0: e.dma_start(out=big[:, q * 8:(q + 1) * 8, :], in_=ygv[:, q * 8:(q + 1) * 8, c0:c0 + TT]),
                      dkey="bigl", writes=["big"])
            for og in range(EC // 4):
                for kg in range(8):
                    buf, key = wload(w_glu, kg, og * 512)
                    for k8 in range(8):
                        kc = kg * 8 + k8
                        for o in range(4):
                            for h in range(NH_):
                                P.op("pe", lambda e, buf=buf, k8=k8, kc=kc, o=o, h=h: e.matmul(
                                    pr.t[o * 2 + h][:, 0:HW], buf[:, k8, o * 128:(o + 1) * 128], big[:, kc, h * HW:(h + 1) * HW],
                                    start=(kc == 0), stop=(kc == EC - 1)), reads=[key, "big"], writes=[("psr", o * 2 + h)])
                for o in range(4):
                    oc = og * 4 + o
                    for h in range(NH_):
                        ps, pk = pr.t[o * 2 + h], ("psr", o * 2 + h)
                        s = zi % 2
                        zi += 1
                        cc = c0 + h * HW
                        P.dma("sp", lambda e, s=s, oc=oc, cc=cc: e.dma_start(out=zs[s][:, :], in_=zT[oc * 128:(oc + 1) * 128, cc:cc + HW]),
                              dkey=("zs", s), writes=[("zs", s)])
                        P.op("act", lambda e, s=s, ps=ps, oc=oc: e.activation(out=t1[s][:, :], in_=ps[:, 0:HW], func=AF.Sigmoid,
                                                                             bias=bg_sb[:, oc:oc + 1]),
                             reads=[pk, "bg"], writes=[("t1", s)])
                        P.op("act", lambda e, s=s: e.activation(out=t2[s][:, :], in_=zs[s][:, :], func=AF.Silu),
                             reads=[("zs", s)], writes=[("t2", s)])
                        P.op("dve", lambda e, s=s, oc=oc, h=h: e.tensor_tensor(out=t1[s][:, :], in0=t1[s][:, :],
                                                                              in1=big[:, oc, h * HW:(h + 1) * HW], op=ALU.mult),
                             reads=[("t1", s), "big"], writes=[("t1", s)])
                        P.op("pool", lambda e, s=s: e.tensor_tensor(out=vst[s][:, :], in0=t1[s][:, :], in1=t2[s][:, :], op=ALU.mult),
                             reads=[("t1", s), ("t2", s)], writes=[("vst", s)])
                        P.dma("sp", lambda e, s=s, oc=oc, cc=cc: e.dma_start(out=vT[oc * 128:(oc + 1) * 128, cc:cc + HW], in_=vst[s][:, :]),
                              dkey=("vso", s), reads=[("vst", s)], writes=[("vT", oc, cc)])
        fin = []
        xi = 0
        for tt in range(T // TT):
            c0 = tt * TT
            for q in range(8):
                P.dma("sp", lambda e, q=q, c0=c0: e.dma_start(out=big[:, q * 8:(q + 1) * 8, :], in_=vTv[:, q * 8:(q + 1) * 8, c0:c0 + TT]),
                      dkey="bigl", reads=[("vT", oc, c0 + h * HW) for oc in range(EC) for h in range(NH_)], writes=["big"])
            for og in range(KC // 4):
                for kg in range(8):
                    buf, key = wload(w_out, kg, og * 512)
                    for k8 in range(8):
                        kc = kg * 8 + k8
                        for o in range(4):
                            for h in range(NH_):
                                P.op("pe", lambda e, buf=buf, k8=k8, kc=kc, o=o, h=h: e.matmul(
                                    pr.t[o * 2 + h][:, 0:HW], buf[:, k8, o * 128:(o + 1) * 128], big[:, kc, h * HW:(h + 1) * HW],
                                    start=(kc == 0), stop=(kc == EC - 1)), reads=[key, "big"], writes=[("psr", o * 2 + h)])
                for o in range(4):
                    oc2 = og * 4 + o
                    for h in range(NH_):
                        ps, pk = pr.t[o * 2 + h], ("psr", o * 2 + h)
                        s = xi % 2
                        xi += 1
                        cc = c0 + h * HW
                        P.dma("sp", lambda e, s=s, oc2=oc2, cc=cc: e.dma_start(out=xs[s][:, :], in_=xT[oc2 * 128:(oc2 + 1) * 128, cc:cc + HW]),
                              dkey=("xs", s), writes=[("xs", s)])
                        P.op("dve", lambda e, s=s, ps=ps, oc2=oc2: e.scalar_tensor_tensor(
                            out=os_[s][:, :], in0=ps[:, 0:HW], scalar=g_sb[:, oc2:oc2 + 1], in1=xs[s][:, :], op0=ALU.mult, op1=ALU.add),
                            reads=[pk, ("xs", s), "gate"], writes=[("os", s)])
                        fin.append(P.dma("sp", lambda e, s=s, oc2=oc2, cc=cc: e.dma_start(out=x1T[oc2 * 128:(oc2 + 1) * 128, cc:cc + HW], in_=os_[s][:, :]),
                                         dkey=("oso", s), reads=[("os", s)], writes=[("x1", oc2, cc)]))
        P.emit(final_wait_ops=fin[-2:])
        print("stage c3 ops", len(P.ops), "sems", P.n_sems)
    return nc


def s5_host_params(lam_re, lam_im, ls, b_re, b_im, c_re, c_im, g0, NT8):
    NG = 4 * NT8
    BpT = np.zeros((2, 2, 128, NT8, 128), np.float32)
    lamR = np.zeros((3, 2, 128, NT8, 128), np.float32)
    Cm = np.zeros((2, 2, 128, NG, 32), np.float32)
    lamM = np.zeros((3, 2, 128, NG), np.float32)
    for d in range(2):
        for g in range(NG):
            t, q = g // 4, g % 4
            for h in range(2):
                G = 2 * (g0 + g) + h
                ms = slice(64 * h, 64 * h + 64)
                rows = slice(32 * q + 16 * h, 32 * q + 16 * h + 16)
                BpT[0, d, rows, t, ms] = b_re[d, G].T
                BpT[1, d, rows, t, ms] = b_im[d, G].T
                lamR[0, d, 32 * q:32 * q + 32, t, ms] = lam_re[d, G][None, :]
                lamR[1, d, 32 * q:32 * q + 32, t, ms] = lam_im[d, G][None, :]
                lamR[2, d, 32 * q:32 * q + 32, t, ms] = ls[d, G]
                Cm[0, d, ms, g, 16 * h:16 * h + 16] = c_re[d, G].T
                Cm[1, d, ms, g, 16 * h:16 * h + 16] = c_im[d, G].T
                lamM[0, d, ms, g] = lam_re[d, G]
                lamM[1, d, ms, g] = lam_im[d, G]
                lamM[2, d, ms, g] = ls[d, G]
    return dict(BpT_re=BpT[0], BpT_im=BpT[1], lamR_re=lamR[0], lamR_im=lamR[1], lsR=lamR[2],
                Cm_re=Cm[0], Cm_im=Cm[1], lamM_re=lamM[0], lamM_im=lamM[1], lsM=lamM[2])


from concourse.bass_utils import run_bass_kernel_spmd

_BF = ml_dtypes.bfloat16


def _run(nc, ins):
    res = run_bass_kernel_spmd(nc, ins, core_ids=list(range(8)))
    return res.results


def kernel(x, c, ctx, c_ctx, norm_w, w_ada, b_ada, w_in, w_out, s5_lam_re, s5_lam_im, s5_log_step, s5_b_re, s5_b_im,
           s5_c_re, s5_c_im, s5_d, s5_w_glu, s5_b_glu, pool_w, pool_scale, final_norm_w):
    f32 = lambda a: np.asarray(a, np.float32)
    x, c, ctx, c_ctx, norm_w, w_ada, b_ada, w_in, w_out = map(f32, (x, c, ctx, c_ctx, norm_w, w_ada, b_ada, w_in, w_out))
    T = 2048
    nc_ada = build_stage_ada()
    cv = np.stack([colv(c[0]), colv(c[1]), colv(c_ctx)], axis=-1)
    ins = []
    for j in range(8):
        wa = np.ascontiguousarray(w_ada.reshape(2, D, 8, 1536)[:, :, j, :])
        ba = np.stack([colv(b_ada[l].reshape(8, 1536)[j]) for l in range(2)])
        ins.append(dict(cT3=cv, wa=wa, ba=ba))
    r = _run(nc_ada, ins)
    mp = np.stack([np.asarray(q["modp"]) for q in r])
    mod = np.ascontiguousarray(mp.transpose(1, 4, 2, 0, 3).reshape(2, 3, 128, 96))

    nc_a = build_stage_a(T, 2 * E)
    ins = []
    for core in range(8):
        b, q = core // 4, core % 4
        ins.append(dict(xT=np.ascontiguousarray(x[b, q * T:(q + 1) * T, :].T), modT=mod[0, b], nw=colv(norm_w[0]),
                        w_in=w_in[0]))
    r = _run(nc_a, ins)
    uz0 = [np.asarray(q["uz"]) for q in r]
    nc_ac = build_stage_a(64, E)
    w_in0_u = np.ascontiguousarray(w_in[0][:, :E])
    ins = []
    for core in range(8):
        b, q = core // 4, core % 4
        ins.append(dict(xT=np.ascontiguousarray(ctx[b, q * 64:(q + 1) * 64, :].T), modT=mod[0, 2], nw=colv(norm_w[0]),
                        w_in=w_in0_u))
    r = _run(nc_ac, ins)
    uc0 = [np.asarray(q["uz"]) for q in r]
    del w_in0_u

    LC, LX = 256, 8192
    U = [np.concatenate([uc0[4 * b + q] for q in range(4)] + [uz0[4 * b + q][:E] for q in range(4)], axis=1) for b in range(2)]
    nc_s5 = build_stage_s5(LC, LX, 8)
    ins = []
    lam_re, lam_im, ls = f32(s5_lam_re)[0], f32(s5_lam_im)[0], f32(s5_log_step)[0]
    b_re, b_im, c_re, c_im = f32(s5_b_re)[0], f32(s5_b_im)[0], f32(s5_c_re)[0], f32(s5_c_im)[0]
    for j in range(8):
        uj = np.stack([U[b][j * 1024:(j + 1) * 1024].reshape(8, 128, LC + LX) for b in range(2)], axis=2)
        prm = s5_host_params(lam_re, lam_im, ls, b_re, b_im, c_re, c_im, j * 32, 8)
        ins.append(dict(u=np.ascontiguousarray(uj), dskip=colv(f32(s5_d)[0][j * 1024:(j + 1) * 1024]), **prm))
    r = _run(nc_s5, ins)
    del U
    YG = [np.concatenate([np.asarray(r[j]["yg"])[:, :, b, :].reshape(1024, LX) for j in range(8)], axis=0) for b in range(2)]

    nc_c = build_stage_c3(T)
    ins = []
    for core in range(8):
        b, q = core // 4, core % 4
        ins.append(dict(ygT=np.ascontiguousarray(YG[b][:, q * T:(q + 1) * T]), zT=np.ascontiguousarray(uz0[core][E:]),
                        xT=np.ascontiguousarray(x[b, q * T:(q + 1) * T, :].T), gate=np.ascontiguousarray(mod[0, b][:, 64:96]),
                        w_glu=f32(s5_w_glu)[0], b_glu=colv(f32(s5_b_glu)[0]), w_out=w_out[0]))
    r = _run(nc_c, ins)
    x1T = [np.asarray(q["x1T"]) for q in r]
    del YG, uz0, uc0

    ins = []
    for core in range(8):
        b = core // 4
        ins.append(dict(xT=x1T[core], modT=mod[1, b], nw=colv(norm_w[1]), w_in=w_in[1]))
    r = _run(nc_a, ins)
    uz1 = [np.asarray(q["uz"]) for q in r]

    nc_d = build_stage_d3()
    ins = []
    for core in range(8):
        b, q = core // 4, core % 4
        uh = np.zeros((E, 48, 64), _BF)
        uh[:, 8:40, :] = uz1[core][:E].reshape(E, 32, 64)
        if q > 0:
            uh[:, 0:8, :] = uz1[core - 1][:E].reshape(E, 32, 64)[:, 24:32, :]
        if q < 3:
            uh[:, 40:48, :] = uz1[core + 1][:E].reshape(E, 32, 64)[:, 0:8, :]
        inv = np.ascontiguousarray(np.broadcast_to(inv_counts(q)[:, None, :], (4, 128, T)))
        ins.append(dict(uh=uh.reshape(E, 48 * 64), zT=np.ascontiguousarray(uz1[core][E:]), xT=x1T[core],
                        gate=np.ascontiguousarray(mod[1, b][:, 64:96]), pool_w=f32(pool_w)[0], pscale=colv(f32(pool_scale)[0]),
                        w_out=w_out[1], fnw=colv(f32(final_norm_w)), invc=inv))
    r = _run(nc_d, ins)
    out = np.empty((2, 8192, D), np.float32)
    for core in range(8):
        b, q = core // 4, core % 4
        out[b, q * T:(q + 1) * T, :] = np.asarray(r[core]["outT"]).T
    return out
```
